# Optimizing a Trainium2 kernel written in Bass

```python
import math
import jax
import jax.numpy as jnp
from jax import lax
import numpy as np

D_MODEL = 1024
BATCH = 4
SEQ = 4096
DEPTH = 1
DEC_BATCH = 128
DEC_SEQ = 8
PAST_LEN = 16384
PAGE_SIZE = 128

SSM_WIDTH = D_MODEL
SSM_GROUP = 16
SSM_GROUPS = SSM_WIDTH // SSM_GROUP
SSM_STATE = 64
SSM_DT_MIN = 0.001
SSM_DT_MAX = 0.1
N_HEADS = 16
QK_NOPE = 64
QK_ROPE = 32
V_DIM = 64
Q_LORA = 384
KV_LORA = 256
ATTN_WIDTH = N_HEADS * V_DIM
ROPE_BASE = 10000.0
ATTN_SCALE = (QK_NOPE + QK_ROPE) ** -0.5
Q_BLOCK = 128
NEG_INF = -1e30
NORM_EPS = 1e-6
IN_WIDTHS = (SSM_WIDTH, SSM_WIDTH, Q_LORA, KV_LORA, QK_ROPE, ATTN_WIDTH, D_MODEL, D_MODEL)
IN_COLS = SSM_WIDTH * 2 + Q_LORA + KV_LORA + QK_ROPE + ATTN_WIDTH + 2 * D_MODEL

kernel_name = "hybrid_s5_mla_gated_step"


def rmsnorm(x, g):
    xf = x.astype(jnp.float32)
    y = xf * lax.rsqrt(jnp.mean(xf * xf, axis=-1, keepdims=True) + NORM_EPS)
    return (y * g.astype(jnp.float32)).astype(x.dtype)


def split_in(z):
    bounds, acc = [], 0
    for w in IN_WIDTHS[:-1]:
        acc += w
        bounds.append(acc)
    return jnp.split(z, bounds, axis=-1)


def rope_cos_sin(pos):
    half = QK_ROPE // 2
    inv = ROPE_BASE ** (-jnp.arange(half, dtype=jnp.float32) * (2.0 / QK_ROPE))
    ang = pos.astype(jnp.float32)[:, None] * inv[None, :]
    return jnp.cos(ang), jnp.sin(ang)


def apply_rope(x, cos, sin):
    half = QK_ROPE // 2
    xf = x.astype(jnp.float32)
    x1, x2 = xf[..., :half], xf[..., half:]
    return jnp.concatenate([x1 * cos - x2 * sin, x1 * sin + x2 * cos], axis=-1).astype(x.dtype)


def _linear_combine(e1, e2):
    a1, b1 = e1
    a2, b2 = e2
    return a1 * a2, a2 * b1 + b2


def s5_branch(u, h0, w):
    B, L, _ = u.shape
    f32 = jnp.float32
    lam = lax.complex(w["ssm_lambda_re"].astype(f32), w["ssm_lambda_im"].astype(f32))
    dt = jnp.exp(w["ssm_log_dt"].astype(f32))[:, None]
    lam_bar = jnp.exp(lam * dt)
    b = lax.complex(w["ssm_b_re"].astype(f32), w["ssm_b_im"].astype(f32))
    b_bar = ((lam_bar - 1.0) / lam)[..., None] * b
    c = lax.complex(w["ssm_c_re"].astype(f32), w["ssm_c_im"].astype(f32))
    uf = u.astype(f32)
    ug = uf.reshape(B, L, SSM_GROUPS, SSM_GROUP).astype(jnp.complex64)
    bu = jnp.einsum('gpc,blgc->blgp', b_bar, ug)
    bu = bu.at[:, 0].add(lam_bar[None] * h0)
    a = jnp.broadcast_to(lam_bar, bu.shape)
    _, h = lax.associative_scan(_linear_combine, (a, bu), axis=1)
    y = jnp.real(jnp.einsum('gcp,blgp->blgc', c, h)).reshape(B, L, SSM_WIDTH)
    y = y + w["ssm_d"].astype(f32) * uf
    zg = jax.nn.gelu(y)
    out = zg * jax.nn.sigmoid(zg @ w["ssm_w_glu"].astype(f32) + w["ssm_b_glu"].astype(f32))
    return out.astype(u.dtype), h[:, -1]


def latent_scores(q_lat, q_rope, ckv, k_rope):
    s = jnp.einsum('bqhc,bkc->bhqk', q_lat, ckv) + jnp.einsum('bqhr,bkr->bhqk', q_rope, k_rope)
    return s.astype(jnp.float32) * ATTN_SCALE


def prompt_attention(q_lat, q_rope, ckv, k_rope):
    B, L = q_lat.shape[:2]
    nb = L // Q_BLOCK
    ql = q_lat.reshape(B, nb, Q_BLOCK, N_HEADS, KV_LORA).transpose(1, 0, 2, 3, 4)
    qr = q_rope.reshape(B, nb, Q_BLOCK, N_HEADS, QK_ROPE).transpose(1, 0, 2, 3, 4)
    kpos = jnp.arange(L)

    def block(args):
        i, qlb, qrb = args
        s = latent_scores(qlb, qrb, ckv, k_rope)
        qpos = i * Q_BLOCK + jnp.arange(Q_BLOCK)
        s = jnp.where(kpos[None, :] <= qpos[:, None], s, NEG_INF)
        p = jax.nn.softmax(s, axis=-1)
        return jnp.einsum('bhqk,bkc->bqhc', p.astype(ckv.dtype), ckv)

    o = lax.map(block, (jnp.arange(nb), ql, qr))
    return o.transpose(1, 0, 2, 3, 4).reshape(B, L, N_HEADS, KV_LORA)


def paged_attention(q_lat, q_rope, ckv, k_rope, cache_ckv, cache_krope, page_table):
    f32 = jnp.float32
    T = q_lat.shape[1]
    s = latent_scores(q_lat, q_rope, ckv, k_rope)
    s = jnp.where(jnp.tril(jnp.ones((T, T), dtype=bool)), s, NEG_INF)
    m = jnp.max(s, axis=-1)
    p = jnp.exp(s - m[..., None])
    l = jnp.sum(p, axis=-1)
    acc = jnp.einsum('bhqk,bkc->bhqc', p, ckv.astype(f32))

    def page_step(carry, pages):
        m, l, acc = carry
        kc = cache_ckv[pages]
        kr = cache_krope[pages]
        sp = latent_scores(q_lat, q_rope, kc, kr)
        m_new = jnp.maximum(m, jnp.max(sp, axis=-1))
        corr = jnp.exp(m - m_new)
        pp = jnp.exp(sp - m_new[..., None])
        l = l * corr + jnp.sum(pp, axis=-1)
        acc = acc * corr[..., None] + jnp.einsum('bhqk,bkc->bhqc', pp, kc.astype(f32))
        return (m_new, l, acc), None

    (m, l, acc), _ = lax.scan(page_step, (m, l, acc), page_table.T)
    o = acc / l[..., None]
    return o.transpose(0, 2, 1, 3).astype(q_lat.dtype)


def hybrid_layer(x, pos, ssm_h0, attend, w):
    B, L, _ = x.shape
    xn = rmsnorm(x, w["norm_in"])
    z = xn @ w["w_in"]
    u_s, g_s, cq, ckv_raw, kr_raw, g_a, m_s, m_a = split_in(z)
    y_s, h_last = s5_branch(u_s, ssm_h0, w)
    y_s = (y_s * jax.nn.silu(g_s)) @ w["w_br_ssm"]
    cos, sin = rope_cos_sin(pos)
    cq = rmsnorm(cq, w["mla_q_norm"])
    q = jnp.einsum('blr,rhd->blhd', cq, w["mla_w_uq"])
    q_rope = apply_rope(q[..., QK_NOPE:], cos[:, None, :], sin[:, None, :])
    q_lat = jnp.einsum('blhd,chd->blhc', q[..., :QK_NOPE], w["mla_w_uk"])
    ckv = rmsnorm(ckv_raw, w["mla_kv_norm"])
    k_rope = apply_rope(kr_raw, cos, sin)
    o_lat = attend(q_lat, q_rope, ckv, k_rope)
    o = jnp.einsum('blhc,chd->blhd', o_lat, w["mla_w_uv"]).reshape(B, L, ATTN_WIDTH)
    y_a = (o * jax.nn.silu(g_a)) @ w["w_br_attn"]
    merged = jax.nn.sigmoid(m_s) * y_s + jax.nn.sigmoid(m_a) * y_a
    h = x + merged @ w["w_out"]
    return h, ckv, k_rope, h_last


def setup_inputs(seed: int = 0) -> dict:
    key = jax.random.key(seed)
    ks = jax.random.split(key, 32)
    f32 = jnp.float32
    n_pages = PAST_LEN // PAGE_SIZE
    n_used = DEC_BATCH * n_pages
    n_pool = n_used + n_used // 4

    def nrm(k, shape, scale):
        return jax.random.normal(k, shape, f32) * scale

    G, P, GC = SSM_GROUPS, SSM_STATE, SSM_GROUP
    return {
        "x_prompt": nrm(ks[0], (BATCH, SEQ, D_MODEL), 1.0),
        "x_sample": nrm(ks[1], (DEC_BATCH, DEC_SEQ, D_MODEL), 1.0),
        "cache_ckv": nrm(ks[2], (n_pool, PAGE_SIZE, KV_LORA), 1.0),
        "cache_krope": nrm(ks[3], (n_pool, PAGE_SIZE, QK_ROPE), 1.0),
        "state_ssm": nrm(ks[4], (DEC_BATCH, G, P, 2), 0.5),
        "page_table": jax.random.permutation(ks[5], n_pool)[:n_used].reshape(DEC_BATCH, n_pages).astype(jnp.int32),
        "norm_in": 1.0 + nrm(ks[6], (D_MODEL,), 0.02),
        "w_in": nrm(ks[7], (D_MODEL, IN_COLS), D_MODEL ** -0.5),
        "ssm_lambda_re": -0.5 + nrm(ks[8], (G, P), 0.01),
        "ssm_lambda_im": math.pi * jnp.arange(P, dtype=f32)[None, :] + nrm(ks[9], (G, P), 0.01),
        "ssm_log_dt": jax.random.uniform(ks[10], (G,), f32, math.log(SSM_DT_MIN), math.log(SSM_DT_MAX)),
        "ssm_b_re": nrm(ks[11], (G, P, GC), (2 * GC) ** -0.5),
        "ssm_b_im": nrm(ks[12], (G, P, GC), (2 * GC) ** -0.5),
        "ssm_c_re": nrm(ks[13], (G, GC, P), (2 * P) ** -0.5),
        "ssm_c_im": nrm(ks[14], (G, GC, P), (2 * P) ** -0.5),
        "ssm_d": nrm(ks[15], (SSM_WIDTH,), 0.5),
        "ssm_w_glu": nrm(ks[16], (SSM_WIDTH, SSM_WIDTH), SSM_WIDTH ** -0.5),
        "ssm_b_glu": nrm(ks[17], (SSM_WIDTH,), 0.02),
        "w_br_ssm": nrm(ks[18], (SSM_WIDTH, D_MODEL), SSM_WIDTH ** -0.5),
        "mla_q_norm": 1.0 + nrm(ks[19], (Q_LORA,), 0.02),
        "mla_w_uq": nrm(ks[20], (Q_LORA, N_HEADS, QK_NOPE + QK_ROPE), Q_LORA ** -0.5),
        "mla_kv_norm": 1.0 + nrm(ks[21], (KV_LORA,), 0.02),
        "mla_w_uk": nrm(ks[22], (KV_LORA, N_HEADS, QK_NOPE), KV_LORA ** -0.5),
        "mla_w_uv": nrm(ks[23], (KV_LORA, N_HEADS, V_DIM), KV_LORA ** -0.5),
        "w_br_attn": nrm(ks[24], (ATTN_WIDTH, D_MODEL), ATTN_WIDTH ** -0.5),
        "w_out": nrm(ks[25], (D_MODEL, D_MODEL), D_MODEL ** -0.5),
        "norm_final": 1.0 + nrm(ks[26], (D_MODEL,), 0.02),
    }


def reference(x_prompt, x_sample, cache_ckv, cache_krope, state_ssm, page_table,
              norm_in, w_in, ssm_lambda_re, ssm_lambda_im, ssm_log_dt, ssm_b_re, ssm_b_im,
              ssm_c_re, ssm_c_im, ssm_d, ssm_w_glu, ssm_b_glu, w_br_ssm,
              mla_q_norm, mla_w_uq, mla_kv_norm, mla_w_uk, mla_w_uv, w_br_attn, w_out, norm_final):
    w = dict(norm_in=norm_in, w_in=w_in, ssm_lambda_re=ssm_lambda_re, ssm_lambda_im=ssm_lambda_im,
             ssm_log_dt=ssm_log_dt, ssm_b_re=ssm_b_re, ssm_b_im=ssm_b_im, ssm_c_re=ssm_c_re,
             ssm_c_im=ssm_c_im, ssm_d=ssm_d, ssm_w_glu=ssm_w_glu, ssm_b_glu=ssm_b_glu,
             w_br_ssm=w_br_ssm, mla_q_norm=mla_q_norm, mla_w_uq=mla_w_uq, mla_kv_norm=mla_kv_norm,
             mla_w_uk=mla_w_uk, mla_w_uv=mla_w_uv, w_br_attn=w_br_attn, w_out=w_out)
    B, L, _ = x_prompt.shape
    Bd, T, _ = x_sample.shape
    h0_p = jnp.zeros((B, SSM_GROUPS, SSM_STATE), jnp.complex64)
    h_p, ckv_p, krope_p, hs_p = hybrid_layer(x_prompt, jnp.arange(L), h0_p, prompt_attention, w)
    h0_s = lax.complex(state_ssm[..., 0].astype(jnp.float32), state_ssm[..., 1].astype(jnp.float32))
    attend_s = lambda ql, qr, ck, kr: paged_attention(ql, qr, ck, kr, cache_ckv, cache_krope, page_table)
    h_s, ckv_s, krope_s, hs_s = hybrid_layer(x_sample, PAST_LEN + jnp.arange(T), h0_s, attend_s, w)
    y_prompt = rmsnorm(h_p, norm_final)
    y_sample = rmsnorm(h_s, norm_final)
    ssm_p = jnp.stack([jnp.real(hs_p), jnp.imag(hs_p)], axis=-1)
    ssm_s = jnp.stack([jnp.real(hs_s), jnp.imag(hs_s)], axis=-1)
    return (y_prompt, y_sample, ckv_p, krope_p, ssm_p, ckv_s, krope_s, ssm_s)
```

```python
import contextlib
import math
import numpy as np
import concourse.bass as bass
import concourse.mybir as mybir
from concourse.bass_utils import run_bass_kernel_spmd

F32 = mybir.dt.float32
BF16 = mybir.dt.bfloat16
I32 = mybir.dt.int32
AF = mybir.ActivationFunctionType
ALU = mybir.AluOpType
AX = mybir.AxisListType

D = 1024
NCOL = 5792
C_US, C_GS, C_CQ, C_KV, C_KR, C_GA, C_MS, C_MA = 0, 1024, 2048, 2432, 2688, 2720, 3744, 4768
NH = 16
SCALE = 96 ** -0.5
EPS = 1e-6
NEG = -1e30
L_OWN = 2048
L_CTX = 2048
PAST = 16384
NPAGE = 128
STAGE = "all"


class Reg:
    __slots__ = ("name", "w", "r")

    def __init__(self, name=""):
        self.name = name
        self.w = None
        self.r = []


class Op:
    __slots__ = ("eng", "fn", "deps", "idx", "dma", "key", "sig", "cnt", "waits")

    def __init__(self, eng, fn, dma, key):
        self.eng, self.fn, self.dma, self.key = eng, fn, dma, key
        self.deps = []
        self.sig = False
        self.cnt = 0
        self.waits = []


class Prog:
    ENGS = ("pe", "act", "dve", "pool", "sp")

    def __init__(self, nc):
        self.nc = nc
        self.ops = {e: [] for e in self.ENGS}
        self.dma_keys = {}

    def add(self, eng, fn, reads=(), writes=(), dma=False, key=None):
        op = Op(eng, fn, dma, key)
        if dma:
            lst = self.dma_keys.setdefault(key, [])
            lst.append(op)
            op.cnt = 16 * len(lst)
            op.sig = True
        for r in reads:
            if r.w is not None:
                op.deps.append((r.w, "raw"))
        for w in writes:
            if w.w is not None:
                op.deps.append((w.w, "waw"))
            for rd in w.r:
                op.deps.append((rd, "war"))
        for r in reads:
            r.r.append(op)
        for w in writes:
            w.w = op
            w.r = []
        op.idx = len(self.ops[eng])
        self.ops[eng].append(op)
        return op

    def barrier(self):
        lasts = []
        for e in self.ENGS:
            comp = [o for o in self.ops[e] if not o.dma]
            if comp:
                lasts.append(comp[-1])
        for key, lst in self.dma_keys.items():
            if lst:
                lasts.append(lst[-1])
        for e in self.ENGS:
            op = Op(e, (lambda eh: eh.nop()), False, None)
            op.deps = [(d, "raw") for d in lasts]
            op.idx = len(self.ops[e])
            self.ops[e].append(op)

    def emit(self):
        nc = self.nc
        for e in self.ENGS:
            for op in self.ops[e]:
                need = {}
                for (d, kind) in op.deps:
                    if d is op:
                        continue
                    if (not d.dma) and (not op.dma) and d.eng == op.eng and kind != "raw":
                        continue
                    k = ("dma", d.key) if d.dma else ("eng", d.eng)
                    if k not in need or (d.dma and need[k].cnt < d.cnt) or ((not d.dma) and need[k].idx < d.idx):
                        need[k] = d
                op.deps = need
        for e in self.ENGS:
            for op in self.ops[e]:
                for k, d in op.deps.items():
                    if not d.dma:
                        d.sig = True
        for e in self.ENGS:
            c = 0
            for op in self.ops[e]:
                if not op.dma and op.sig:
                    c += 1
                    op.cnt = c
        sem_eng, sem_key = {}, {}
        stack = contextlib.ExitStack()
        for e in self.ENGS:
            sem_eng[e] = stack.enter_context(nc.semaphore("se_" + e))
        for key in self.dma_keys:
            sem_key[key] = stack.enter_context(nc.semaphore("sd_" + str(key)))
        for e in self.ENGS:
            seen = {}
            for op in self.ops[e]:
                for k, d in op.deps.items():
                    sem = sem_key[d.key] if d.dma else sem_eng[d.eng]
                    if seen.get(k, 0) >= d.cnt:
                        continue
                    seen[k] = d.cnt
                    op.waits.append((sem, d.cnt))
        nops = sum(len(self.ops[e]) for e in self.ENGS)
        nw = sum(len(op.waits) for e in self.ENGS for op in self.ops[e])
        print(f"[prog] ops={nops} waits={nw} dma_keys={len(self.dma_keys)}", flush=True)

        def run(e_name, eh):
            for op in self.ops[e_name]:
                for (sem, v) in op.waits:
                    eh.wait_ge(sem, v)
                ins = op.fn(eh)
                if op.sig:
                    if op.dma:
                        ins.then_inc(sem_key[op.key], 16)
                    else:
                        ins.then_inc(sem_eng[e_name], 1)
            if e_name == "sp":
                for key, lst in self.dma_keys.items():
                    if lst:
                        eh.wait_ge(sem_key[key], 16 * len(lst))

        with nc.Block() as block:
            @block.tensor
            def _(t):
                run("pe", t)

            @block.scalar
            def _(s):
                run("act", s)

            @block.vector
            def _(v):
                run("dve", v)

            @block.gpsimd
            def _(g):
                run("pool", g)

            @block.sync
            def _(sy):
                run("sp", sy)
        stack.close()


class T:
    __slots__ = ("t", "r")

    def __init__(self, t, name):
        self.t = t
        self.r = Reg(name)

    def __getitem__(self, k):
        return self.t[k]


class Ctx:
    def __init__(self, nc, P):
        self.nc, self.P = nc, P
        self.n = 0

    def sb(self, st, shape, dt, name=None):
        self.n += 1
        name = f"{name or 't'}_{self.n}"
        return T(st.enter_context(self.nc.sbuf_tensor(name, shape, dt)), name)

    def ps(self, st, shape, dt, name=None):
        self.n += 1
        name = f"{name or 'p'}_{self.n}"
        return T(st.enter_context(self.nc.psum_tensor(name, shape, dt)), name)

    def dram(self, name, shape, dt, kind="Internal"):
        return T(self.nc.dram_tensor(name, shape, dt, kind=kind).ap(), name)


def _regs(xs):
    return [x.r if isinstance(x, T) else x for x in xs]


class B:
    def __init__(self, cx):
        self.cx, self.P = cx, cx.P
        self.dq = 0

    def dma(self, out, in_, reads, writes, key, eng="sp"):
        self.P.add(eng, lambda e: e.dma_start(out=out, in_=in_), _regs(reads), _regs(writes), dma=True, key=key)

    def mm(self, out, lhsT, rhs, start, stop, reads, writes):
        self.P.add("pe", lambda e: e.matmul(out=out, lhsT=lhsT, rhs=rhs, start=start, stop=stop), _regs(reads), _regs(writes))

    def tr(self, out, in_, ident, reads, writes):
        self.P.add("pe", lambda e: e.transpose(out=out, in_=in_, identity=ident), _regs(reads), _regs(writes))

    def act(self, out, in_, func, reads, writes, bias=None, scale=None, accum=None, eng="act"):
        kw = {}
        if bias is not None:
            kw["bias"] = bias
        if scale is not None:
            kw["scale"] = scale
        if accum is not None:
            kw["accum_out"] = accum
        self.P.add("act", lambda e: e.activation(out=out, in_=in_, func=func, **kw), _regs(reads), _regs(writes))

    def copy(self, eng, out, in_, reads, writes):
        if eng == "act":
            self.P.add("act", lambda e: e.activation(out=out, in_=in_, func=AF.Copy), _regs(reads), _regs(writes))
        else:
            self.P.add(eng, lambda e: e.tensor_copy(out=out, in_=in_), _regs(reads), _regs(writes))

    def tt(self, eng, out, in0, in1, op, reads, writes):
        self.P.add(eng, lambda e: e.tensor_tensor(out=out, in0=in0, in1=in1, op=op), _regs(reads), _regs(writes))

    def ts(self, eng, out, in0, s1, s2, op0, op1, reads, writes):
        if op1 is None:
            self.P.add(eng, lambda e: e.tensor_scalar(out=out, in0=in0, scalar1=s1, scalar2=None, op0=op0), _regs(reads), _regs(writes))
        else:
            self.P.add(eng, lambda e: e.tensor_scalar(out=out, in0=in0, scalar1=s1, scalar2=s2, op0=op0, op1=op1), _regs(reads), _regs(writes))

    def stt(self, eng, out, in0, scalar, in1, op0, op1, reads, writes):
        self.P.add(eng, lambda e: e.scalar_tensor_tensor(out=out, in0=in0, scalar=scalar, in1=in1, op0=op0, op1=op1), _regs(reads), _regs(writes))

    def rstd(self, st_, c_in, c_tmp, c_out, inv_n):
        self.ts("dve", st_[:, c_tmp:c_tmp + 1], st_[:, c_in:c_in + 1], inv_n, EPS, ALU.mult, ALU.add, [st_], [st_])
        self.act(st_[:, c_tmp:c_tmp + 1], st_[:, c_tmp:c_tmp + 1], AF.Sqrt, [st_], [st_])
        self.P.add("dve", lambda e: e.reciprocal(out=st_[:, c_out:c_out + 1], in_=st_[:, c_tmp:c_tmp + 1]), [st_.r], [st_.r])

    def memset(self, eng, ap, val, writes):
        self.P.add(eng, lambda e: e.memset(ap, val), [], _regs(writes))

    def reduce(self, eng, out, in_, op, reads, writes):
        self.P.add(eng, lambda e: e.tensor_reduce(out=out, in_=in_, axis=AX.X, op=op), _regs(reads), _regs(writes))


def bcast_rows(ap1d, n, p=128):
    return ap1d.rearrange("(o n) -> o n", o=1).broadcast_to([p, n])


def build(debug=False, phases=("A",)):
    nc = bass.Bass("TRN2", target_bir_lowering=False)
    P = Prog(nc)
    cx = Ctx(nc, P)
    b = B(cx)
    kio = "ExternalOutput" if debug else "Internal"

    def din(name, shape, dt=F32):
        return T(nc.dram_tensor(name, shape, dt, kind="ExternalInput").ap(), name)

    def dout(name, shape, dt=F32):
        return T(nc.dram_tensor(name, shape, dt, kind="ExternalOutput").ap(), name)

    x_ctx = din("x_ctx", [L_CTX, D])
    x_own = din("x_own", [L_OWN, D])
    x_smp = din("x_smp", [128, D])
    cs_ctx = din("cs_ctx", [L_CTX, 32])
    cs_own = din("cs_own", [L_OWN, 32])
    cs_smp = din("cs_smp", [128, 32])
    norm_in = din("norm_in", [D])
    w_in = din("w_in", [D, NCOL])
    mla_q_norm = din("mla_q_norm", [384])
    mla_kv_norm = din("mla_kv_norm", [256])
    mla_w_uq = din("mla_w_uq", [384, 1536])
    mla_w_uk = din("mla_w_uk", [256, 1024])
    ssm_lambda_re = din("ssm_lambda_re", [64, 64])
    ssm_lambda_im = din("ssm_lambda_im", [64, 64])
    ssm_log_dt = din("ssm_log_dt", [64])
    ssm_b_re = din("ssm_b_re", [64, 64, 16])
    ssm_b_im = din("ssm_b_im", [64, 64, 16])
    ssm_c_re = din("ssm_c_re", [64, 16, 64])
    ssm_c_im = din("ssm_c_im", [64, 16, 64])
    ssm_d = din("ssm_d", [D])
    state_s = din("state_s", [16, 64, 64, 2])
    mla_w_uv = din("mla_w_uv", [256, 1024])
    ctx_bias = din("ctx_bias", [128, 1])
    if "G" in phases:
        smp_mask = din("smp_mask", [16, 128, 128])
        pt_core = din("pt_core", [16, 128], I32)
        cache_ckv = din("cache_ckv", [20480, 128 * 256])
        cache_krope = din("cache_krope", [20480, 128 * 32])
    if "B" in phases:
        ssm_w_glu = din("ssm_w_glu", [D, D])
        ssm_b_glu = din("ssm_b_glu", [D])
        w_br_ssm = din("w_br_ssm", [D, D])
        w_br_attn = din("w_br_attn", [D, D])
        w_out = din("w_out", [D, D])
        norm_final = din("norm_final", [D])
        o_y_p = dout("o_y_p", [L_OWN, D])
        o_y_s = dout("o_y_s", [128, D])
    o_ssm_p = dout("o_ssm_p", [64, 64, 2])
    o_ssm_s = dout("o_ssm_s", [16, 64, 64, 2])
    o_ckv_p = dout("o_ckv_p", [L_OWN, 256])
    o_kr_p = dout("o_kr_p", [L_OWN, 32])
    o_ckv_s = dout("o_ckv_s", [128, 256])
    o_kr_s = dout("o_kr_s", [128, 32])
    NTOK = L_CTX + L_OWN + 128
    u_scr = cx.dram("u_scr", [NTOK, D], BF16, kio)
    g_scr = cx.dram("g_scr", [L_OWN + 128, 4096], BF16, kio)
    o_scr = cx.dram("o_scr", [L_OWN + 128, D], BF16, kio)
    ys_scr = cx.dram("ys_scr", [L_OWN + 128, D], BF16, kio)
    qT_scr = cx.dram("qT_scr", [NH, 96, L_OWN], BF16, kio)
    ckvT_scr = cx.dram("ckvT_scr", [2, 128, L_CTX + L_OWN], BF16, kio)
    krT_scr = cx.dram("krT_scr", [32, L_CTX + L_OWN], BF16, kio)
    qlT_scr = cx.dram("qlT_scr", [2, 128, 16, 128], BF16, kio)
    qrT_scr = cx.dram("qrT_scr", [32, 16, 128], BF16, kio)
    ckvs_scr = cx.dram("ckvs_scr", [128, 256], BF16, kio)
    ckvsT_scr = cx.dram("ckvsT_scr", [2, 128, 128], BF16, kio)
    krsT_scr = cx.dram("krsT_scr", [32, 128], BF16, kio)

    if debug:
        dbg_sm = dout("dbg_sm", [16, 128, 12])
        dbg_acc = dout("dbg_acc", [16, 128, 256])
        dbg_L = dout("dbg_L", [128, 2304])
        dbg_T = dout("dbg_T", [128, 8192], BF16)
        dbg_R = dout("dbg_R", [128, 8192], BF16)
        dbg_O = dout("dbg_O", [128, 8192], BF16)
    with contextlib.ExitStack() as top:
        ident = cx.sb(top, [128, 128], BF16, "ident")
        identf = cx.sb(top, [128, 128], F32, "identf")
        b.memset("pool", identf[:], 0.0, [identf])
        P.add("pool", lambda e: e.affine_select(out=identf[:], in_=identf[:], pattern=[[-1, 128]], compare_op=ALU.not_equal,
                                                fill=1.0, base=0, channel_multiplier=1), [identf.r], [identf.r])
        b.copy("dve", ident[:], identf[:], [identf], [ident])

        if "A" in phases:
            phase_A(nc, cx, b, P, locals())
        if "S" in phases:
            phase_S(nc, cx, b, P, locals())
        if "P" in phases:
            phase_P(nc, cx, b, P, locals())
        if "G" in phases:
            phase_G(nc, cx, b, P, locals())
        if "B" in phases:
            phase_B(nc, cx, b, P, locals())
    P.emit()
    return nc


def phase_A(nc, cx, b, P, g):
    ident = g["ident"]
    x_ctx, x_own, x_smp = g["x_ctx"], g["x_own"], g["x_smp"]
    cs_ctx, cs_own, cs_smp = g["cs_ctx"], g["cs_own"], g["cs_smp"]
    w_in, norm_in = g["w_in"], g["norm_in"]
    with contextlib.ExitStack() as st:
        w_sb = cx.sb(st, [128, 8, NCOL], BF16, "w_in_sb")
        wuq_sb = cx.sb(st, [128, 3, 1536], BF16, "wuq")
        wukT_sb = cx.sb(st, [64, NH, 256], BF16, "wukT")
        gin = cx.sb(st, [128, D], F32, "gin")
        gq = cx.sb(st, [128, 384], F32, "gq")
        gkv = cx.sb(st, [128, 256], F32, "gkv")
        pz = [cx.ps(st, [128, 512], F32, "pz") for _ in range(4)]
        pT = [cx.ps(st, [128, 1024], BF16, "pT") for _ in range(2)]
        pq = [cx.ps(st, [128, 512], F32, "pq") for _ in range(2)]
        cnt = {"pz": 0, "pT": 0, "pq": 0, "cast": 0}

        def nxt(lst, k):
            cnt[k] += 1
            return lst[cnt[k] % len(lst)]

        cast_engs = ["pool", "dve", "act"]

        def cast(out, in_, reads, writes):
            cnt["cast"] += 1
            b.copy(cast_engs[cnt["cast"] % 3], out, in_, reads, writes)

        st0 = contextlib.ExitStack()
        stg = [cx.sb(st0, [128, 2896], F32, "stg") for _ in range(2)]
        w_v = w_in.t.rearrange("(k p) c -> k p c", p=128)
        for k in range(8):
            for h in range(2):
                s = stg[h]
                b.dma(s[:, :], w_v[k][:, h * 2896:(h + 1) * 2896], [], [s], key=f"stg{h}")
                cast(w_sb[:, k, h * 2896:(h + 1) * 2896], s[:, :], [s], [w_sb])
        wq_v = g["mla_w_uq"].t.rearrange("(k p) c -> k p c", p=128)
        for k in range(3):
            s = stg[k % 2]
            b.dma(s[:, 0:1536], wq_v[k], [], [s], key=f"stg{k % 2}")
            cast(wuq_sb[:, k, :], s[:, 0:1536], [s], [wuq_sb])
        wk_v = g["mla_w_uk"].t.rearrange("(k p) c -> k p c", p=128)
        wk_bf = cx.sb(st0, [128, 2, 1024], BF16, "wk_bf")
        for k in range(2):
            s = stg[k % 2]
            b.dma(s[:, 0:1024], wk_v[k], [], [s], key=f"stg{k % 2}")
            cast(wk_bf[:, k, :], s[:, 0:1024], [s], [wk_bf])
        for k in range(2):
            for hg in range(2):
                pt = nxt(pT, "pT")
                for hh in range(8):
                    h = hg * 8 + hh
                    b.tr(pt[0:64, hh * 128:(hh + 1) * 128], wk_bf[:, k, h * 64:(h + 1) * 64], ident[:], [wk_bf, ident], [pt])
                b.copy("dve", wukT_sb[:, hg * 8:(hg + 1) * 8, k * 128:(k + 1) * 128],
                       pt[0:64, :].rearrange("p (h c) -> p h c", h=8), [pt], [wukT_sb])
        st0.close()
        P.barrier()
        b.dma(gin[:], bcast_rows(norm_in.t, D), [], [gin], key="gin")
        b.dma(gq[:], bcast_rows(g["mla_q_norm"].t, 384), [], [gq], key="gq")
        b.dma(gkv[:], bcast_rows(g["mla_kv_norm"].t, 256), [], [gkv], key="gkv")

        NB = 2
        xt = [cx.sb(st, [128, D], F32, "xt") for _ in range(NB)]
        cs = [cx.sb(st, [128, 32], F32, "cs") for _ in range(NB)]
        junk1 = cx.sb(st, [128, D], BF16, "junk")
        junk = [junk1, junk1]
        stat = [cx.sb(st, [128, 8], F32, "stat") for _ in range(NB)]
        xn = [cx.sb(st, [128, D], BF16, "xn") for _ in range(NB)]
        xnT = [cx.sb(st, [128, 8, 128], BF16, "xnT") for _ in range(NB)]
        u_t = [cx.sb(st, [128, D], BF16, "u_t") for _ in range(NB)]
        g_t = [cx.sb(st, [128, 2048], BF16, "g_t") for _ in range(NB)]
        cqn = [cx.sb(st, [128, 384], BF16, "cqn") for _ in range(NB)]
        cqnT = [cx.sb(st, [128, 3, 128], BF16, "cqnT") for _ in range(NB)]
        q_t1 = cx.sb(st, [128, NH, 96], F32, "q_t")
        q_t = [q_t1, q_t1]
        qb_t = [cx.sb(st, [128, NH, 96], BF16, "qb_t") for _ in range(NB)]
        qr_tmp1 = cx.sb(st, [128, 4, NH, 16], F32, "qr_tmp")
        qr_tmp = [qr_tmp1, qr_tmp1]
        qT_t = [cx.sb(st, [96, NH, 128], BF16, "qT_t") for _ in range(NB)]
        ckv_t = [cx.sb(st, [128, 256], F32, "ckv_t") for _ in range(NB)]
        ckvb_t = [cx.sb(st, [128, 288], BF16, "ckvb_t") for _ in range(NB)]
        kr_t = [cx.sb(st, [128, 32], F32, "kr_t") for _ in range(NB)]
        kr_tmp = [cx.sb(st, [128, 64], F32, "kr_tmp") for _ in range(NB)]
        kvT_t = [cx.sb(st, [128, 3, 128], BF16, "kvT_t") for _ in range(NB)]
        qlT_t = cx.sb(st, [128, 2, 16, NH, 8], BF16, "qlT_t")

        tiles = [("ctx", i) for i in range(L_CTX // 128)] + [("own", i) for i in range(L_OWN // 128)] + [("smp", 0)]
        for ti, (kind, i) in enumerate(tiles):
            s_ = ti % NB
            X, CS = {"ctx": (x_ctx, cs_ctx), "own": (x_own, cs_own), "smp": (x_smp, cs_smp)}[kind]
            tok0 = {"ctx": 0, "own": L_CTX, "smp": L_CTX + L_OWN}[kind] + i * 128
            full = kind != "ctx"
            xt_, st_, xn_, xnT_ = xt[s_], stat[s_], xn[s_], xnT[s_]
            b.dma(xt_[:], X.t[i * 128:(i + 1) * 128, :], [], [xt_], key=f"xt{s_}")
            b.dma(cs[s_][:], CS.t[i * 128:(i + 1) * 128, :], [], [cs[s_]], key=f"cs{s_}")
            b.act(junk[s_][:], xt_[:], AF.Square, [xt_], [st_], accum=st_[:, 0:1])
            b.rstd(st_, 0, 1, 2, 1.0 / D)
            b.stt("dve", xn_[:], xt_[:], st_[:, 2:3], gin[:], ALU.mult, ALU.mult, [xt_, st_, gin], [xn_])
            pt = nxt(pT, "pT")
            for k in range(8):
                b.tr(pt[:, k * 128:(k + 1) * 128], xn_[:, k * 128:(k + 1) * 128], ident[:], [xn_, ident], [pt])
            b.copy("act", xnT_[:].rearrange("p k t -> p (k t)"), pt[:], [pt], [xnT_])

            def proj(c0, n):
                pzt = nxt(pz, "pz")
                for k in range(8):
                    b.mm(pzt[:, 0:n], xnT_[:, k, :], w_sb[:, k, c0:c0 + n], k == 0, k == 7, [xnT_, w_sb], [pzt])
                return pzt

            for hf in range(2):
                pzt = proj(C_US + hf * 512, 512)
                b.copy("dve" if hf else "act", u_t[s_][:, hf * 512:(hf + 1) * 512], pzt[:, :], [pzt], [u_t[s_]])
            b.dma(g["u_scr"].t[tok0:tok0 + 128, :], u_t[s_][:], [u_t[s_]], [], key=f"uo{s_}")
            pzt = proj(C_KV, 288)
            b.act(junk[s_][:, 0:256], pzt[:, 0:256], AF.Square, [pzt], [st_], accum=st_[:, 3:4])
            b.rstd(st_, 3, 4, 5, 1.0 / 256)
            b.stt("dve", ckv_t[s_][:], pzt[:, 0:256], st_[:, 5:6], gkv[:], ALU.mult, ALU.mult, [pzt, st_, gkv], [ckv_t[s_]])
            kt = kr_tmp[s_]
            c_, s2_ = cs[s_][:, 0:16], cs[s_][:, 16:32]
            b.tt("dve", kt[:, 0:16], pzt[:, 256:272], c_, ALU.mult, [pzt, cs[s_]], [kt])
            b.tt("dve", kt[:, 16:32], pzt[:, 272:288], s2_, ALU.mult, [pzt, cs[s_]], [kt])
            b.tt("dve", kt[:, 32:48], pzt[:, 256:272], s2_, ALU.mult, [pzt, cs[s_]], [kt])
            b.tt("dve", kt[:, 48:64], pzt[:, 272:288], c_, ALU.mult, [pzt, cs[s_]], [kt])
            b.tt("dve", kr_t[s_][:, 0:16], kt[:, 0:16], kt[:, 16:32], ALU.subtract, [kt], [kr_t[s_]])
            b.tt("dve", kr_t[s_][:, 16:32], kt[:, 32:48], kt[:, 48:64], ALU.add, [kt], [kr_t[s_]])
            if kind == "own":
                b.dma(g["o_ckv_p"].t[i * 128:(i + 1) * 128, :], ckv_t[s_][:], [ckv_t[s_]], [], key=f"ckvo{s_}")
                b.dma(g["o_kr_p"].t[i * 128:(i + 1) * 128, :], kr_t[s_][:], [kr_t[s_]], [], key=f"kro{s_}")
            if kind == "smp":
                b.dma(g["o_ckv_s"].t[:, :], ckv_t[s_][:], [ckv_t[s_]], [], key=f"ckvo{s_}")
                b.dma(g["o_kr_s"].t[:, :], kr_t[s_][:], [kr_t[s_]], [], key=f"kro{s_}")
            cb = ckvb_t[s_]
            b.copy("pool", cb[:, 0:256], ckv_t[s_][:], [ckv_t[s_]], [cb])
            b.copy("pool", cb[:, 256:288], kr_t[s_][:], [kr_t[s_]], [cb])
            pt = nxt(pT, "pT")
            b.tr(pt[:, 0:128], cb[:, 0:128], ident[:], [cb, ident], [pt])
            b.tr(pt[:, 128:256], cb[:, 128:256], ident[:], [cb, ident], [pt])
            b.tr(pt[0:32, 256:384], cb[:, 256:288], ident[:], [cb, ident], [pt])
            kvT = kvT_t[s_]
            b.copy("act", kvT[:, 0:2, :].rearrange("p k t -> p (k t)"), pt[:, 0:256], [pt], [kvT])
            b.copy("act", kvT[0:32, 2, :], pt[0:32, 256:384], [pt], [kvT])
            if kind == "smp":
                b.dma(g["ckvs_scr"].t[:, :], cb[:, 0:256], [cb], [], key="ckvs")
                for k in range(2):
                    b.dma(g["ckvsT_scr"].t[k], kvT[:, k, :], [kvT], [], key=f"kvTo{s_}")
                b.dma(g["krsT_scr"].t[:, :], kvT[0:32, 2, :], [kvT], [], key=f"kvTo{s_}")
            else:
                p0 = tok0
                for k in range(2):
                    b.dma(g["ckvT_scr"].t[k, :, p0:p0 + 128], kvT[:, k, :], [kvT], [], key=f"kvTo{s_}")
                b.dma(g["krT_scr"].t[:, p0:p0 + 128], kvT[0:32, 2, :], [kvT], [], key=f"kvTo{s_}")
            if not full:
                continue
            gr0 = i * 128 if kind == "own" else L_OWN
            glist = ((C_GS, 0, AF.Silu), (C_GS + 512, 512, AF.Silu), (C_GA, 1024, AF.Silu), (C_GA + 512, 1536, AF.Silu),
                     (C_MS, 2048, AF.Sigmoid), (C_MS + 512, 2560, AF.Sigmoid), (C_MA, 3072, AF.Sigmoid), (C_MA + 512, 3584, AF.Sigmoid))
            for hf in range(2):
                gt = g_t[hf]
                for (c0, o0, fn) in glist[hf * 4:(hf + 1) * 4]:
                    pzt = proj(c0, 512)
                    b.act(gt[:, o0 - hf * 2048:o0 - hf * 2048 + 512], pzt[:, :], fn, [pzt], [gt])
                b.dma(g["g_scr"].t[gr0:gr0 + 128, hf * 2048:(hf + 1) * 2048], gt[:], [gt], [], key=f"go{hf}")
            pzt = proj(C_CQ, 384)
            b.act(junk[s_][:, 0:384], pzt[:, 0:384], AF.Square, [pzt], [st_], accum=st_[:, 6:7])
            b.rstd(st_, 6, 7, 6, 1.0 / 384)
            b.stt("dve", cqn[s_][:], pzt[:, 0:384], st_[:, 6:7], gq[:], ALU.mult, ALU.mult, [pzt, st_, gq], [cqn[s_]])
            pt = nxt(pT, "pT")
            for k in range(3):
                b.tr(pt[:, k * 128:(k + 1) * 128], cqn[s_][:, k * 128:(k + 1) * 128], ident[:], [cqn[s_], ident], [pt])
            b.copy("act", cqnT[s_][:].rearrange("p k t -> p (k t)"), pt[:, 0:384], [pt], [cqnT[s_]])
            qv = q_t[s_][:].rearrange("p h d -> p (h d)")
            for cg in range(3):
                pqt = nxt(pq, "pq")
                for k in range(3):
                    b.mm(pqt[:, :], cqnT[s_][:, k, :], wuq_sb[:, k, cg * 512:(cg + 1) * 512], k == 0, k == 2, [cqnT[s_], wuq_sb], [pqt])
                b.copy("act" if cg % 2 else "dve", qv[:, cg * 512:(cg + 1) * 512], pqt[:, :], [pqt], [q_t[s_]])
            q3 = q_t[s_]
            tm = qr_tmp[s_]
            cb3 = cs[s_][:, 0:16].unsqueeze(1).broadcast_to([128, NH, 16])
            sb3 = cs[s_][:, 16:32].unsqueeze(1).broadcast_to([128, NH, 16])
            b.tt("pool", tm[:, 0], q3[:, :, 64:80], cb3, ALU.mult, [q3, cs[s_]], [tm])
            b.tt("pool", tm[:, 1], q3[:, :, 80:96], sb3, ALU.mult, [q3, cs[s_]], [tm])
            b.tt("pool", tm[:, 2], q3[:, :, 64:80], sb3, ALU.mult, [q3, cs[s_]], [tm])
            b.tt("pool", tm[:, 3], q3[:, :, 80:96], cb3, ALU.mult, [q3, cs[s_]], [tm])
            qb = qb_t[s_]
            b.copy("pool", qb[:, :, 0:64], q3[:, :, 0:64], [q3], [qb])
            b.tt("dve", qb[:, :, 64:80], tm[:, 0], tm[:, 1], ALU.subtract, [tm], [qb])
            b.tt("dve", qb[:, :, 80:96], tm[:, 2], tm[:, 3], ALU.add, [tm], [qb])
            qT = qT_t[s_]
            for hg in range(2):
                pt = nxt(pT, "pT")
                for hh in range(8):
                    b.tr(pt[0:96, hh * 128:(hh + 1) * 128], qb[:, hg * 8 + hh, :], ident[:], [qb, ident], [pt])
                b.copy("act" if hg else "dve", qT[:, hg * 8:(hg + 1) * 8, :].rearrange("p h t -> p (h t)"), pt[0:96, :], [pt], [qT])
            if kind == "own":
                b.dma(g["qT_scr"].t[:, :, i * 128:(i + 1) * 128].rearrange("h r t -> r h t"), qT[:, :, :], [qT], [], key=f"qTo{s_}")
            else:
                for hg in range(4):
                    for k in range(2):
                        pqt = nxt(pq, "pq")
                        for hh in range(4):
                            h = hg * 4 + hh
                            b.mm(pqt[:, hh * 128:(hh + 1) * 128], wukT_sb[:, h, k * 128:(k + 1) * 128], qT[0:64, h, :], True, True, [wukT_sb, qT], [pqt])
                        b.copy("dve", qlT_t[:, k, :, hg * 4:(hg + 1) * 4, :].rearrange("p b h t -> p h b t"),
                               pqt[:, :].rearrange("p (h b t) -> p h b t", h=4, b=16), [pqt], [qlT_t])
                for k in range(2):
                    b.dma(g["qlT_scr"].t[k], qlT_t[:, k].rearrange("p b h t -> p b (h t)"), [qlT_t], [], key="qlo")
                for h in range(NH):
                    b.dma(g["qrT_scr"].t[:, :, h * 8:(h + 1) * 8], qT[64:96, h, :].rearrange("p (b t) -> p b t", b=16), [qT], [], key="qro")
    P.barrier()


def rope_tables(pos):
    half = 16
    inv = (10000.0 ** (-np.arange(half, dtype=np.float32) * np.float32(2.0 / 32))).astype(np.float32)
    ang = pos.astype(np.float32)[:, None] * inv[None, :]
    return np.concatenate([np.cos(ang), np.sin(ang)], axis=1).astype(np.float32)


_NC_CACHE = {}


def make_in_maps(inp, phases=("A", "S", "P", "G", "B")):
    maps = []
    r_ = np.arange(128)
    hq, tq = r_ // 8, r_ % 8
    bk, tk = r_ // 8, r_ % 8
    smp_mask = np.where((bk[None, None, :] == np.arange(16)[:, None, None]) & (tk[None, None, :] <= tq[None, :, None]), 0.0, NEG).astype(np.float32)
    cck = np.asarray(inp["cache_ckv"], np.float32).reshape(20480, 128 * 256) if "G" in phases else None
    ckr = np.asarray(inp["cache_krope"], np.float32).reshape(20480, 128 * 32) if "G" in phases else None
    xp = np.asarray(inp["x_prompt"], np.float32)
    xs = np.asarray(inp["x_sample"], np.float32)
    for c in range(8):
        bi, h = c // 2, c % 2
        m = {}
        m["x_own"] = np.ascontiguousarray(xp[bi, h * L_OWN:(h + 1) * L_OWN])
        m["x_ctx"] = np.ascontiguousarray(xp[bi, 0:L_CTX]) if h == 1 else np.zeros((L_CTX, D), np.float32)
        m["x_smp"] = np.ascontiguousarray(xs[16 * c:16 * c + 16].reshape(128, D))
        m["cs_ctx"] = rope_tables(np.arange(L_CTX))
        m["cs_own"] = rope_tables(h * L_OWN + np.arange(L_OWN))
        m["cs_smp"] = rope_tables(np.tile(PAST + np.arange(8), 16))
        m["state_s"] = np.ascontiguousarray(np.asarray(inp["state_ssm"], np.float32)[16 * c:16 * c + 16])
        for k in ("norm_in", "w_in", "mla_q_norm", "mla_kv_norm", "ssm_lambda_re", "ssm_lambda_im", "ssm_log_dt",
                  "ssm_b_re", "ssm_b_im", "ssm_c_re", "ssm_c_im", "ssm_d"):
            m[k] = np.asarray(inp[k], np.float32)
        m["mla_w_uq"] = np.asarray(inp["mla_w_uq"], np.float32).reshape(384, 1536)
        m["mla_w_uk"] = np.asarray(inp["mla_w_uk"], np.float32).reshape(256, 1024)
        m["mla_w_uv"] = np.asarray(inp["mla_w_uv"], np.float32).reshape(256, 1024)
        m["ctx_bias"] = np.full((128, 1), 0.0 if h == 1 else NEG, np.float32)
        if "G" in phases:
            m["smp_mask"] = smp_mask
            m["pt_core"] = np.ascontiguousarray(np.asarray(inp["page_table"], np.int32)[16 * c:16 * c + 16])
            m["cache_ckv"] = cck
            m["cache_krope"] = ckr
        if "B" in phases:
            for k in ("ssm_w_glu", "ssm_b_glu", "w_br_ssm", "w_br_attn", "w_out", "norm_final"):
                m[k] = np.asarray(inp[k], np.float32)
        maps.append(m)
    return maps


def kernel(**inp):
    if "nc" not in _NC_CACHE:
        _NC_CACHE["nc"] = build(phases=("A", "S", "P", "G", "B"))
    nc = _NC_CACHE["nc"]
    maps = make_in_maps(inp)
    res = run_bass_kernel_spmd(nc, maps, core_ids=list(range(8)))
    R = res.results
    B_, L_ = 4, 4096
    y_p = np.zeros((B_, L_, D), np.float32)
    y_s = np.zeros((128, 8, D), np.float32)
    ckv_p = np.zeros((B_, L_, 256), np.float32)
    kr_p = np.zeros((B_, L_, 32), np.float32)
    ssm_p = np.zeros((B_, 64, 64, 2), np.float32)
    ckv_s = np.zeros((128, 8, 256), np.float32)
    kr_s = np.zeros((128, 8, 32), np.float32)
    ssm_s = np.zeros((128, 64, 64, 2), np.float32)
    for c in range(8):
        bi, h = c // 2, c % 2
        r = R[c]
        ckv_p[bi, h * L_OWN:(h + 1) * L_OWN] = r["o_ckv_p"]
        kr_p[bi, h * L_OWN:(h + 1) * L_OWN] = r["o_kr_p"]
        ckv_s[16 * c:16 * c + 16] = r["o_ckv_s"].reshape(16, 8, 256)
        kr_s[16 * c:16 * c + 16] = r["o_kr_s"].reshape(16, 8, 32)
        if "o_y_p" in r:
            y_p[bi, h * L_OWN:(h + 1) * L_OWN] = r["o_y_p"]
            y_s[16 * c:16 * c + 16] = r["o_y_s"].reshape(16, 8, D)
        if "o_ssm_s" in r:
            ssm_s[16 * c:16 * c + 16] = r["o_ssm_s"]
            if h == 1:
                ssm_p[bi] = r["o_ssm_p"]
    return (y_p, y_s, ckv_p, kr_p, ssm_p, ckv_s, kr_s, ssm_s)


def phase_S(nc, cx, b, P, g):
    ident, identf = g["ident"], g["identf"]
    NG = 64
    GP = 16
    with contextlib.ExitStack() as st:
        swapf = cx.sb(st, [128, 128], F32, "swapf")
        sgn = cx.sb(st, [128, 1], F32, "sgn")
        b.memset("pool", swapf[:], 0.0, [swapf])
        for base in (64, -64):
            P.add("pool", lambda e, base=base: e.affine_select(out=swapf[:], in_=swapf[:], pattern=[[-1, 128]], compare_op=ALU.not_equal,
                                                               fill=1.0, base=base, channel_multiplier=1), [swapf.r], [swapf.r])
        b.memset("pool", sgn[0:64, :], 1.0, [sgn])
        b.memset("pool", sgn[64:128, :], -1.0, [sgn])
        pf = [cx.ps(st, [128, 512], F32, "pf") for _ in range(3)]
        pb = [cx.ps(st, [128, 1024], BF16, "pb") for _ in range(2)]
        cnt = {"pf": 0, "pb": 0}

        def nxt(lst, k):
            cnt[k] += 1
            return lst[cnt[k] % len(lst)]

        lre = cx.sb(st, [128, NG], F32, "lre")
        lim = cx.sb(st, [128, NG], F32, "lim")
        dt = cx.sb(st, [128, NG], F32, "dt")
        wk = cx.sb(st, [128, 12, NG], F32, "wk")
        Lr = cx.sb(st, [128, 9, NG], F32, "Lr")
        Li = cx.sb(st, [128, 9, NG], F32, "Li")
        Ar = cx.sb(st, [128, 9, NG], F32, "Ar")
        Ai = cx.sb(st, [128, 9, NG], F32, "Ai")
        Aiu = cx.sb(st, [128, 9, NG], F32, "Aiu")
        Lis = cx.sb(st, [128, 9, NG], F32, "Lis")
        O_all = cx.sb(st, [128, NG, 8, 16], BF16, "O_all")
        Dv = cx.sb(st, [128, NG], F32, "Dv")
        T_all = cx.sb(st, [128, NG, 128], BF16, "T_all")
        R_all = cx.sb(st, [128, NG, 128], BF16, "R_all")
        pre = contextlib.ExitStack()
        ld = cx.sb(pre, [64, 2, 128], F32, "ld")
        for j, nm in enumerate(("ssm_lambda_re", "ssm_lambda_im")):
            for hf in range(2):
                b.dma(ld[:, j, hf * 64:(hf + 1) * 64], g[nm].t[:, :], [], [ld], key="ld")
        for j, dst in enumerate((lre, lim)):
            pt = nxt(pf, "pf")
            b.tr(pt[:, 0:64], ld[:, j, :], identf[0:64, 0:64], [ld, identf], [pt])
            b.copy("dve", dst[:], pt[:, 0:64], [pt], [dst])
        b.dma(dt[:], bcast_rows(g["ssm_log_dt"].t, NG), [], [dt], key="dt")
        b.act(dt[:], dt[:], AF.Exp, [dt], [dt])
        a_, th, mag, r1, r2, sn, cs_ = (wk[:, i, :] for i in range(7))
        b.tt("dve", a_, lre[:], dt[:], ALU.mult, [lre, dt], [wk])
        b.tt("dve", th, lim[:], dt[:], ALU.mult, [lim, dt], [wk])
        b.act(mag, a_, AF.Exp, [wk], [wk])
        b.ts("dve", r1, th, 1.0 / 64, None, ALU.mult, None, [wk], [wk])
        b.ts("dve", r2, th, 1.0 / 64, 0.5 * math.pi, ALU.mult, ALU.add, [wk], [wk])
        b.act(sn, r1, AF.Sin, [wk], [wk])
        b.act(cs_, r2, AF.Sin, [wk], [wk])
        for _ in range(6):
            b.tt("dve", r1, cs_, cs_, ALU.mult, [wk], [wk])
            b.tt("dve", r2, sn, sn, ALU.mult, [wk], [wk])
            b.tt("dve", wk[:, 7, :], cs_, sn, ALU.mult, [wk], [wk])
            b.tt("dve", cs_, r1, r2, ALU.subtract, [wk], [wk])
            b.ts("dve", sn, wk[:, 7, :], 2.0, None, ALU.mult, None, [wk], [wk])
        b.memset("dve", Lr[:, 0, :], 1.0, [Lr])
        b.memset("dve", Li[:, 0, :], 0.0, [Li])
        b.tt("dve", Lr[:, 1, :], mag, cs_, ALU.mult, [wk], [Lr])
        b.tt("dve", Li[:, 1, :], mag, sn, ALU.mult, [wk], [Li])
        t0, t1 = wk[:, 7, :], wk[:, 8, :]

        def cmul(or_, oi_, ar, ai, br, bi, regs_w):
            b.tt("dve", t0, ar, br, ALU.mult, [Lr, Li, Ar, Aiu, wk], [wk])
            b.tt("dve", t1, ai, bi, ALU.mult, [Lr, Li, Ar, Aiu, wk], [wk])
            b.tt("dve", or_, t0, t1, ALU.subtract, [wk], regs_w)
            b.tt("dve", t0, ar, bi, ALU.mult, [Lr, Li, Ar, Aiu, wk], [wk])
            b.tt("dve", t1, ai, br, ALU.mult, [Lr, Li, Ar, Aiu, wk], [wk])
            b.tt("dve", oi_, t0, t1, ALU.add, [wk], regs_w)

        for e in range(1, 8):
            cmul(Lr[:, e + 1, :], Li[:, e + 1, :], Lr[:, e, :], Li[:, e, :], Lr[:, 1, :], Li[:, 1, :], [Lr, Li])
        b.copy("dve", Ar[:, 0, :], Lr[:, 8, :], [Lr], [Ar])
        b.copy("dve", Aiu[:, 0, :], Li[:, 8, :], [Li], [Aiu])
        for l in range(8):
            cmul(Ar[:, l + 1, :], Aiu[:, l + 1, :], Ar[:, l, :], Aiu[:, l, :], Ar[:, l, :], Aiu[:, l, :], [Ar, Aiu])
        b.ts("dve", Ai[:].rearrange("p l g -> p (l g)"), Aiu[:].rearrange("p l g -> p (l g)"), sgn[:, 0:1], None, ALU.mult, None, [Aiu, sgn], [Ai])
        cr, ci, den, nr = wk[:, 9, :], wk[:, 10, :], wk[:, 11, :], wk[:, 0, :]
        b.ts("dve", nr, Lr[:, 1, :], -1.0, None, ALU.add, None, [Lr], [wk])
        b.tt("dve", t0, lre[:], lre[:], ALU.mult, [lre], [wk])
        b.tt("dve", t1, lim[:], lim[:], ALU.mult, [lim], [wk])
        b.tt("dve", den, t0, t1, ALU.add, [wk], [wk])
        P.add("dve", lambda e: e.reciprocal(out=den, in_=den), [wk.r], [wk.r])
        b.tt("dve", t0, nr, lre[:], ALU.mult, [wk, lre], [wk])
        b.tt("dve", t1, Li[:, 1, :], lim[:], ALU.mult, [Li, lim], [wk])
        b.tt("dve", cr, t0, t1, ALU.add, [wk], [wk])
        b.tt("dve", cr, cr, den, ALU.mult, [wk], [wk])
        b.tt("dve", t0, Li[:, 1, :], lre[:], ALU.mult, [Li, lre], [wk])
        b.tt("dve", t1, nr, lim[:], ALU.mult, [wk, lim], [wk])
        b.tt("dve", ci, t0, t1, ALU.subtract, [wk], [wk])
        b.tt("dve", ci, ci, den, ALU.mult, [wk], [wk])
        cis = wk[:, 1, :]
        b.ts("dve", cis, ci, sgn[:, 0:1], None, ALU.mult, None, [wk, sgn], [wk])
        b.ts("dve", Lis[:].rearrange("p l g -> p (l g)"), Li[:].rearrange("p l g -> p (l g)"), sgn[:, 0:1], None, ALU.mult, None, [Li, sgn], [Lis])

        Bx = cx.sb(pre, [128, NG, 16], F32, "Bx")
        By = cx.sb(pre, [128, NG, 16], F32, "By")
        bre = g["ssm_b_re"].t.rearrange("g p c -> p g c")
        bim = g["ssm_b_im"].t.rearrange("g p c -> p g c")
        b.dma(Bx[0:64], bre, [], [Bx], key="Bx")
        b.dma(Bx[64:128], bim, [], [Bx], key="Bx")
        b.dma(By[0:64], bim, [], [By], key="By")
        b.dma(By[64:128], bre, [], [By], key="By")
        BS = cx.sb(pre, [128, NG, 16], F32, "BS")
        BSp = cx.sb(pre, [128, NG, 16], F32, "BSp")
        tmpB = cx.sb(pre, [128, NG, 16], F32, "tmpB")

        def bc(ap2d):
            return ap2d.unsqueeze(2).broadcast_to([128, NG, 16])

        b.tt("pool", tmpB[:], By[:], bc(cis), ALU.mult, [By, wk], [tmpB])
        b.tt("pool", BS[:], Bx[:], bc(cr), ALU.mult, [Bx, wk], [BS])
        b.tt("pool", BS[:], BS[:], tmpB[:], ALU.subtract, [BS, tmpB], [BS])
        b.tt("pool", tmpB[:], Bx[:], bc(cis), ALU.mult, [Bx, wk], [tmpB])
        b.tt("pool", BSp[:], By[:], bc(cr), ALU.mult, [By, wk], [BSp])
        b.tt("pool", BSp[:], BSp[:], tmpB[:], ALU.add, [BSp, tmpB], [BSp])
        GT = cx.sb(pre, [128, NG, 15, 16], BF16, "GT")
        b.memset("pool", GT[:, :, 8:15, :], 0.0, [GT])
        tmpG = cx.sb(pre, [128, NG, 16], F32, "tmpG")
        for e in range(8):
            b.tt("pool", tmpB[:], BSp[:], bc(Lis[:, e, :]), ALU.mult, [BSp, Lis], [tmpB])
            b.tt("dve", tmpG[:], BS[:], bc(Lr[:, e, :]), ALU.mult, [BS, Lr], [tmpG])
            b.tt("dve", GT[:, :, 7 - e, :], tmpG[:], tmpB[:], ALU.subtract, [tmpG, tmpB], [GT])
        Cx = cx.sb(pre, [128, NG, 16], F32, "Cx")
        Cy = cx.sb(pre, [128, NG, 16], F32, "Cy")
        Cxb = cx.sb(pre, [128, NG, 16], BF16, "Cxb")
        cin = [cx.sb(pre, [128, 128], F32, "cin") for _ in range(2)]
        cre = g["ssm_c_re"].t.rearrange("g c p -> (g c) p")
        cim = g["ssm_c_im"].t.rearrange("g c p -> (g c) p")
        q = 0
        for j in range(8):
            for which in range(2):
                ci_ = cin[q % 2]
                q += 1
                a0, a1 = (cre, cim) if which == 0 else (cim, cre)
                b.dma(ci_[:, 0:64], a0[j * 128:(j + 1) * 128, :], [], [ci_], key=f"cin{q % 2}")
                b.dma(ci_[:, 64:128], a1[j * 128:(j + 1) * 128, :], [], [ci_], key=f"cin{q % 2}")
                pt = nxt(pf, "pf")
                b.tr(pt[:, 0:128], ci_[:], identf[:], [ci_, identf], [pt])
                dst = Cx if which == 0 else Cy
                dv = dst[:, j * 8:(j + 1) * 8, :].rearrange("p g c -> p (g c)")
                if which == 0:
                    b.ts("dve", dv, pt[:, 0:128], sgn[:, 0:1], None, ALU.mult, None, [pt, sgn], [dst])
                else:
                    b.copy("dve", dv, pt[:, 0:128], [pt], [dst])
        b.copy("pool", Cxb[:], Cx[:], [Cx], [Cxb])
        for t in range(8):
            b.tt("pool", tmpB[:], Cy[:], bc(Li[:, t + 1, :]), ALU.mult, [Cy, Li], [tmpB])
            b.tt("dve", tmpG[:], Cx[:], bc(Lr[:, t + 1, :]), ALU.mult, [Cx, Lr], [tmpG])
            b.tt("dve", O_all[:, :, t, :], tmpG[:], tmpB[:], ALU.subtract, [tmpG, tmpB], [O_all])
        dsrc = g["ssm_d"].t.rearrange("(g c) -> c g", c=16)
        for s in range(8):
            P.add("sp", lambda e, s=s: e.dma_start(out=Dv[s * 16:(s + 1) * 16, :], in_=dsrc, allow_slow_non_contiguous=True), [], [Dv.r], dma=True, key="Dv")
        for gi in range(NG):
            pt = nxt(pf, "pf")
            for t in range(8):
                b.mm(pt[:, t * 16:(t + 1) * 16], GT[:, gi, 7 - t:15 - t, :].rearrange("p s c -> p (s c)"), Cxb[:, gi, :], True, True, [GT, Cxb], [pt])
            b.stt("dve", T_all[:, gi, :], identf[:], Dv[:, gi:gi + 1], pt[:, 0:128], ALU.mult, ALU.add, [identf, Dv, pt], [T_all])
            if gi % 8 == 0:
                ptb = nxt(pb, "pb")
            b.tr(ptb[:, (gi % 8) * 128:(gi % 8 + 1) * 128], GT[:, gi, 0:8, :].rearrange("p s c -> p (s c)"), ident[:], [GT, ident], [ptb])
            if gi % 8 == 7:
                b.copy("act", R_all[:, gi - 7:gi + 1, :].rearrange("p g s -> p (g s)"), ptb[:], [ptb], [R_all])

        if g.get("debug"):
            b.dma(g["dbg_L"].t[:, 0:576], Lr[:].rearrange("p l g -> p (l g)"), [Lr], [], key="dbg")
            b.dma(g["dbg_L"].t[:, 576:1152], Li[:].rearrange("p l g -> p (l g)"), [Li], [], key="dbg")
            b.dma(g["dbg_L"].t[:, 1152:1728], Ar[:].rearrange("p l g -> p (l g)"), [Ar], [], key="dbg")
            b.dma(g["dbg_L"].t[:, 1728:2304], Ai[:].rearrange("p l g -> p (l g)"), [Ai], [], key="dbg")
            b.dma(g["dbg_T"].t[:, :], T_all[:].rearrange("p g c -> p (g c)"), [T_all], [], key="dbg")
            b.dma(g["dbg_R"].t[:, :], R_all[:].rearrange("p g c -> p (g c)"), [R_all], [], key="dbg")
            b.dma(g["dbg_O"].t[:, :], O_all[:].rearrange("p g t c -> p (g t c)"), [O_all], [], key="dbg")
        pre.close()
        P.barrier()
        if STAGE == "pre":
            return
        Mtmp = [cx.sb(st, [128, 128], F32, "Mtmp") for _ in range(2)]
        mt = [0]

        def build_M(eng, M, sr, si):
            if eng == "dve":
                b.ts(eng, M[:], identf[:], sr, None, ALU.mult, None, [identf, Ar, Ai], [M])
                b.stt(eng, M[:], swapf[:], si, M[:], ALU.mult, ALU.add, [swapf, Ar, Ai, M], [M])
            else:
                tmp = Mtmp[mt[0] % 2]
                tmp2 = Mtmp[(mt[0] + 1) % 2]
                b.ts(eng, tmp[:], identf[:], sr, None, ALU.mult, None, [identf, Ar, Ai], [tmp])
                b.ts(eng, tmp2[:], swapf[:], si, None, ALU.mult, None, [swapf, Ar, Ai], [tmp2])
                b.tt(eng, M[:], tmp[:], tmp2[:], ALU.add, [tmp, tmp2], [M])

        NCH = (L_CTX + L_OWN) // 8
        NOWN = L_OWN // 8
        uview = g["u_scr"].t[0:L_CTX + L_OWN, :].rearrange("(k s) c -> k s c", s=8)
        yview = g["ys_scr"].t[0:L_OWN, :].rearrange("(k s) c -> k s c", s=8)
        usv = g["u_scr"].t[L_CTX + L_OWN:L_CTX + L_OWN + 128, :].rearrange("(k s) c -> k s c", s=8)
        ysv = g["ys_scr"].t[L_OWN:L_OWN + 128, :].rearrange("(k s) c -> k s c", s=8)
        Up = [cx.sb(st, [128, GP, 8, 16], BF16, "Up") for _ in range(4)]
        Ups = cx.sb(st, [16, GP, 8, 16], BF16, "Ups")
        Uraw = [cx.sb(st, [128, 8, GP * 16], BF16, "Uraw") for _ in range(2)]
        Uraw_s = cx.sb(st, [16, 8, GP * 16], BF16, "Uraw_s")
        Yp = [cx.sb(st, [128, 8, GP * 16], BF16, "Yp") for _ in range(2)]
        Yps = cx.sb(st, [16, 8, GP * 16], BF16, "Yps")
        U8 = [cx.sb(st, [128, NCH], BF16, "U8") for _ in range(2)]
        U8s = [cx.sb(st, [128, 16], BF16, "U8s") for _ in range(2)]
        Ssb = [cx.sb(st, [128, NCH], BF16, "Ssb") for _ in range(2)]
        Msb = [cx.sb(st, [128, 128], BF16, "Msb") for _ in range(4)]
        Hsb = [cx.sb(st, [128, NOWN], BF16, "Hsb") for _ in range(2)]
        M0f = [cx.sb(st, [128, 128], F32, "M0f") for _ in range(2)]
        Y8 = [cx.sb(st, [128, NOWN], BF16, "Y8") for _ in range(2)]
        Y8s = [cx.sb(st, [128, 16], BF16, "Y8s") for _ in range(2)]
        hl = cx.sb(st, [128, NG], F32, "hl")
        hlo = cx.sb(st, [64, 64, 2], F32, "hlo")
        st_in = cx.sb(st, [16, GP, 64, 2], F32, "st_in")
        st_out = cx.sb(st, [16, GP, 64, 2], F32, "st_out")
        st_r = cx.sb(st, [16, GP, 2, 64], F32, "st_r")
        h0T = [cx.sb(st, [128, 16], F32, "h0T") for _ in range(2)]
        h0Tb = [cx.sb(st, [128, 16], BF16, "h0Tb") for _ in range(2)]
        hoT = [cx.sb(st, [128, 16], F32, "hoT") for _ in range(2)]
        mcnt = 0
        for half in range(NG // GP):
            c0 = half * GP * 16
            for j in range(4):
                raw = Uraw[j % 2]
                b.dma(raw[:], uview[j * 128:(j + 1) * 128, :, c0:c0 + GP * 16], [], [raw], key=f"Uraw{j % 2}")
                b.copy("pool" if j % 2 else "act", Up[j][:], raw[:].rearrange("k s (g c) -> k g s c", c=16), [raw], [Up[j]])
            b.dma(Uraw_s[:], usv[:, :, c0:c0 + GP * 16], [], [Uraw_s], key="Ups")
            b.copy("pool", Ups[:], Uraw_s[:].rearrange("k s (g c) -> k g s c", c=16), [Uraw_s], [Ups])
            b.dma(st_in[:].rearrange("b g p r -> b (g p r)"), g["state_s"].t[:, half * GP:(half + 1) * GP, :, :].rearrange("b g p r -> b (g p r)"), [], [st_in], key="st_in")
            b.copy("pool", st_r[:], st_in[:].rearrange("b g p r -> b g r p"), [st_in], [st_r])
            for gl in range(GP):
                gi = half * GP + gl
                s_ = gi % 2
                ptb = nxt(pb, "pb")
                for j in range(4):
                    b.tr(ptb[:, j * 128:(j + 1) * 128], Up[j][:, gl, :, :].rearrange("k s c -> k (s c)"), ident[:], [Up[j], ident], [ptb])
                b.tr(ptb[:, 512:528], Ups[:, gl, :, :].rearrange("k s c -> k (s c)"), ident[0:16, 0:16], [Ups, ident], [ptb])
                b.copy("act", U8[s_][:], ptb[:, 0:512], [ptb], [U8[s_]])
                b.copy("dve", U8s[s_][:], ptb[:, 512:528], [ptb], [U8s[s_]])
                if STAGE == "m1":
                    continue
                pS = nxt(pf, "pf")
                b.mm(pS[:, :], R_all[:, gi, :], U8[s_][:], True, False, [R_all, U8[s_]], [pS])
                for l in range(9):
                    d = 1 << l
                    b.copy("act" if l % 2 else "dve", Ssb[s_][:], pS[:, :], [pS], [Ssb[s_]])
                    M = Msb[mcnt % 4]
                    mcnt += 1
                    build_M("pool" if l % 3 else "dve", M, Ar[:, l, gi:gi + 1], Ai[:, l, gi:gi + 1])
                    b.mm(pS[:, d:NCH], M[:], Ssb[s_][:, 0:NCH - d], False, l == 8, [M, Ssb[s_]], [pS])
                b.copy("dve", hl[:, gi:gi + 1], pS[:, NCH - 1:NCH], [pS], [hl])
                if STAGE == "m2":
                    continue
                pY = nxt(pf, "pf")
                b.mm(pY[:, 0:NOWN], T_all[:, gi, :], U8[s_][:, NCH - NOWN:NCH], True, False, [T_all, U8[s_]], [pY])
                b.copy("act", Hsb[s_][:], pS[:, NCH - NOWN - 1:NCH - 1], [pS], [Hsb[s_]])
                b.mm(pY[:, 0:NOWN], O_all[:, gi, :, :].rearrange("p t c -> p (t c)"), Hsb[s_][:], False, True, [O_all, Hsb[s_]], [pY])
                b.copy("dve", Y8[s_][:], pY[:, 0:NOWN], [pY], [Y8[s_]])
                if STAGE == "m3":
                    continue
                ptb2 = nxt(pb, "pb")
                for j in range(2):
                    b.tr(ptb2[:, j * 128:(j + 1) * 128], Y8[s_][:, j * 128:(j + 1) * 128], ident[:], [Y8[s_], ident], [ptb2])
                for j in range(2):
                    b.copy("dve", Yp[j][:, :, gl * 16:(gl + 1) * 16],
                           ptb2[:, j * 128:(j + 1) * 128].rearrange("p (t c) -> p t c", t=8), [ptb2], [Yp[j]])
                if STAGE != "nosmp":
                    pH = nxt(pf, "pf")
                    b.tr(pH[:, 0:16], st_r[:, gl, :, :].rearrange("b r p -> b (r p)"), identf[0:16, 0:16], [st_r, identf], [pH])
                    b.copy("dve", h0T[s_][:], pH[:, 0:16], [pH], [h0T[s_]])
                    b.copy("dve", h0Tb[s_][:], pH[:, 0:16], [pH], [h0Tb[s_]])
                    Mf = M0f[s_]
                    build_M("pool", Mf, Ar[:, 0, gi:gi + 1], Ai[:, 0, gi:gi + 1])
                    pX = nxt(pf, "pf")
                    b.mm(pX[:, 0:16], R_all[:, gi, :], U8s[s_][:], True, False, [R_all, U8s[s_]], [pX])
                    b.mm(pX[:, 0:16], Mf[:], h0T[s_][:], False, True, [Mf, h0T[s_]], [pX])
                    b.mm(pX[:, 16:32], T_all[:, gi, :], U8s[s_][:], True, False, [T_all, U8s[s_]], [pX])
                    b.mm(pX[:, 16:32], O_all[:, gi, :, :].rearrange("p t c -> p (t c)"), h0Tb[s_][:], False, True, [O_all, h0Tb[s_]], [pX])
                    b.copy("dve", hoT[s_][:], pX[:, 0:16], [pX], [hoT[s_]])
                    b.copy("dve", Y8s[s_][:], pX[:, 16:32], [pX], [Y8s[s_]])
                    pO = nxt(pf, "pf")
                    b.tr(pO[0:16, 0:128], hoT[s_][:], identf[:], [hoT[s_], identf], [pO])
                    b.copy("dve", st_out[:, gl, :, :].rearrange("b p r -> b r p"), pO[0:16, 0:128].rearrange("b (r p) -> b r p", r=2), [pO], [st_out])
                    ptb3 = nxt(pb, "pb")
                    b.tr(ptb3[0:16, 0:128], Y8s[s_][:], ident[:], [Y8s[s_], ident], [ptb3])
                    b.copy("dve", Yps[:, :, gl * 16:(gl + 1) * 16], ptb3[0:16, 0:128].rearrange("p (t c) -> p t c", t=8), [ptb3], [Yps])
            for j in range(2):
                b.dma(yview[j * 128:(j + 1) * 128, :, c0:c0 + GP * 16], Yp[j][:], [Yp[j]], [], key=f"Yp{j}")
            b.dma(ysv[:, :, c0:c0 + GP * 16], Yps[:], [Yps], [], key="Yps")
            b.dma(g["o_ssm_s"].t[:, half * GP:(half + 1) * GP, :, :].rearrange("b g p r -> b (g p r)"), st_out[:].rearrange("b g p r -> b (g p r)"), [st_out], [], key="st_out")
        pO = nxt(pf, "pf")
        b.tr(pO[0:64, 0:128], hl[:], identf[:], [hl, identf], [pO])
        b.copy("dve", hlo[:, :, :].rearrange("g p r -> g r p"), pO[0:64, 0:128].rearrange("g (r p) -> g r p", r=2), [pO], [hlo])
        b.dma(g["o_ssm_p"].t[:, :, :], hlo[:], [hlo], [], key="hlo")
    P.barrier()


def load_cast_w(cx, b, st, src2d, kchunks, ncols, name, stg, eng_cycle=("pool", "dve", "act")):
    w = cx.sb(st, [128, kchunks, ncols], BF16, name)
    v = src2d.rearrange("(k p) c -> k p c", p=128)
    for k in range(kchunks):
        s = stg[k % len(stg)]
        b.dma(s[:, 0:ncols], v[k], [], [s], key=f"wstg{k % len(stg)}")
        b.copy(eng_cycle[k % len(eng_cycle)], w[:, k, :], s[:, 0:ncols], [s], [w])
    return w


def phase_P(nc, cx, b, P, g):
    ident = g["ident"]
    LT = L_CTX + L_OWN
    NQB = L_OWN // 128
    with contextlib.ExitStack() as st:
        stg = [cx.sb(st, [128, 1024], F32, "stgP") for _ in range(2)]
        wuk = load_cast_w(cx, b, st, g["mla_w_uk"].t, 2, 1024, "wukP", stg)
        wuv = load_cast_w(cx, b, st, g["mla_w_uv"].t, 2, 1024, "wuvP", stg)
        ckvT = cx.sb(st, [128, 2, LT], BF16, "ckvT")
        for k in range(2):
            b.dma(ckvT[:, k, :], g["ckvT_scr"].t[k], [], [ckvT], key="ckvT")
        KT = [cx.sb(st, [96, LT], BF16, "KT") for _ in range(2)]
        for i in range(2):
            b.dma(KT[i][64:96, :], g["krT_scr"].t[:, :], [], [KT[i]], key="KTr")
        Vh = [cx.sb(st, [128, LT // 128, 64], BF16, "Vh") for _ in range(2)]
        qT = [cx.sb(st, [96, L_OWN], BF16, "qTh") for _ in range(2)]
        S_sb = [cx.sb(st, [128, LT], F32, "S_sb") for _ in range(2)]
        P_bf = [cx.sb(st, [128, LT], BF16, "P_bf") for _ in range(2)]
        PT = [cx.sb(st, [128, LT // 128, 128], BF16, "PT") for _ in range(2)]
        sm = [cx.sb(st, [128, 8], F32, "smP") for _ in range(2)]
        o_t = [cx.sb(st, [128, 64], BF16, "o_t") for _ in range(4)]
        tril = cx.sb(st, [128, 128], F32, "tril")
        cbias = cx.sb(st, [128, 1], F32, "cbias")
        b.dma(cbias[:], g["ctx_bias"].t[:, :], [], [cbias], key="cbias")
        b.memset("pool", tril[:], 0.0, [tril])
        P.add("pool", lambda e: e.affine_select(out=tril[:], in_=tril[:], pattern=[[-1, 128]], compare_op=ALU.is_ge,
                                                fill=NEG, base=0, channel_multiplier=1), [tril.r], [tril.r])
        pf = [cx.ps(st, [128, 512], F32, "pfP") for _ in range(4)]
        pb = [cx.ps(st, [128, 1024], BF16, "pbP") for _ in range(2)]
        po = [cx.ps(st, [128, 512], F32, "poP") for _ in range(2)]
        cnt = {"pf": 0, "pb": 0, "po": 0, "ev": 0}

        def nxt(lst, k):
            cnt[k] += 1
            return lst[cnt[k] % len(lst)]

        def ev_eng():
            cnt["ev"] += 1
            return "act" if cnt["ev"] % 2 else "dve"

        MAXENG = "dve"
        work = []

        def head_prep(h):
            hb = h % 2
            b.dma(qT[hb][:, :], g["qT_scr"].t[h], [], [qT[hb]], key=f"qTh{hb}")
            for tg in range(LT // 512):
                pz = nxt(pf, "pf")
                for k in range(2):
                    b.mm(pz[0:64, :], wuk[:, k, h * 64:(h + 1) * 64], ckvT[:, k, tg * 512:(tg + 1) * 512], k == 0, k == 1, [wuk, ckvT], [pz])
                b.copy(ev_eng(), KT[hb][0:64, tg * 512:(tg + 1) * 512], pz[0:64, :], [pz], [KT[hb]])
            for vg in range(LT // 1024):
                pz = nxt(pf, "pf")
                for j in range(8):
                    kt = vg * 8 + j
                    for k in range(2):
                        b.mm(pz[:, j * 64:(j + 1) * 64], ckvT[:, k, kt * 128:(kt + 1) * 128], wuv[:, k, h * 64:(h + 1) * 64], k == 0, k == 1, [ckvT, wuv], [pz])
                b.copy(ev_eng(), Vh[hb][:, vg * 8:(vg + 1) * 8, :].rearrange("p j d -> p (j d)"), pz[:, :], [pz], [Vh[hb]])

        for h in range(NH):
            hb = h % 2
            for j in range(NQB):
                work.append((h, hb, j))
        def part1(it, h, hb, j):
            s_ = it % 2
            nkb = L_CTX // 128 + j + 1
            nk = nkb * 128
            S, sm_ = S_sb[s_], sm[s_]
            for kg in range((nk + 511) // 512):
                n = min(512, nk - kg * 512)
                pz = nxt(pf, "pf")
                b.mm(pz[:, 0:n], qT[hb][:, j * 128:(j + 1) * 128], KT[hb][:, kg * 512:kg * 512 + n], True, True, [qT[hb], KT[hb]], [pz])
                if kg < L_CTX // 512:
                    b.act(S[:, kg * 512:kg * 512 + n], pz[:, 0:n], AF.Identity, [pz, cbias], [S], bias=cbias[:, 0:1], scale=SCALE)
                else:
                    b.ts("dve", S[:, kg * 512:kg * 512 + n], pz[:, 0:n], SCALE, None, ALU.mult, None, [pz], [S])
            b.tt("dve", S[:, nk - 128:nk], S[:, nk - 128:nk], tril[:], ALU.add, [S, tril], [S])
            b.reduce(MAXENG, sm_[:, 0:1], S[:, 0:nk], ALU.max, [S], [sm_])
            b.ts(MAXENG, sm_[:, 1:2], sm_[:, 0:1], -1.0, None, ALU.mult, None, [sm_], [sm_])

        def part2(it, h, hb, j):
            s_ = it % 2
            nkb = L_CTX // 128 + j + 1
            nk = nkb * 128
            S, Pb, PT_, sm_ = S_sb[s_], P_bf[s_], PT[s_], sm[s_]
            b.act(Pb[:, 0:nk], S[:, 0:nk], AF.Exp, [S, sm_], [Pb, sm_], bias=sm_[:, 1:2], scale=1.0, accum=sm_[:, 2:3])
            P.add("dve", lambda e, sm_=sm_: e.reciprocal(out=sm_[:, 3:4], in_=sm_[:, 2:3]), [sm_.r], [sm_.r])
            for kb0 in range(0, nkb, 8):
                nb = min(8, nkb - kb0)
                pt = nxt(pb, "pb")
                for q in range(nb):
                    kb = kb0 + q
                    b.tr(pt[:, q * 128:(q + 1) * 128], Pb[:, kb * 128:(kb + 1) * 128], ident[:], [Pb, ident], [pt])
                b.copy(ev_eng(), PT_[:, kb0:kb0 + nb, :].rearrange("p k q -> p (k q)"), pt[:, 0:nb * 128], [pt], [PT_])
            pov = nxt(po, "po")
            for kb in range(nkb):
                b.mm(pov[:, 0:64], PT_[:, kb, :], Vh[hb][:, kb, :], kb == 0, kb == nkb - 1, [PT_, Vh[hb]], [pov])
            ot = o_t[it % 4]
            b.ts("dve", ot[:], pov[:, 0:64], sm_[:, 3:4], None, ALU.mult, None, [pov, sm_], [ot])
            b.dma(g["o_scr"].t[j * 128:(j + 1) * 128, h * 64:(h + 1) * 64], ot[:], [ot], [], key=f"ot{it % 4}")

        for i, (h, hb, j) in enumerate(work):
            if j == 0:
                head_prep(h)
            part1(i, h, hb, j)
            if i > 0:
                part2(i - 1, *work[i - 1])
        part2(len(work) - 1, *work[-1])
    P.barrier()


def phase_G(nc, cx, b, P, g):
    ident = g["ident"]
    NB = 16
    NSLOT = 8
    NSTEP = 128 // NSLOT
    with contextlib.ExitStack() as st:
        stg = [cx.sb(st, [128, 1024], F32, "stgG") for _ in range(2)]
        wuv = load_cast_w(cx, b, st, g["mla_w_uv"].t, 2, 1024, "wuvG", stg)
        qlT = cx.sb(st, [128, 2, NB, 128], BF16, "qlT")
        qrT = cx.sb(st, [32, NB, 128], BF16, "qrT")
        for k in range(2):
            b.dma(qlT[:, k], g["qlT_scr"].t[k], [], [qlT], key="qlT")
        b.dma(qrT[:], g["qrT_scr"].t[:, :, :], [], [qrT], key="qrT")
        ckvsT = cx.sb(st, [128, 2, 128], BF16, "ckvsT")
        krsT = cx.sb(st, [32, 128], BF16, "krsT")
        ckvs = cx.sb(st, [128, 256], BF16, "ckvs")
        for k in range(2):
            b.dma(ckvsT[:, k, :], g["ckvsT_scr"].t[k], [], [ckvsT], key="ckvsT")
        b.dma(krsT[:], g["krsT_scr"].t[:, :], [], [krsT], key="krsT")
        b.dma(ckvs[:], g["ckvs_scr"].t[:, :], [], [ckvs], key="ckvsG")
        ptab = cx.sb(st, [128, NB], I32, "ptab")
        P.add("sp", lambda e: e.dma_start(out=ptab[:], in_=g["pt_core"].t.rearrange("b n -> n b"), allow_slow_non_contiguous=True), [], [ptab.r], dma=True, key="ptab")
        G_ = [cx.sb(st, [128, NSLOT, 256], F32, "Gf") for _ in range(2)]
        Gr = [cx.sb(st, [128, NSLOT, 32], F32, "Grf") for _ in range(2)]
        Gb = [cx.sb(st, [128, NSLOT, 288], BF16, "Gb") for _ in range(2)]
        KTg = [cx.sb(st, [128, 2, NSLOT * 128], BF16, "KTg") for _ in range(2)]
        KrT = [cx.sb(st, [32, NSLOT * 128], BF16, "KrT") for _ in range(2)]
        Pb = [cx.sb(st, [128, NSLOT * 128], BF16, "PbG") for _ in range(2)]
        PTg = [cx.sb(st, [128, NSLOT, 128], BF16, "PTg") for _ in range(2)]
        Snew = cx.sb(st, [128, 128], F32, "Snew")
        msk = cx.sb(st, [128, NB, 128], F32, "mskG")
        b.dma(msk[:], g["smp_mask"].t.rearrange("b r k -> r b k"), [], [msk], key="msk")
        acc = cx.sb(st, [128, 256], F32, "acc")
        sm = [cx.sb(st, [128, 12], F32, "smG") for _ in range(2)]
        ol_bf = cx.sb(st, [128, 256], BF16, "ol_bf")
        olT = cx.sb(st, [128, 2, 128], BF16, "olT")
        oT_s = cx.sb(st, [64, NH, 128], BF16, "oT_s")
        o_smp = cx.sb(st, [128, 1024], BF16, "o_smp")
        pf = [cx.ps(st, [128, 512], F32, "pfG") for _ in range(1)]
        pb = [cx.ps(st, [128, 1024], BF16, "pbG") for _ in range(2)]
        po = [cx.ps(st, [128, 512], F32, "poG") for _ in range(1)]
        Pnew = cx.sb(st, [128, 128], BF16, "Pnew")
        PTnew = cx.sb(st, [128, 128], BF16, "PTnew")
        cnt = {"pf": 0, "pb": 0, "ev": 0}

        def nxt(lst, k):
            cnt[k] += 1
            return lst[cnt[k] % len(lst)]

        def ev_eng():
            cnt["ev"] += 1
            return "act" if cnt["ev"] % 2 else "dve"

        ckv_rows = g["cache_ckv"].t.rearrange("n (s c) -> (n s) c", c=NSLOT * 256)
        kr_rows = g["cache_krope"].t.rearrange("n (s c) -> (n s) c", c=NSLOT * 32)
        ptf = cx.sb(st, [128, NB], F32, "ptf")
        stpf = cx.sb(st, [128, NSTEP], F32, "stpf")
        idx_f = cx.sb(st, [128, NB, NSTEP], F32, "idx_f")
        idx_all = cx.sb(st, [128, NB, NSTEP], I32, "idx_all")
        b.copy("dve", ptf[:], ptab[:], [ptab], [ptf])
        b.ts("dve", ptf[:], ptf[:], float(NSTEP), None, ALU.mult, None, [ptf], [ptf])
        for i in range(NSTEP):
            b.memset("pool", stpf[:, i:i + 1], float(i), [stpf])
        b.tt("dve", idx_f[:], ptf[:].unsqueeze(2).broadcast_to([128, NB, NSTEP]), stpf[:].unsqueeze(1).broadcast_to([128, NB, NSTEP]), ALU.add, [ptf, stpf], [idx_f])
        b.copy("dve", idx_all[:], idx_f[:], [idx_f], [idx_all])
        pS2 = [cx.ps(st, [128, 1024], F32, "pS2") for _ in range(2)]
        smx = [cx.sb(st, [128, 2], F32, "smx") for _ in range(2)]

        def init_sample(bi):
            sm_ = sm[bi % 2]
            pz = pf[0]
            for k in range(2):
                b.mm(pz[:, 0:128], qlT[:, k, bi, :], ckvsT[:, k, :], k == 0, False, [qlT, ckvsT], [pz])
            b.mm(pz[:, 0:128], qrT[:, bi, :], krsT[:, :], False, True, [qrT, krsT], [pz])
            b.tt("dve", Snew[:], pz[:, 0:128], msk[:, bi, :], ALU.add, [pz, msk], [Snew])
            b.reduce("dve", sm_[:, 0:1], Snew[:], ALU.max, [Snew], [sm_])
            b.ts("dve", sm_[:, 1:2], sm_[:, 0:1], -SCALE, None, ALU.mult, None, [sm_], [sm_])
            b.act(Pnew[:], Snew[:], AF.Exp, [Snew, sm_], [Pnew, sm_], bias=sm_[:, 1:2], scale=SCALE, accum=sm_[:, 2:3])
            pt = nxt(pb, "pb")
            b.tr(pt[:, 0:128], Pnew[:], ident[:], [Pnew, ident], [pt])
            b.copy("dve", PTnew[:], pt[:, 0:128], [pt], [PTnew])
            pov = po[0]
            b.mm(pov[:, 0:256], PTnew[:], ckvs[:, :], True, True, [PTnew, ckvs], [pov])
            b.copy("dve", acc[:], pov[:, 0:256], [pov], [acc])

        def stage_x(i, bi, stp):
            q_ = i % 2
            Gf, Grf, Gb_, KT_, KrT_ = G_[q_], Gr[q_], Gb[q_], KTg[q_], KrT[q_]
            P.add("pool", lambda e, Gf=Gf, stp=stp, bi=bi: e.indirect_dma_start(
                out=Gf[:].rearrange("p s c -> p (s c)"), out_offset=None, in_=ckv_rows,
                in_offset=bass.IndirectOffsetOnAxis(ap=idx_all[:, bi, stp:stp + 1], axis=0)), [idx_all.r], [Gf.r], dma=True, key=f"Gf{q_}")
            P.add("pool", lambda e, Grf=Grf, stp=stp, bi=bi: e.indirect_dma_start(
                out=Grf[:].rearrange("p s c -> p (s c)"), out_offset=None, in_=kr_rows,
                in_offset=bass.IndirectOffsetOnAxis(ap=idx_all[:, bi, stp:stp + 1], axis=0)), [idx_all.r], [Grf.r], dma=True, key=f"Grf{q_}")
            b.copy("pool", Gb_[:, :, 0:256], Gf[:], [Gf], [Gb_])
            b.copy("pool", Gb_[:, :, 256:288], Grf[:], [Grf], [Gb_])
            for k in range(2):
                pt = nxt(pb, "pb")
                for s in range(NSLOT):
                    b.tr(pt[:, s * 128:(s + 1) * 128], Gb_[:, s, k * 128:(k + 1) * 128], ident[:], [Gb_, ident], [pt])
                b.copy("act" if k else "dve", KT_[:, k, :], pt[:, :], [pt], [KT_])
            pt = nxt(pb, "pb")
            for s in range(NSLOT):
                b.tr(pt[0:32, s * 128:(s + 1) * 128], Gb_[:, s, 256:288], ident[:], [Gb_, ident], [pt])
            b.copy("dve", KrT_[:, :], pt[0:32, :], [pt], [KrT_])
            pz = pS2[q_]
            for hf in range(2):
                sl = slice(hf * 512, (hf + 1) * 512)
                for k in range(2):
                    b.mm(pz[:, sl], qlT[:, k, bi, :], KT_[:, k, sl], k == 0, False, [qlT, KT_], [pz])
                b.mm(pz[:, sl], qrT[:, bi, :], KrT_[:, sl], False, True, [qrT, KrT_], [pz])
            b.reduce("dve", smx[q_][:, 0:1], pz[:, :], ALU.max, [pz], [smx[q_]])

        def stage_y(i, bi, stp):
            q_ = i % 2
            sm_ = sm[bi % 2]
            Gb_, Pb_, PT_, pz = Gb[q_], Pb[q_], PTg[q_], pS2[q_]
            b.tt("dve", sm_[:, 6:7], smx[q_][:, 0:1], sm_[:, 0:1], ALU.max, [sm_, smx[q_]], [sm_])
            b.ts("dve", sm_[:, 7:8], sm_[:, 6:7], -SCALE, None, ALU.mult, None, [sm_], [sm_])
            b.act(sm_[:, 8:9], sm_[:, 0:1], AF.Exp, [sm_], [sm_], bias=sm_[:, 7:8], scale=SCALE)
            b.act(Pb_[:, :], pz[:, :], AF.Exp, [pz, sm_], [Pb_, sm_], bias=sm_[:, 7:8], scale=SCALE, accum=sm_[:, 9:10])
            b.stt("dve", sm_[:, 2:3], sm_[:, 2:3], sm_[:, 8:9], sm_[:, 9:10], ALU.mult, ALU.add, [sm_], [sm_])
            b.copy("dve", sm_[:, 0:1], sm_[:, 6:7], [sm_], [sm_])
            pt = nxt(pb, "pb")
            for s in range(NSLOT):
                b.tr(pt[:, s * 128:(s + 1) * 128], Pb_[:, s * 128:(s + 1) * 128], ident[:], [Pb_, ident], [pt])
            b.copy("act", PT_[:].rearrange("p s q -> p (s q)"), pt[:, :], [pt], [PT_])
            pov = po[0]
            for s in range(NSLOT):
                b.mm(pov[:, 0:256], PT_[:, s, :], Gb_[:, s, 0:256], s == 0, s == NSLOT - 1, [PT_, Gb_], [pov])
            b.stt("dve", acc[:], acc[:], sm_[:, 8:9], pov[:, 0:256], ALU.mult, ALU.add, [acc, sm_, pov], [acc])

        def finalize(bi):
            sm_ = sm[bi % 2]
            if g.get("debug"):
                b.dma(g["dbg_sm"].t[bi], sm_[:], [sm_], [], key="dbgsm")
                b.dma(g["dbg_acc"].t[bi], acc[:], [acc], [], key="dbgacc")
            P.add("dve", lambda e, sm_=sm_: e.reciprocal(out=sm_[:, 3:4], in_=sm_[:, 2:3]), [sm_.r], [sm_.r])
            b.ts("dve", ol_bf[:], acc[:], sm_[:, 3:4], None, ALU.mult, None, [acc, sm_], [ol_bf])
            pt = nxt(pb, "pb")
            for k in range(2):
                b.tr(pt[:, k * 128:(k + 1) * 128], ol_bf[:, k * 128:(k + 1) * 128], ident[:], [ol_bf, ident], [pt])
            b.copy("dve", olT[:].rearrange("p k r -> p (k r)"), pt[:, 0:256], [pt], [olT])
            pz = pf[0]
            for h in range(NH):
                for k in range(2):
                    b.mm(pz[0:64, h * 8:(h + 1) * 8], wuv[:, k, h * 64:(h + 1) * 64], olT[:, k, h * 8:(h + 1) * 8], k == 0, k == 1, [wuv, olT], [pz])
            b.copy("dve", oT_s[:, :, bi * 8:(bi + 1) * 8], pz[0:64, 0:128].rearrange("p (h t) -> p h t", h=NH), [pz], [oT_s])

        items = [(bi, stp) for bi in range(NB) for stp in range(NSTEP)]

        def emit_y(i):
            bi, stp = items[i]
            if stp == 0:
                init_sample(bi)
            stage_y(i, bi, stp)
            if stp == NSTEP - 1:
                finalize(bi)

        for i, (bi, stp) in enumerate(items):
            stage_x(i, bi, stp)
            if i > 0:
                emit_y(i - 1)
        emit_y(len(items) - 1)
        for hg in range(2):
            pt = nxt(pb, "pb")
            for hh in range(8):
                b.tr(pt[:, hh * 64:(hh + 1) * 64], oT_s[:, hg * 8 + hh, :], ident[0:64, 0:64], [oT_s, ident], [pt])
            b.copy("dve", o_smp[:, hg * 512:(hg + 1) * 512], pt[:, 0:512], [pt], [o_smp])
        b.dma(g["o_scr"].t[L_OWN:L_OWN + 128, :], o_smp[:], [o_smp], [], key="o_smp")
    P.barrier()


def phase_B(nc, cx, b, P, g):
    ident = g["ident"]
    with contextlib.ExitStack() as st:
        stg = [cx.sb(st, [128, 1024], F32, "stgB") for _ in range(2)]
        wglu = load_cast_w(cx, b, st, g["ssm_w_glu"].t, 8, 1024, "wglu", stg)
        wbs = load_cast_w(cx, b, st, g["w_br_ssm"].t, 8, 1024, "wbs", stg)
        wba = load_cast_w(cx, b, st, g["w_br_attn"].t, 8, 1024, "wba", stg)
        wo = load_cast_w(cx, b, st, g["w_out"].t, 8, 1024, "wo", stg)
        bglu = cx.sb(st, [128, D], F32, "bglu")
        gfin = cx.sb(st, [128, D], F32, "gfin")
        b.dma(bglu[:], bcast_rows(g["ssm_b_glu"].t, D), [], [bglu], key="bglu")
        b.dma(gfin[:], bcast_rows(g["norm_final"].t, D), [], [gfin], key="gfin")
        NB = 2
        ys = [cx.sb(st, [128, D], BF16, "ysB") for _ in range(NB)]
        gt = [cx.sb(st, [128, 4096], BF16, "gtB") for _ in range(NB)]
        ot = [cx.sb(st, [128, D], BF16, "otB") for _ in range(NB)]
        xt = [cx.sb(st, [128, D], F32, "xtB") for _ in range(NB)]
        t1 = cx.sb(st, [128, D], F32, "t1B")
        zg = cx.sb(st, [128, D], F32, "zgB")
        ab = cx.sb(st, [128, D], BF16, "abB")
        aT = [cx.sb(st, [128, 8, 128], BF16, "aTB") for _ in range(2)]
        mg = cx.sb(st, [128, D], F32, "mgB")
        hh = cx.sb(st, [128, D], F32, "hhB")
        yo = [cx.sb(st, [128, D], F32, "yoB") for _ in range(NB)]
        stt_ = [cx.sb(st, [128, 4], F32, "stB") for _ in range(NB)]
        junk = cx.sb(st, [128, D], BF16, "junkB")
        pz = [cx.ps(st, [128, 512], F32, "pzB") for _ in range(4)]
        pT = [cx.ps(st, [128, 1024], BF16, "pTB") for _ in range(2)]
        cnt = {"pz": 0, "pT": 0, "aT": 0}

        def nxt(lst, k):
            cnt[k] += 1
            return lst[cnt[k] % len(lst)]

        def transp(src):
            pt = nxt(pT, "pT")
            for k in range(8):
                b.tr(pt[:, k * 128:(k + 1) * 128], src[:, k * 128:(k + 1) * 128], ident[:], [src, ident], [pt])
            a = nxt(aT, "aT")
            b.copy("act", a[:].rearrange("p k t -> p (k t)"), pt[:], [pt], [a])
            return a

        def linear(a, w, cg):
            p_ = nxt(pz, "pz")
            for k in range(8):
                b.mm(p_[:, :], a[:, k, :], w[:, k, cg * 512:(cg + 1) * 512], k == 0, k == 7, [a, w], [p_])
            return p_

        tiles = [("own", i) for i in range(L_OWN // 128)] + [("smp", 0)]
        for ti, (kind, i) in enumerate(tiles):
            s_ = ti % NB
            r0 = i * 128 if kind == "own" else L_OWN
            X = g["x_own"].t[i * 128:(i + 1) * 128, :] if kind == "own" else g["x_smp"].t[:, :]
            OUT = g["o_y_p"].t[i * 128:(i + 1) * 128, :] if kind == "own" else g["o_y_s"].t[:, :]
            ys_, gt_, ot_, xt_, st_ = ys[s_], gt[s_], ot[s_], xt[s_], stt_[s_]
            b.dma(ys_[:], g["ys_scr"].t[r0:r0 + 128, :], [], [ys_], key=f"ysB{s_}")
            b.dma(gt_[:], g["g_scr"].t[r0:r0 + 128, :], [], [gt_], key=f"gtB{s_}")
            b.dma(ot_[:], g["o_scr"].t[r0:r0 + 128, :], [], [ot_], key=f"otB{s_}")
            b.dma(xt_[:], X, [], [xt_], key=f"xtB{s_}")
            b.tt("pool", t1[:], ys_[:], ys_[:], ALU.mult, [ys_], [t1])
            b.ts("pool", t1[:], t1[:], 0.044715, 1.0, ALU.mult, ALU.add, [t1], [t1])
            b.tt("pool", t1[:], t1[:], ys_[:], ALU.mult, [t1, ys_], [t1])
            b.act(t1[:], t1[:], AF.Sigmoid, [t1], [t1], scale=1.5957691216057308)
            b.tt("dve", zg[:], t1[:], ys_[:], ALU.mult, [t1, ys_], [zg])
            b.copy("pool", ab[:], zg[:], [zg], [ab])
            a = transp(ab)
            for cg in range(2):
                p_ = linear(a, wglu, cg)
                sl = slice(cg * 512, (cg + 1) * 512)
                b.tt("dve", t1[:, sl], p_[:, :], bglu[:, sl], ALU.add, [p_, bglu], [t1])
                b.act(t1[:, sl], t1[:, sl], AF.Sigmoid, [t1], [t1])
                b.tt("dve", t1[:, sl], t1[:, sl], zg[:, sl], ALU.mult, [t1, zg], [t1])
            b.tt("dve", ab[:], t1[:], gt_[:, 0:1024], ALU.mult, [t1, gt_], [ab])
            a = transp(ab)
            for cg in range(2):
                p_ = linear(a, wbs, cg)
                sl = slice(cg * 512, (cg + 1) * 512)
                b.tt("dve", mg[:, sl], p_[:, :], gt_[:, 2048 + cg * 512:2048 + (cg + 1) * 512], ALU.mult, [p_, gt_], [mg])
            b.tt("pool", ab[:], ot_[:], gt_[:, 1024:2048], ALU.mult, [ot_, gt_], [ab])
            a = transp(ab)
            for cg in range(2):
                p_ = linear(a, wba, cg)
                sl = slice(cg * 512, (cg + 1) * 512)
                b.tt("dve", t1[:, sl], p_[:, :], gt_[:, 3072 + cg * 512:3072 + (cg + 1) * 512], ALU.mult, [p_, gt_], [t1])
            b.tt("pool", mg[:], mg[:], t1[:], ALU.add, [mg, t1], [mg])
            b.copy("pool", ab[:], mg[:], [mg], [ab])
            a = transp(ab)
            for cg in range(2):
                p_ = linear(a, wo, cg)
                sl = slice(cg * 512, (cg + 1) * 512)
                b.tt("dve", hh[:, sl], p_[:, :], xt_[:, sl], ALU.add, [p_, xt_], [hh])
            b.act(junk[:], hh[:], AF.Square, [hh], [st_], accum=st_[:, 0:1])
            b.rstd(st_, 0, 1, 2, 1.0 / D)
            b.stt("dve", yo[s_][:], hh[:], st_[:, 2:3], gfin[:], ALU.mult, ALU.mult, [hh, st_, gfin], [yo[s_]])
            b.dma(OUT, yo[s_][:], [yo[s_]], [], key=f"yoB{s_}")
```

```python
import contextlib
import math
import numpy as np
import concourse.bass as bass
import concourse.mybir as mybir
from concourse.bass_utils import run_bass_kernel_spmd

F32 = mybir.dt.float32
BF16 = mybir.dt.bfloat16
I32 = mybir.dt.int32
AF = mybir.ActivationFunctionType
ALU = mybir.AluOpType
AX = mybir.AxisListType

D = 1024
NCOL = 5792
C_US, C_GS, C_CQ, C_KV, C_KR, C_GA, C_MS, C_MA = 0, 1024, 2048, 2432, 2688, 2720, 3744, 4768
NH = 16
SCALE = 96 ** -0.5
EPS = 1e-6
NEG = -1e30
L_OWN = 2048
L_CTX = 2048
PAST = 16384
NPAGE = 128
STAGE = "all"


class Reg:
    __slots__ = ("name", "w", "r")

    def __init__(self, name=""):
        self.name = name
        self.w = None
        self.r = []


class Op:
    __slots__ = ("eng", "fn", "deps", "idx", "dma", "key", "sig", "cnt", "waits")

    def __init__(self, eng, fn, dma, key):
        self.eng, self.fn, self.dma, self.key = eng, fn, dma, key
        self.deps = []
        self.sig = False
        self.cnt = 0
        self.waits = []


class Prog:
    ENGS = ("pe", "act", "dve", "pool", "sp")

    def __init__(self, nc):
        self.nc = nc
        self.ops = {e: [] for e in self.ENGS}
        self.dma_keys = {}

    def add(self, eng, fn, reads=(), writes=(), dma=False, key=None):
        op = Op(eng, fn, dma, key)
        if dma:
            lst = self.dma_keys.setdefault(key, [])
            lst.append(op)
            op.cnt = 16 * len(lst)
            op.sig = True
        for r in reads:
            if r.w is not None:
                op.deps.append((r.w, "raw"))
        for w in writes:
            if w.w is not None:
                op.deps.append((w.w, "waw"))
            for rd in w.r:
                op.deps.append((rd, "war"))
        for r in reads:
            r.r.append(op)
        for w in writes:
            w.w = op
            w.r = []
        op.idx = len(self.ops[eng])
        self.ops[eng].append(op)
        return op

    def barrier(self):
        lasts = []
        for e in self.ENGS:
            comp = [o for o in self.ops[e] if not o.dma]
            if comp:
                lasts.append(comp[-1])
        for key, lst in self.dma_keys.items():
            if lst:
                lasts.append(lst[-1])
        for e in self.ENGS:
            op = Op(e, (lambda eh: eh.nop()), False, None)
            op.deps = [(d, "raw") for d in lasts]
            op.idx = len(self.ops[e])
            self.ops[e].append(op)

    def emit(self):
        nc = self.nc
        for e in self.ENGS:
            for op in self.ops[e]:
                need = {}
                for (d, kind) in op.deps:
                    if d is op:
                        continue
                    if (not d.dma) and (not op.dma) and d.eng == op.eng and kind != "raw":
                        continue
                    k = ("dma", d.key) if d.dma else ("eng", d.eng)
                    if k not in need or (d.dma and need[k].cnt < d.cnt) or ((not d.dma) and need[k].idx < d.idx):
                        need[k] = d
                op.deps = need
        for e in self.ENGS:
            for op in self.ops[e]:
                for k, d in op.deps.items():
                    if not d.dma:
                        d.sig = True
        for e in self.ENGS:
            c = 0
            for op in self.ops[e]:
                if not op.dma and op.sig:
                    c += 1
                    op.cnt = c
        sem_eng, sem_key = {}, {}
        stack = contextlib.ExitStack()
        for e in self.ENGS:
            sem_eng[e] = stack.enter_context(nc.semaphore("se_" + e))
        for key in self.dma_keys:
            sem_key[key] = stack.enter_context(nc.semaphore("sd_" + str(key)))
        for e in self.ENGS:
            seen = {}
            for op in self.ops[e]:
                for k, d in op.deps.items():
                    sem = sem_key[d.key] if d.dma else sem_eng[d.eng]
                    if seen.get(k, 0) >= d.cnt:
                        continue
                    seen[k] = d.cnt
                    op.waits.append((sem, d.cnt))
        nops = sum(len(self.ops[e]) for e in self.ENGS)
        nw = sum(len(op.waits) for e in self.ENGS for op in self.ops[e])
        print(f"[prog] ops={nops} waits={nw} dma_keys={len(self.dma_keys)}", flush=True)

        def run(e_name, eh):
            for op in self.ops[e_name]:
                for (sem, v) in op.waits:
                    eh.wait_ge(sem, v)
                ins = op.fn(eh)
                if op.sig:
                    if op.dma:
                        ins.then_inc(sem_key[op.key], 16)
                    else:
                        ins.then_inc(sem_eng[e_name], 1)
            if e_name == "sp":
                for key, lst in self.dma_keys.items():
                    if lst:
                        eh.wait_ge(sem_key[key], 16 * len(lst))

        with nc.Block() as block:
            @block.tensor
            def _(t):
                run("pe", t)

            @block.scalar
            def _(s):
                run("act", s)

            @block.vector
            def _(v):
                run("dve", v)

            @block.gpsimd
            def _(g):
                run("pool", g)

            @block.sync
            def _(sy):
                run("sp", sy)
        stack.close()


class T:
    __slots__ = ("t", "r")

    def __init__(self, t, name):
        self.t = t
        self.r = Reg(name)

    def __getitem__(self, k):
        return self.t[k]


class Ctx:
    def __init__(self, nc, P):
        self.nc, self.P = nc, P
        self.n = 0

    def sb(self, st, shape, dt, name=None):
        self.n += 1
        name = f"{name or 't'}_{self.n}"
        return T(st.enter_context(self.nc.sbuf_tensor(name, shape, dt)), name)

    def ps(self, st, shape, dt, name=None):
        self.n += 1
        name = f"{name or 'p'}_{self.n}"
        return T(st.enter_context(self.nc.psum_tensor(name, shape, dt)), name)

    def dram(self, name, shape, dt, kind="Internal"):
        return T(self.nc.dram_tensor(name, shape, dt, kind=kind).ap(), name)


def _regs(xs):
    return [x.r if isinstance(x, T) else x for x in xs]


class B:
    def __init__(self, cx):
        self.cx, self.P = cx, cx.P
        self.dq = 0

    def dma(self, out, in_, reads, writes, key, eng="sp"):
        self.P.add(eng, lambda e: e.dma_start(out=out, in_=in_), _regs(reads), _regs(writes), dma=True, key=key)

    def mm(self, out, lhsT, rhs, start, stop, reads, writes):
        self.P.add("pe", lambda e: e.matmul(out=out, lhsT=lhsT, rhs=rhs, start=start, stop=stop), _regs(reads), _regs(writes))

    def tr(self, out, in_, ident, reads, writes):
        self.P.add("pe", lambda e: e.transpose(out=out, in_=in_, identity=ident), _regs(reads), _regs(writes))

    def act(self, out, in_, func, reads, writes, bias=None, scale=None, accum=None, eng="act"):
        kw = {}
        if bias is not None:
            kw["bias"] = bias
        if scale is not None:
            kw["scale"] = scale
        if accum is not None:
            kw["accum_out"] = accum
        self.P.add("act", lambda e: e.activation(out=out, in_=in_, func=func, **kw), _regs(reads), _regs(writes))

    def copy(self, eng, out, in_, reads, writes):
        if eng == "act":
            self.P.add("act", lambda e: e.activation(out=out, in_=in_, func=AF.Copy), _regs(reads), _regs(writes))
        else:
            self.P.add(eng, lambda e: e.tensor_copy(out=out, in_=in_), _regs(reads), _regs(writes))

    def tt(self, eng, out, in0, in1, op, reads, writes):
        self.P.add(eng, lambda e: e.tensor_tensor(out=out, in0=in0, in1=in1, op=op), _regs(reads), _regs(writes))

    def ts(self, eng, out, in0, s1, s2, op0, op1, reads, writes):
        if op1 is None:
            self.P.add(eng, lambda e: e.tensor_scalar(out=out, in0=in0, scalar1=s1, scalar2=None, op0=op0), _regs(reads), _regs(writes))
        else:
            self.P.add(eng, lambda e: e.tensor_scalar(out=out, in0=in0, scalar1=s1, scalar2=s2, op0=op0, op1=op1), _regs(reads), _regs(writes))

    def stt(self, eng, out, in0, scalar, in1, op0, op1, reads, writes):
        self.P.add(eng, lambda e: e.scalar_tensor_tensor(out=out, in0=in0, scalar=scalar, in1=in1, op0=op0, op1=op1), _regs(reads), _regs(writes))

    def rstd(self, st_, c_in, c_tmp, c_out, inv_n):
        self.ts("dve", st_[:, c_tmp:c_tmp + 1], st_[:, c_in:c_in + 1], inv_n, EPS, ALU.mult, ALU.add, [st_], [st_])
        self.act(st_[:, c_tmp:c_tmp + 1], st_[:, c_tmp:c_tmp + 1], AF.Sqrt, [st_], [st_])
        self.P.add("dve", lambda e: e.reciprocal(out=st_[:, c_out:c_out + 1], in_=st_[:, c_tmp:c_tmp + 1]), [st_.r], [st_.r])

    def memset(self, eng, ap, val, writes):
        self.P.add(eng, lambda e: e.memset(ap, val), [], _regs(writes))

    def reduce(self, eng, out, in_, op, reads, writes):
        self.P.add(eng, lambda e: e.tensor_reduce(out=out, in_=in_, axis=AX.X, op=op), _regs(reads), _regs(writes))


def bcast_rows(ap1d, n, p=128):
    return ap1d.rearrange("(o n) -> o n", o=1).broadcast_to([p, n])


def build(debug=False, phases=("A",)):
    nc = bass.Bass("TRN2", target_bir_lowering=False)
    P = Prog(nc)
    cx = Ctx(nc, P)
    b = B(cx)
    kio = "ExternalOutput" if debug else "Internal"

    def din(name, shape, dt=F32):
        return T(nc.dram_tensor(name, shape, dt, kind="ExternalInput").ap(), name)

    def dout(name, shape, dt=F32):
        return T(nc.dram_tensor(name, shape, dt, kind="ExternalOutput").ap(), name)

    x_ctx = din("x_ctx", [L_CTX, D])
    x_own = din("x_own", [L_OWN, D])
    x_smp = din("x_smp", [128, D])
    cs_ctx = din("cs_ctx", [L_CTX, 32])
    cs_own = din("cs_own", [L_OWN, 32])
    cs_smp = din("cs_smp", [128, 32])
    norm_in = din("norm_in", [D])
    w_in = din("w_in", [D, NCOL])
    mla_q_norm = din("mla_q_norm", [384])
    mla_kv_norm = din("mla_kv_norm", [256])
    mla_w_uq = din("mla_w_uq", [384, 1536])
    mla_w_uk = din("mla_w_uk", [256, 1024])
    ssm_lambda_re = din("ssm_lambda_re", [64, 64])
    ssm_lambda_im = din("ssm_lambda_im", [64, 64])
    ssm_log_dt = din("ssm_log_dt", [64])
    ssm_b_re = din("ssm_b_re", [64, 64, 16])
    ssm_b_im = din("ssm_b_im", [64, 64, 16])
    ssm_c_re = din("ssm_c_re", [64, 16, 64])
    ssm_c_im = din("ssm_c_im", [64, 16, 64])
    ssm_d = din("ssm_d", [D])
    state_s = din("state_s", [16, 64, 64, 2])
    mla_w_uv = din("mla_w_uv", [256, 1024])
    ctx_bias = din("ctx_bias", [128, 1])
    if "G" in phases:
        smp_mask = din("smp_mask", [16, 128, 128])
        pt_core = din("pt_core", [16, 128], I32)
        cache_ckv = din("cache_ckv", [20480, 128 * 256])
        cache_krope = din("cache_krope", [20480, 128 * 32])
    if "B" in phases:
        ssm_w_glu = din("ssm_w_glu", [D, D])
        ssm_b_glu = din("ssm_b_glu", [D])
        w_br_ssm = din("w_br_ssm", [D, D])
        w_br_attn = din("w_br_attn", [D, D])
        w_out = din("w_out", [D, D])
        norm_final = din("norm_final", [D])
        o_y_p = dout("o_y_p", [L_OWN, D])
        o_y_s = dout("o_y_s", [128, D])
    o_ssm_p = dout("o_ssm_p", [64, 64, 2])
    o_ssm_s = dout("o_ssm_s", [16, 64, 64, 2])
    o_ckv_p = dout("o_ckv_p", [L_OWN, 256])
    o_kr_p = dout("o_kr_p", [L_OWN, 32])
    o_ckv_s = dout("o_ckv_s", [128, 256])
    o_kr_s = dout("o_kr_s", [128, 32])
    NTOK = L_CTX + L_OWN + 128
    u_scr = cx.dram("u_scr", [NTOK, D], BF16, kio)
    g_scr = cx.dram("g_scr", [L_OWN + 128, 4096], BF16, kio)
    o_scr = cx.dram("o_scr", [L_OWN + 128, D], BF16, kio)
    ys_scr = cx.dram("ys_scr", [L_OWN + 128, D], BF16, kio)
    qT_scr = cx.dram("qT_scr", [NH, 96, L_OWN], BF16, kio)
    ckvT_scr = cx.dram("ckvT_scr", [2, 128, L_CTX + L_OWN], BF16, kio)
    krT_scr = cx.dram("krT_scr", [32, L_CTX + L_OWN], BF16, kio)
    qlT_scr = cx.dram("qlT_scr", [2, 128, 16, 128], BF16, kio)
    qrT_scr = cx.dram("qrT_scr", [32, 16, 128], BF16, kio)
    ckvs_scr = cx.dram("ckvs_scr", [128, 256], BF16, kio)
    ckvsT_scr = cx.dram("ckvsT_scr", [2, 128, 128], BF16, kio)
    krsT_scr = cx.dram("krsT_scr", [32, 128], BF16, kio)

    if debug:
        dbg_sm = dout("dbg_sm", [16, 128, 12])
        dbg_acc = dout("dbg_acc", [16, 128, 256])
        dbg_L = dout("dbg_L", [128, 2304])
        dbg_T = dout("dbg_T", [128, 8192], BF16)
        dbg_R = dout("dbg_R", [128, 8192], BF16)
        dbg_O = dout("dbg_O", [128, 8192], BF16)
    with contextlib.ExitStack() as top:
        ident = cx.sb(top, [128, 128], BF16, "ident")
        identf = cx.sb(top, [128, 128], F32, "identf")
        b.memset("pool", identf[:], 0.0, [identf])
        P.add("pool", lambda e: e.affine_select(out=identf[:], in_=identf[:], pattern=[[-1, 128]], compare_op=ALU.not_equal,
                                                fill=1.0, base=0, channel_multiplier=1), [identf.r], [identf.r])
        b.copy("dve", ident[:], identf[:], [identf], [ident])

        if "A" in phases:
            phase_A(nc, cx, b, P, locals())
        if "S" in phases:
            phase_S(nc, cx, b, P, locals())
        if "P" in phases:
            phase_P(nc, cx, b, P, locals())
        if "G" in phases:
            phase_G(nc, cx, b, P, locals())
        if "B" in phases:
            phase_B(nc, cx, b, P, locals())
    P.emit()
    return nc


def phase_A(nc, cx, b, P, g):
    ident = g["ident"]
    x_ctx, x_own, x_smp = g["x_ctx"], g["x_own"], g["x_smp"]
    cs_ctx, cs_own, cs_smp = g["cs_ctx"], g["cs_own"], g["cs_smp"]
    w_in, norm_in = g["w_in"], g["norm_in"]
    with contextlib.ExitStack() as st:
        w_sb = cx.sb(st, [128, 8, NCOL], BF16, "w_in_sb")
        wuq_sb = cx.sb(st, [128, 3, 1536], BF16, "wuq")
        wukT_sb = cx.sb(st, [64, NH, 256], BF16, "wukT")
        gin = cx.sb(st, [128, D], F32, "gin")
        gq = cx.sb(st, [128, 384], F32, "gq")
        gkv = cx.sb(st, [128, 256], F32, "gkv")
        pz = [cx.ps(st, [128, 512], F32, "pz") for _ in range(4)]
        pT = [cx.ps(st, [128, 1024], BF16, "pT") for _ in range(2)]
        pq = [cx.ps(st, [128, 512], F32, "pq") for _ in range(2)]
        cnt = {"pz": 0, "pT": 0, "pq": 0, "cast": 0}

        def nxt(lst, k):
            cnt[k] += 1
            return lst[cnt[k] % len(lst)]

        cast_engs = ["pool", "dve", "act"]

        def cast(out, in_, reads, writes):
            cnt["cast"] += 1
            b.copy(cast_engs[cnt["cast"] % 3], out, in_, reads, writes)

        st0 = contextlib.ExitStack()
        stg = [cx.sb(st0, [128, 2896], F32, "stg") for _ in range(2)]
        w_v = w_in.t.rearrange("(k p) c -> k p c", p=128)
        for k in range(8):
            for h in range(2):
                s = stg[h]
                b.dma(s[:, :], w_v[k][:, h * 2896:(h + 1) * 2896], [], [s], key=f"stg{h}")
                cast(w_sb[:, k, h * 2896:(h + 1) * 2896], s[:, :], [s], [w_sb])
        wq_v = g["mla_w_uq"].t.rearrange("(k p) c -> k p c", p=128)
        for k in range(3):
            s = stg[k % 2]
            b.dma(s[:, 0:1536], wq_v[k], [], [s], key=f"stg{k % 2}")
            cast(wuq_sb[:, k, :], s[:, 0:1536], [s], [wuq_sb])
        wk_v = g["mla_w_uk"].t.rearrange("(k p) c -> k p c", p=128)
        wk_bf = cx.sb(st0, [128, 2, 1024], BF16, "wk_bf")
        for k in range(2):
            s = stg[k % 2]
            b.dma(s[:, 0:1024], wk_v[k], [], [s], key=f"stg{k % 2}")
            cast(wk_bf[:, k, :], s[:, 0:1024], [s], [wk_bf])
        for k in range(2):
            for hg in range(2):
                pt = nxt(pT, "pT")
                for hh in range(8):
                    h = hg * 8 + hh
                    b.tr(pt[0:64, hh * 128:(hh + 1) * 128], wk_bf[:, k, h * 64:(h + 1) * 64], ident[:], [wk_bf, ident], [pt])
                b.copy("dve", wukT_sb[:, hg * 8:(hg + 1) * 8, k * 128:(k + 1) * 128],
                       pt[0:64, :].rearrange("p (h c) -> p h c", h=8), [pt], [wukT_sb])
        st0.close()
        P.barrier()
        b.dma(gin[:], bcast_rows(norm_in.t, D), [], [gin], key="gin")
        b.dma(gq[:], bcast_rows(g["mla_q_norm"].t, 384), [], [gq], key="gq")
        b.dma(gkv[:], bcast_rows(g["mla_kv_norm"].t, 256), [], [gkv], key="gkv")

        NB = 2
        xt = [cx.sb(st, [128, D], F32, "xt") for _ in range(NB)]
        cs = [cx.sb(st, [128, 32], F32, "cs") for _ in range(NB)]
        junk1 = cx.sb(st, [128, D], BF16, "junk")
        junk = [junk1, junk1]
        stat = [cx.sb(st, [128, 8], F32, "stat") for _ in range(NB)]
        xn = [cx.sb(st, [128, D], BF16, "xn") for _ in range(NB)]
        xnT = [cx.sb(st, [128, 8, 128], BF16, "xnT") for _ in range(NB)]
        u_t = [cx.sb(st, [128, D], BF16, "u_t") for _ in range(NB)]
        g_t = [cx.sb(st, [128, 2048], BF16, "g_t") for _ in range(NB)]
        cqn = [cx.sb(st, [128, 384], BF16, "cqn") for _ in range(NB)]
        cqnT = [cx.sb(st, [128, 3, 128], BF16, "cqnT") for _ in range(NB)]
        q_t1 = cx.sb(st, [128, NH, 96], F32, "q_t")
        q_t = [q_t1, q_t1]
        qb_t = [cx.sb(st, [128, NH, 96], BF16, "qb_t") for _ in range(NB)]
        qr_tmp1 = cx.sb(st, [128, 4, NH, 16], F32, "qr_tmp")
        qr_tmp = [qr_tmp1, qr_tmp1]
        qT_t = [cx.sb(st, [96, NH, 128], BF16, "qT_t") for _ in range(NB)]
        ckv_t = [cx.sb(st, [128, 256], F32, "ckv_t") for _ in range(NB)]
        ckvb_t = [cx.sb(st, [128, 288], BF16, "ckvb_t") for _ in range(NB)]
        kr_t = [cx.sb(st, [128, 32], F32, "kr_t") for _ in range(NB)]
        kr_tmp = [cx.sb(st, [128, 64], F32, "kr_tmp") for _ in range(NB)]
        kvT_t = [cx.sb(st, [128, 3, 128], BF16, "kvT_t") for _ in range(NB)]
        qlT_t = cx.sb(st, [128, 2, 16, NH, 8], BF16, "qlT_t")

        tiles = [("ctx", i) for i in range(L_CTX // 128)] + [("own", i) for i in range(L_OWN // 128)] + [("smp", 0)]
        for ti, (kind, i) in enumerate(tiles):
            s_ = ti % NB
            X, CS = {"ctx": (x_ctx, cs_ctx), "own": (x_own, cs_own), "smp": (x_smp, cs_smp)}[kind]
            tok0 = {"ctx": 0, "own": L_CTX, "smp": L_CTX + L_OWN}[kind] + i * 128
            full = kind != "ctx"
            xt_, st_, xn_, xnT_ = xt[s_], stat[s_], xn[s_], xnT[s_]
            b.dma(xt_[:], X.t[i * 128:(i + 1) * 128, :], [], [xt_], key=f"xt{s_}")
            b.dma(cs[s_][:], CS.t[i * 128:(i + 1) * 128, :], [], [cs[s_]], key=f"cs{s_}")
            b.act(junk[s_][:], xt_[:], AF.Square, [xt_], [st_], accum=st_[:, 0:1])
            b.rstd(st_, 0, 1, 2, 1.0 / D)
            b.stt("dve", xn_[:], xt_[:], st_[:, 2:3], gin[:], ALU.mult, ALU.mult, [xt_, st_, gin], [xn_])
            pt = nxt(pT, "pT")
            for k in range(8):
                b.tr(pt[:, k * 128:(k + 1) * 128], xn_[:, k * 128:(k + 1) * 128], ident[:], [xn_, ident], [pt])
            b.copy("act", xnT_[:].rearrange("p k t -> p (k t)"), pt[:], [pt], [xnT_])

            def proj(c0, n):
                pzt = nxt(pz, "pz")
                for k in range(8):
                    b.mm(pzt[:, 0:n], xnT_[:, k, :], w_sb[:, k, c0:c0 + n], k == 0, k == 7, [xnT_, w_sb], [pzt])
                return pzt

            for hf in range(2):
                pzt = proj(C_US + hf * 512, 512)
                b.copy("dve" if hf else "act", u_t[s_][:, hf * 512:(hf + 1) * 512], pzt[:, :], [pzt], [u_t[s_]])
            b.dma(g["u_scr"].t[tok0:tok0 + 128, :], u_t[s_][:], [u_t[s_]], [], key=f"uo{s_}")
            pzt = proj(C_KV, 288)
            b.act(junk[s_][:, 0:256], pzt[:, 0:256], AF.Square, [pzt], [st_], accum=st_[:, 3:4])
            b.rstd(st_, 3, 4, 5, 1.0 / 256)
            b.stt("dve", ckv_t[s_][:], pzt[:, 0:256], st_[:, 5:6], gkv[:], ALU.mult, ALU.mult, [pzt, st_, gkv], [ckv_t[s_]])
            kt = kr_tmp[s_]
            c_, s2_ = cs[s_][:, 0:16], cs[s_][:, 16:32]
            b.tt("dve", kt[:, 0:16], pzt[:, 256:272], c_, ALU.mult, [pzt, cs[s_]], [kt])
            b.tt("dve", kt[:, 16:32], pzt[:, 272:288], s2_, ALU.mult, [pzt, cs[s_]], [kt])
            b.tt("dve", kt[:, 32:48], pzt[:, 256:272], s2_, ALU.mult, [pzt, cs[s_]], [kt])
            b.tt("dve", kt[:, 48:64], pzt[:, 272:288], c_, ALU.mult, [pzt, cs[s_]], [kt])
            b.tt("dve", kr_t[s_][:, 0:16], kt[:, 0:16], kt[:, 16:32], ALU.subtract, [kt], [kr_t[s_]])
            b.tt("dve", kr_t[s_][:, 16:32], kt[:, 32:48], kt[:, 48:64], ALU.add, [kt], [kr_t[s_]])
            if kind == "own":
                b.dma(g["o_ckv_p"].t[i * 128:(i + 1) * 128, :], ckv_t[s_][:], [ckv_t[s_]], [], key=f"ckvo{s_}")
                b.dma(g["o_kr_p"].t[i * 128:(i + 1) * 128, :], kr_t[s_][:], [kr_t[s_]], [], key=f"kro{s_}")
            if kind == "smp":
                b.dma(g["o_ckv_s"].t[:, :], ckv_t[s_][:], [ckv_t[s_]], [], key=f"ckvo{s_}")
                b.dma(g["o_kr_s"].t[:, :], kr_t[s_][:], [kr_t[s_]], [], key=f"kro{s_}")
            cb = ckvb_t[s_]
            b.copy("pool", cb[:, 0:256], ckv_t[s_][:], [ckv_t[s_]], [cb])
            b.copy("pool", cb[:, 256:288], kr_t[s_][:], [kr_t[s_]], [cb])
            pt = nxt(pT, "pT")
            b.tr(pt[:, 0:128], cb[:, 0:128], ident[:], [cb, ident], [pt])
            b.tr(pt[:, 128:256], cb[:, 128:256], ident[:], [cb, ident], [pt])
            b.tr(pt[0:32, 256:384], cb[:, 256:288], ident[:], [cb, ident], [pt])
            kvT = kvT_t[s_]
            b.copy("act", kvT[:, 0:2, :].rearrange("p k t -> p (k t)"), pt[:, 0:256], [pt], [kvT])
            b.copy("act", kvT[0:32, 2, :], pt[0:32, 256:384], [pt], [kvT])
            if kind == "smp":
                b.dma(g["ckvs_scr"].t[:, :], cb[:, 0:256], [cb], [], key="ckvs")
                for k in range(2):
                    b.dma(g["ckvsT_scr"].t[k], kvT[:, k, :], [kvT], [], key=f"kvTo{s_}")
                b.dma(g["krsT_scr"].t[:, :], kvT[0:32, 2, :], [kvT], [], key=f"kvTo{s_}")
            else:
                p0 = tok0
                for k in range(2):
                    b.dma(g["ckvT_scr"].t[k, :, p0:p0 + 128], kvT[:, k, :], [kvT], [], key=f"kvTo{s_}")
                b.dma(g["krT_scr"].t[:, p0:p0 + 128], kvT[0:32, 2, :], [kvT], [], key=f"kvTo{s_}")
            if not full:
                continue
            gr0 = i * 128 if kind == "own" else L_OWN
            glist = ((C_GS, 0, AF.Silu), (C_GS + 512, 512, AF.Silu), (C_GA, 1024, AF.Silu), (C_GA + 512, 1536, AF.Silu),
                     (C_MS, 2048, AF.Sigmoid), (C_MS + 512, 2560, AF.Sigmoid), (C_MA, 3072, AF.Sigmoid), (C_MA + 512, 3584, AF.Sigmoid))
            for hf in range(2):
                gt = g_t[hf]
                for (c0, o0, fn) in glist[hf * 4:(hf + 1) * 4]:
                    pzt = proj(c0, 512)
                    b.act(gt[:, o0 - hf * 2048:o0 - hf * 2048 + 512], pzt[:, :], fn, [pzt], [gt])
                b.dma(g["g_scr"].t[gr0:gr0 + 128, hf * 2048:(hf + 1) * 2048], gt[:], [gt], [], key=f"go{hf}")
            pzt = proj(C_CQ, 384)
            b.act(junk[s_][:, 0:384], pzt[:, 0:384], AF.Square, [pzt], [st_], accum=st_[:, 6:7])
            b.rstd(st_, 6, 7, 6, 1.0 / 384)
            b.stt("dve", cqn[s_][:], pzt[:, 0:384], st_[:, 6:7], gq[:], ALU.mult, ALU.mult, [pzt, st_, gq], [cqn[s_]])
            pt = nxt(pT, "pT")
            for k in range(3):
                b.tr(pt[:, k * 128:(k + 1) * 128], cqn[s_][:, k * 128:(k + 1) * 128], ident[:], [cqn[s_], ident], [pt])
            b.copy("act", cqnT[s_][:].rearrange("p k t -> p (k t)"), pt[:, 0:384], [pt], [cqnT[s_]])
            qv = q_t[s_][:].rearrange("p h d -> p (h d)")
            for cg in range(3):
                pqt = nxt(pq, "pq")
                for k in range(3):
                    b.mm(pqt[:, :], cqnT[s_][:, k, :], wuq_sb[:, k, cg * 512:(cg + 1) * 512], k == 0, k == 2, [cqnT[s_], wuq_sb], [pqt])
                b.copy("act" if cg % 2 else "dve", qv[:, cg * 512:(cg + 1) * 512], pqt[:, :], [pqt], [q_t[s_]])
            q3 = q_t[s_]
            tm = qr_tmp[s_]
            cb3 = cs[s_][:, 0:16].unsqueeze(1).broadcast_to([128, NH, 16])
            sb3 = cs[s_][:, 16:32].unsqueeze(1).broadcast_to([128, NH, 16])
            b.tt("pool", tm[:, 0], q3[:, :, 64:80], cb3, ALU.mult, [q3, cs[s_]], [tm])
            b.tt("pool", tm[:, 1], q3[:, :, 80:96], sb3, ALU.mult, [q3, cs[s_]], [tm])
            b.tt("pool", tm[:, 2], q3[:, :, 64:80], sb3, ALU.mult, [q3, cs[s_]], [tm])
            b.tt("pool", tm[:, 3], q3[:, :, 80:96], cb3, ALU.mult, [q3, cs[s_]], [tm])
            qb = qb_t[s_]
            b.copy("pool", qb[:, :, 0:64], q3[:, :, 0:64], [q3], [qb])
            b.tt("dve", qb[:, :, 64:80], tm[:, 0], tm[:, 1], ALU.subtract, [tm], [qb])
            b.tt("dve", qb[:, :, 80:96], tm[:, 2], tm[:, 3], ALU.add, [tm], [qb])
            qT = qT_t[s_]
            for hg in range(2):
                pt = nxt(pT, "pT")
                for hh in range(8):
                    b.tr(pt[0:96, hh * 128:(hh + 1) * 128], qb[:, hg * 8 + hh, :], ident[:], [qb, ident], [pt])
                b.copy("act" if hg else "dve", qT[:, hg * 8:(hg + 1) * 8, :].rearrange("p h t -> p (h t)"), pt[0:96, :], [pt], [qT])
            if kind == "own":
                b.dma(g["qT_scr"].t[:, :, i * 128:(i + 1) * 128].rearrange("h r t -> r h t"), qT[:, :, :], [qT], [], key=f"qTo{s_}")
            else:
                for hg in range(4):
                    for k in range(2):
                        pqt = nxt(pq, "pq")
                        for hh in range(4):
                            h = hg * 4 + hh
                            b.mm(pqt[:, hh * 128:(hh + 1) * 128], wukT_sb[:, h, k * 128:(k + 1) * 128], qT[0:64, h, :], True, True, [wukT_sb, qT], [pqt])
                        b.copy("dve", qlT_t[:, k, :, hg * 4:(hg + 1) * 4, :].rearrange("p b h t -> p h b t"),
                               pqt[:, :].rearrange("p (h b t) -> p h b t", h=4, b=16), [pqt], [qlT_t])
                for k in range(2):
                    b.dma(g["qlT_scr"].t[k], qlT_t[:, k].rearrange("p b h t -> p b (h t)"), [qlT_t], [], key="qlo")
                for h in range(NH):
                    b.dma(g["qrT_scr"].t[:, :, h * 8:(h + 1) * 8], qT[64:96, h, :].rearrange("p (b t) -> p b t", b=16), [qT], [], key="qro")
    P.barrier()


def rope_tables(pos):
    half = 16
    inv = (10000.0 ** (-np.arange(half, dtype=np.float32) * np.float32(2.0 / 32))).astype(np.float32)
    ang = pos.astype(np.float32)[:, None] * inv[None, :]
    return np.concatenate([np.cos(ang), np.sin(ang)], axis=1).astype(np.float32)


_NC_CACHE = {}


def make_in_maps(inp, phases=("A", "S", "P", "G", "B")):
    maps = []
    r_ = np.arange(128)
    hq, tq = r_ // 8, r_ % 8
    bk, tk = r_ // 8, r_ % 8
    smp_mask = np.where((bk[None, None, :] == np.arange(16)[:, None, None]) & (tk[None, None, :] <= tq[None, :, None]), 0.0, NEG).astype(np.float32)
    cck = np.asarray(inp["cache_ckv"], np.float32).reshape(20480, 128 * 256) if "G" in phases else None
    ckr = np.asarray(inp["cache_krope"], np.float32).reshape(20480, 128 * 32) if "G" in phases else None
    xp = np.asarray(inp["x_prompt"], np.float32)
    xs = np.asarray(inp["x_sample"], np.float32)
    for c in range(8):
        bi, h = c // 2, c % 2
        m = {}
        m["x_own"] = np.ascontiguousarray(xp[bi, h * L_OWN:(h + 1) * L_OWN])
        m["x_ctx"] = np.ascontiguousarray(xp[bi, 0:L_CTX]) if h == 1 else np.zeros((L_CTX, D), np.float32)
        m["x_smp"] = np.ascontiguousarray(xs[16 * c:16 * c + 16].reshape(128, D))
        m["cs_ctx"] = rope_tables(np.arange(L_CTX))
        m["cs_own"] = rope_tables(h * L_OWN + np.arange(L_OWN))
        m["cs_smp"] = rope_tables(np.tile(PAST + np.arange(8), 16))
        m["state_s"] = np.ascontiguousarray(np.asarray(inp["state_ssm"], np.float32)[16 * c:16 * c + 16])
        for k in ("norm_in", "w_in", "mla_q_norm", "mla_kv_norm", "ssm_lambda_re", "ssm_lambda_im", "ssm_log_dt",
                  "ssm_b_re", "ssm_b_im", "ssm_c_re", "ssm_c_im", "ssm_d"):
            m[k] = np.asarray(inp[k], np.float32)
        m["mla_w_uq"] = np.asarray(inp["mla_w_uq"], np.float32).reshape(384, 1536)
        m["mla_w_uk"] = np.asarray(inp["mla_w_uk"], np.float32).reshape(256, 1024)
        m["mla_w_uv"] = np.asarray(inp["mla_w_uv"], np.float32).reshape(256, 1024)
        m["ctx_bias"] = np.full((128, 1), 0.0 if h == 1 else NEG, np.float32)
        if "G" in phases:
            m["smp_mask"] = smp_mask
            m["pt_core"] = np.ascontiguousarray(np.asarray(inp["page_table"], np.int32)[16 * c:16 * c + 16])
            m["cache_ckv"] = cck
            m["cache_krope"] = ckr
        if "B" in phases:
            for k in ("ssm_w_glu", "ssm_b_glu", "w_br_ssm", "w_br_attn", "w_out", "norm_final"):
                m[k] = np.asarray(inp[k], np.float32)
        maps.append(m)
    return maps


def kernel(**inp):
    if "nc" not in _NC_CACHE:
        _NC_CACHE["nc"] = build(phases=("A", "S", "P", "G", "B"))
    nc = _NC_CACHE["nc"]
    maps = make_in_maps(inp)
    res = run_bass_kernel_spmd(nc, maps, core_ids=list(range(8)))
    R = res.results
    B_, L_ = 4, 4096
    y_p = np.zeros((B_, L_, D), np.float32)
    y_s = np.zeros((128, 8, D), np.float32)
    ckv_p = np.zeros((B_, L_, 256), np.float32)
    kr_p = np.zeros((B_, L_, 32), np.float32)
    ssm_p = np.zeros((B_, 64, 64, 2), np.float32)
    ckv_s = np.zeros((128, 8, 256), np.float32)
    kr_s = np.zeros((128, 8, 32), np.float32)
    ssm_s = np.zeros((128, 64, 64, 2), np.float32)
    for c in range(8):
        bi, h = c // 2, c % 2
        r = R[c]
        ckv_p[bi, h * L_OWN:(h + 1) * L_OWN] = r["o_ckv_p"]
        kr_p[bi, h * L_OWN:(h + 1) * L_OWN] = r["o_kr_p"]
        ckv_s[16 * c:16 * c + 16] = r["o_ckv_s"].reshape(16, 8, 256)
        kr_s[16 * c:16 * c + 16] = r["o_kr_s"].reshape(16, 8, 32)
        if "o_y_p" in r:
            y_p[bi, h * L_OWN:(h + 1) * L_OWN] = r["o_y_p"]
            y_s[16 * c:16 * c + 16] = r["o_y_s"].reshape(16, 8, D)
        if "o_ssm_s" in r:
            ssm_s[16 * c:16 * c + 16] = r["o_ssm_s"]
            if h == 1:
                ssm_p[bi] = r["o_ssm_p"]
    return (y_p, y_s, ckv_p, kr_p, ssm_p, ckv_s, kr_s, ssm_s)


def phase_S(nc, cx, b, P, g):
    ident, identf = g["ident"], g["identf"]
    NG = 64
    GP = 16
    with contextlib.ExitStack() as st:
        swapf = cx.sb(st, [128, 128], F32, "swapf")
        sgn = cx.sb(st, [128, 1], F32, "sgn")
        b.memset("pool", swapf[:], 0.0, [swapf])
        for base in (64, -64):
            P.add("pool", lambda e, base=base: e.affine_select(out=swapf[:], in_=swapf[:], pattern=[[-1, 128]], compare_op=ALU.not_equal,
                                                               fill=1.0, base=base, channel_multiplier=1), [swapf.r], [swapf.r])
        b.memset("pool", sgn[0:64, :], 1.0, [sgn])
        b.memset("pool", sgn[64:128, :], -1.0, [sgn])
        pf = [cx.ps(st, [128, 512], F32, "pf") for _ in range(3)]
        pb = [cx.ps(st, [128, 1024], BF16, "pb") for _ in range(2)]
        cnt = {"pf": 0, "pb": 0}

        def nxt(lst, k):
            cnt[k] += 1
            return lst[cnt[k] % len(lst)]

        lre = cx.sb(st, [128, NG], F32, "lre")
        lim = cx.sb(st, [128, NG], F32, "lim")
        dt = cx.sb(st, [128, NG], F32, "dt")
        wk = cx.sb(st, [128, 12, NG], F32, "wk")
        Lr = cx.sb(st, [128, 9, NG], F32, "Lr")
        Li = cx.sb(st, [128, 9, NG], F32, "Li")
        Ar = cx.sb(st, [128, 9, NG], F32, "Ar")
        Ai = cx.sb(st, [128, 9, NG], F32, "Ai")
        Aiu = cx.sb(st, [128, 9, NG], F32, "Aiu")
        Lis = cx.sb(st, [128, 9, NG], F32, "Lis")
        O_all = cx.sb(st, [128, NG, 8, 16], BF16, "O_all")
        Dv = cx.sb(st, [128, NG], F32, "Dv")
        T_all = cx.sb(st, [128, NG, 128], BF16, "T_all")
        R_all = cx.sb(st, [128, NG, 128], BF16, "R_all")
        pre = contextlib.ExitStack()
        ld = cx.sb(pre, [64, 2, 128], F32, "ld")
        for j, nm in enumerate(("ssm_lambda_re", "ssm_lambda_im")):
            for hf in range(2):
                b.dma(ld[:, j, hf * 64:(hf + 1) * 64], g[nm].t[:, :], [], [ld], key="ld")
        for j, dst in enumerate((lre, lim)):
            pt = nxt(pf, "pf")
            b.tr(pt[:, 0:64], ld[:, j, :], identf[0:64, 0:64], [ld, identf], [pt])
            b.copy("dve", dst[:], pt[:, 0:64], [pt], [dst])
        b.dma(dt[:], bcast_rows(g["ssm_log_dt"].t, NG), [], [dt], key="dt")
        b.act(dt[:], dt[:], AF.Exp, [dt], [dt])
        a_, th, mag, r1, r2, sn, cs_ = (wk[:, i, :] for i in range(7))
        b.tt("dve", a_, lre[:], dt[:], ALU.mult, [lre, dt], [wk])
        b.tt("dve", th, lim[:], dt[:], ALU.mult, [lim, dt], [wk])
        b.act(mag, a_, AF.Exp, [wk], [wk])
        b.ts("dve", r1, th, 1.0 / 64, None, ALU.mult, None, [wk], [wk])
        b.ts("dve", r2, th, 1.0 / 64, 0.5 * math.pi, ALU.mult, ALU.add, [wk], [wk])
        b.act(sn, r1, AF.Sin, [wk], [wk])
        b.act(cs_, r2, AF.Sin, [wk], [wk])
        for _ in range(6):
            b.tt("dve", r1, cs_, cs_, ALU.mult, [wk], [wk])
            b.tt("dve", r2, sn, sn, ALU.mult, [wk], [wk])
            b.tt("dve", wk[:, 7, :], cs_, sn, ALU.mult, [wk], [wk])
            b.tt("dve", cs_, r1, r2, ALU.subtract, [wk], [wk])
            b.ts("dve", sn, wk[:, 7, :], 2.0, None, ALU.mult, None, [wk], [wk])
        b.memset("dve", Lr[:, 0, :], 1.0, [Lr])
        b.memset("dve", Li[:, 0, :], 0.0, [Li])
        b.tt("dve", Lr[:, 1, :], mag, cs_, ALU.mult, [wk], [Lr])
        b.tt("dve", Li[:, 1, :], mag, sn, ALU.mult, [wk], [Li])
        t0, t1 = wk[:, 7, :], wk[:, 8, :]

        def cmul(or_, oi_, ar, ai, br, bi, regs_w):
            b.tt("dve", t0, ar, br, ALU.mult, [Lr, Li, Ar, Aiu, wk], [wk])
            b.tt("dve", t1, ai, bi, ALU.mult, [Lr, Li, Ar, Aiu, wk], [wk])
            b.tt("dve", or_, t0, t1, ALU.subtract, [wk], regs_w)
            b.tt("dve", t0, ar, bi, ALU.mult, [Lr, Li, Ar, Aiu, wk], [wk])
            b.tt("dve", t1, ai, br, ALU.mult, [Lr, Li, Ar, Aiu, wk], [wk])
            b.tt("dve", oi_, t0, t1, ALU.add, [wk], regs_w)

        for e in range(1, 8):
            cmul(Lr[:, e + 1, :], Li[:, e + 1, :], Lr[:, e, :], Li[:, e, :], Lr[:, 1, :], Li[:, 1, :], [Lr, Li])
        b.copy("dve", Ar[:, 0, :], Lr[:, 8, :], [Lr], [Ar])
        b.copy("dve", Aiu[:, 0, :], Li[:, 8, :], [Li], [Aiu])
        for l in range(8):
            cmul(Ar[:, l + 1, :], Aiu[:, l + 1, :], Ar[:, l, :], Aiu[:, l, :], Ar[:, l, :], Aiu[:, l, :], [Ar, Aiu])
        b.ts("dve", Ai[:].rearrange("p l g -> p (l g)"), Aiu[:].rearrange("p l g -> p (l g)"), sgn[:, 0:1], None, ALU.mult, None, [Aiu, sgn], [Ai])
        cr, ci, den, nr = wk[:, 9, :], wk[:, 10, :], wk[:, 11, :], wk[:, 0, :]
        b.ts("dve", nr, Lr[:, 1, :], -1.0, None, ALU.add, None, [Lr], [wk])
        b.tt("dve", t0, lre[:], lre[:], ALU.mult, [lre], [wk])
        b.tt("dve", t1, lim[:], lim[:], ALU.mult, [lim], [wk])
        b.tt("dve", den, t0, t1, ALU.add, [wk], [wk])
        P.add("dve", lambda e: e.reciprocal(out=den, in_=den), [wk.r], [wk.r])
        b.tt("dve", t0, nr, lre[:], ALU.mult, [wk, lre], [wk])
        b.tt("dve", t1, Li[:, 1, :], lim[:], ALU.mult, [Li, lim], [wk])
        b.tt("dve", cr, t0, t1, ALU.add, [wk], [wk])
        b.tt("dve", cr, cr, den, ALU.mult, [wk], [wk])
        b.tt("dve", t0, Li[:, 1, :], lre[:], ALU.mult, [Li, lre], [wk])
        b.tt("dve", t1, nr, lim[:], ALU.mult, [wk, lim], [wk])
        b.tt("dve", ci, t0, t1, ALU.subtract, [wk], [wk])
        b.tt("dve", ci, ci, den, ALU.mult, [wk], [wk])
        cis = wk[:, 1, :]
        b.ts("dve", cis, ci, sgn[:, 0:1], None, ALU.mult, None, [wk, sgn], [wk])
        b.ts("dve", Lis[:].rearrange("p l g -> p (l g)"), Li[:].rearrange("p l g -> p (l g)"), sgn[:, 0:1], None, ALU.mult, None, [Li, sgn], [Lis])

        Bx = cx.sb(pre, [128, NG, 16], F32, "Bx")
        By = cx.sb(pre, [128, NG, 16], F32, "By")
        bre = g["ssm_b_re"].t.rearrange("g p c -> p g c")
        bim = g["ssm_b_im"].t.rearrange("g p c -> p g c")
        b.dma(Bx[0:64], bre, [], [Bx], key="Bx")
        b.dma(Bx[64:128], bim, [], [Bx], key="Bx")
        b.dma(By[0:64], bim, [], [By], key="By")
        b.dma(By[64:128], bre, [], [By], key="By")
        BS = cx.sb(pre, [128, NG, 16], F32, "BS")
        BSp = cx.sb(pre, [128, NG, 16], F32, "BSp")
        tmpB = cx.sb(pre, [128, NG, 16], F32, "tmpB")

        def bc(ap2d):
            return ap2d.unsqueeze(2).broadcast_to([128, NG, 16])

        b.tt("pool", tmpB[:], By[:], bc(cis), ALU.mult, [By, wk], [tmpB])
        b.tt("pool", BS[:], Bx[:], bc(cr), ALU.mult, [Bx, wk], [BS])
        b.tt("pool", BS[:], BS[:], tmpB[:], ALU.subtract, [BS, tmpB], [BS])
        b.tt("pool", tmpB[:], Bx[:], bc(cis), ALU.mult, [Bx, wk], [tmpB])
        b.tt("pool", BSp[:], By[:], bc(cr), ALU.mult, [By, wk], [BSp])
        b.tt("pool", BSp[:], BSp[:], tmpB[:], ALU.add, [BSp, tmpB], [BSp])
        GT = cx.sb(pre, [128, NG, 15, 16], BF16, "GT")
        b.memset("pool", GT[:, :, 8:15, :], 0.0, [GT])
        tmpG = cx.sb(pre, [128, NG, 16], F32, "tmpG")
        for e in range(8):
            b.tt("pool", tmpB[:], BSp[:], bc(Lis[:, e, :]), ALU.mult, [BSp, Lis], [tmpB])
            b.tt("dve", tmpG[:], BS[:], bc(Lr[:, e, :]), ALU.mult, [BS, Lr], [tmpG])
            b.tt("dve", GT[:, :, 7 - e, :], tmpG[:], tmpB[:], ALU.subtract, [tmpG, tmpB], [GT])
        Cx = cx.sb(pre, [128, NG, 16], F32, "Cx")
        Cy = cx.sb(pre, [128, NG, 16], F32, "Cy")
        Cxb = cx.sb(pre, [128, NG, 16], BF16, "Cxb")
        cin = [cx.sb(pre, [128, 128], F32, "cin") for _ in range(2)]
        cre = g["ssm_c_re"].t.rearrange("g c p -> (g c) p")
        cim = g["ssm_c_im"].t.rearrange("g c p -> (g c) p")
        q = 0
        for j in range(8):
            for which in range(2):
                ci_ = cin[q % 2]
                q += 1
                a0, a1 = (cre, cim) if which == 0 else (cim, cre)
                b.dma(ci_[:, 0:64], a0[j * 128:(j + 1) * 128, :], [], [ci_], key=f"cin{q % 2}")
                b.dma(ci_[:, 64:128], a1[j * 128:(j + 1) * 128, :], [], [ci_], key=f"cin{q % 2}")
                pt = nxt(pf, "pf")
                b.tr(pt[:, 0:128], ci_[:], identf[:], [ci_, identf], [pt])
                dst = Cx if which == 0 else Cy
                dv = dst[:, j * 8:(j + 1) * 8, :].rearrange("p g c -> p (g c)")
                if which == 0:
                    b.ts("dve", dv, pt[:, 0:128], sgn[:, 0:1], None, ALU.mult, None, [pt, sgn], [dst])
                else:
                    b.copy("dve", dv, pt[:, 0:128], [pt], [dst])
        b.copy("pool", Cxb[:], Cx[:], [Cx], [Cxb])
        for t in range(8):
            b.tt("pool", tmpB[:], Cy[:], bc(Li[:, t + 1, :]), ALU.mult, [Cy, Li], [tmpB])
            b.tt("dve", tmpG[:], Cx[:], bc(Lr[:, t + 1, :]), ALU.mult, [Cx, Lr], [tmpG])
            b.tt("dve", O_all[:, :, t, :], tmpG[:], tmpB[:], ALU.subtract, [tmpG, tmpB], [O_all])
        dsrc = g["ssm_d"].t.rearrange("(g c) -> c g", c=16)
        for s in range(8):
            P.add("sp", lambda e, s=s: e.dma_start(out=Dv[s * 16:(s + 1) * 16, :], in_=dsrc, allow_slow_non_contiguous=True), [], [Dv.r], dma=True, key="Dv")
        for gi in range(NG):
            pt = nxt(pf, "pf")
            for t in range(8):
                b.mm(pt[:, t * 16:(t + 1) * 16], GT[:, gi, 7 - t:15 - t, :].rearrange("p s c -> p (s c)"), Cxb[:, gi, :], True, True, [GT, Cxb], [pt])
            b.stt("dve", T_all[:, gi, :], identf[:], Dv[:, gi:gi + 1], pt[:, 0:128], ALU.mult, ALU.add, [identf, Dv, pt], [T_all])
            if gi % 8 == 0:
                ptb = nxt(pb, "pb")
            b.tr(ptb[:, (gi % 8) * 128:(gi % 8 + 1) * 128], GT[:, gi, 0:8, :].rearrange("p s c -> p (s c)"), ident[:], [GT, ident], [ptb])
            if gi % 8 == 7:
                b.copy("act", R_all[:, gi - 7:gi + 1, :].rearrange("p g s -> p (g s)"), ptb[:], [ptb], [R_all])

        if g.get("debug"):
            b.dma(g["dbg_L"].t[:, 0:576], Lr[:].rearrange("p l g -> p (l g)"), [Lr], [], key="dbg")
            b.dma(g["dbg_L"].t[:, 576:1152], Li[:].rearrange("p l g -> p (l g)"), [Li], [], key="dbg")
            b.dma(g["dbg_L"].t[:, 1152:1728], Ar[:].rearrange("p l g -> p (l g)"), [Ar], [], key="dbg")
            b.dma(g["dbg_L"].t[:, 1728:2304], Ai[:].rearrange("p l g -> p (l g)"), [Ai], [], key="dbg")
            b.dma(g["dbg_T"].t[:, :], T_all[:].rearrange("p g c -> p (g c)"), [T_all], [], key="dbg")
            b.dma(g["dbg_R"].t[:, :], R_all[:].rearrange("p g c -> p (g c)"), [R_all], [], key="dbg")
            b.dma(g["dbg_O"].t[:, :], O_all[:].rearrange("p g t c -> p (g t c)"), [O_all], [], key="dbg")
        pre.close()
        P.barrier()
        if STAGE == "pre":
            return
        Mtmp = [cx.sb(st, [128, 128], F32, "Mtmp") for _ in range(2)]
        mt = [0]

        def build_M(eng, M, sr, si):
            if eng == "dve":
                b.ts(eng, M[:], identf[:], sr, None, ALU.mult, None, [identf, Ar, Ai], [M])
                b.stt(eng, M[:], swapf[:], si, M[:], ALU.mult, ALU.add, [swapf, Ar, Ai, M], [M])
            else:
                tmp = Mtmp[mt[0] % 2]
                tmp2 = Mtmp[(mt[0] + 1) % 2]
                b.ts(eng, tmp[:], identf[:], sr, None, ALU.mult, None, [identf, Ar, Ai], [tmp])
                b.ts(eng, tmp2[:], swapf[:], si, None, ALU.mult, None, [swapf, Ar, Ai], [tmp2])
                b.tt(eng, M[:], tmp[:], tmp2[:], ALU.add, [tmp, tmp2], [M])

        NCH = (L_CTX + L_OWN) // 8
        NOWN = L_OWN // 8
        uview = g["u_scr"].t[0:L_CTX + L_OWN, :].rearrange("(k s) c -> k s c", s=8)
        yview = g["ys_scr"].t[0:L_OWN, :].rearrange("(k s) c -> k s c", s=8)
        usv = g["u_scr"].t[L_CTX + L_OWN:L_CTX + L_OWN + 128, :].rearrange("(k s) c -> k s c", s=8)
        ysv = g["ys_scr"].t[L_OWN:L_OWN + 128, :].rearrange("(k s) c -> k s c", s=8)
        Up = [cx.sb(st, [128, GP, 8, 16], BF16, "Up") for _ in range(4)]
        Ups = cx.sb(st, [16, GP, 8, 16], BF16, "Ups")
        Uraw = [cx.sb(st, [128, 8, GP * 16], BF16, "Uraw") for _ in range(2)]
        Uraw_s = cx.sb(st, [16, 8, GP * 16], BF16, "Uraw_s")
        Yp = [cx.sb(st, [128, 8, GP * 16], BF16, "Yp") for _ in range(2)]
        Yps = cx.sb(st, [16, 8, GP * 16], BF16, "Yps")
        U8 = [cx.sb(st, [128, NCH], BF16, "U8") for _ in range(2)]
        U8s = [cx.sb(st, [128, 16], BF16, "U8s") for _ in range(2)]
        Ssb = [cx.sb(st, [128, NCH], BF16, "Ssb") for _ in range(2)]
        Msb = [cx.sb(st, [128, 128], BF16, "Msb") for _ in range(4)]
        Hsb = [cx.sb(st, [128, NOWN], BF16, "Hsb") for _ in range(2)]
        M0f = [cx.sb(st, [128, 128], F32, "M0f") for _ in range(2)]
        Y8 = [cx.sb(st, [128, NOWN], BF16, "Y8") for _ in range(2)]
        Y8s = [cx.sb(st, [128, 16], BF16, "Y8s") for _ in range(2)]
        hl = cx.sb(st, [128, NG], F32, "hl")
        hlo = cx.sb(st, [64, 64, 2], F32, "hlo")
        st_in = cx.sb(st, [16, GP, 64, 2], F32, "st_in")
        st_out = cx.sb(st, [16, GP, 64, 2], F32, "st_out")
        st_r = cx.sb(st, [16, GP, 2, 64], F32, "st_r")
        h0T = [cx.sb(st, [128, 16], F32, "h0T") for _ in range(2)]
        h0Tb = [cx.sb(st, [128, 16], BF16, "h0Tb") for _ in range(2)]
        hoT = [cx.sb(st, [128, 16], F32, "hoT") for _ in range(2)]
        mcnt = 0
        for half in range(NG // GP):
            c0 = half * GP * 16
            for j in range(4):
                raw = Uraw[j % 2]
                b.dma(raw[:], uview[j * 128:(j + 1) * 128, :, c0:c0 + GP * 16], [], [raw], key=f"Uraw{j % 2}")
                b.copy("pool" if j % 2 else "act", Up[j][:], raw[:].rearrange("k s (g c) -> k g s c", c=16), [raw], [Up[j]])
            b.dma(Uraw_s[:], usv[:, :, c0:c0 + GP * 16], [], [Uraw_s], key="Ups")
            b.copy("pool", Ups[:], Uraw_s[:].rearrange("k s (g c) -> k g s c", c=16), [Uraw_s], [Ups])
            b.dma(st_in[:].rearrange("b g p r -> b (g p r)"), g["state_s"].t[:, half * GP:(half + 1) * GP, :, :].rearrange("b g p r -> b (g p r)"), [], [st_in], key="st_in")
            b.copy("pool", st_r[:], st_in[:].rearrange("b g p r -> b g r p"), [st_in], [st_r])
            for gl in range(GP):
                gi = half * GP + gl
                s_ = gi % 2
                ptb = nxt(pb, "pb")
                for j in range(4):
                    b.tr(ptb[:, j * 128:(j + 1) * 128], Up[j][:, gl, :, :].rearrange("k s c -> k (s c)"), ident[:], [Up[j], ident], [ptb])
                b.tr(ptb[:, 512:528], Ups[:, gl, :, :].rearrange("k s c -> k (s c)"), ident[0:16, 0:16], [Ups, ident], [ptb])
                b.copy("act", U8[s_][:], ptb[:, 0:512], [ptb], [U8[s_]])
                b.copy("dve", U8s[s_][:], ptb[:, 512:528], [ptb], [U8s[s_]])
                if STAGE == "m1":
                    continue
                pS = nxt(pf, "pf")
                b.mm(pS[:, :], R_all[:, gi, :], U8[s_][:], True, False, [R_all, U8[s_]], [pS])
                for l in range(9):
                    d = 1 << l
                    b.copy("act" if l % 2 else "dve", Ssb[s_][:], pS[:, :], [pS], [Ssb[s_]])
                    M = Msb[mcnt % 4]
                    mcnt += 1
                    build_M("pool" if l % 3 else "dve", M, Ar[:, l, gi:gi + 1], Ai[:, l, gi:gi + 1])
                    b.mm(pS[:, d:NCH], M[:], Ssb[s_][:, 0:NCH - d], False, l == 8, [M, Ssb[s_]], [pS])
                b.copy("dve", hl[:, gi:gi + 1], pS[:, NCH - 1:NCH], [pS], [hl])
                if STAGE == "m2":
                    continue
                pY = nxt(pf, "pf")
                b.mm(pY[:, 0:NOWN], T_all[:, gi, :], U8[s_][:, NCH - NOWN:NCH], True, False, [T_all, U8[s_]], [pY])
                b.copy("act", Hsb[s_][:], pS[:, NCH - NOWN - 1:NCH - 1], [pS], [Hsb[s_]])
                b.mm(pY[:, 0:NOWN], O_all[:, gi, :, :].rearrange("p t c -> p (t c)"), Hsb[s_][:], False, True, [O_all, Hsb[s_]], [pY])
                b.copy("dve", Y8[s_][:], pY[:, 0:NOWN], [pY], [Y8[s_]])
                if STAGE == "m3":
                    continue
                ptb2 = nxt(pb, "pb")
                for j in range(2):
                    b.tr(ptb2[:, j * 128:(j + 1) * 128], Y8[s_][:, j * 128:(j + 1) * 128], ident[:], [Y8[s_], ident], [ptb2])
                for j in range(2):
                    b.copy("dve", Yp[j][:, :, gl * 16:(gl + 1) * 16],
                           ptb2[:, j * 128:(j + 1) * 128].rearrange("p (t c) -> p t c", t=8), [ptb2], [Yp[j]])
                if STAGE != "nosmp":
                    pH = nxt(pf, "pf")
                    b.tr(pH[:, 0:16], st_r[:, gl, :, :].rearrange("b r p -> b (r p)"), identf[0:16, 0:16], [st_r, identf], [pH])
                    b.copy("dve", h0T[s_][:], pH[:, 0:16], [pH], [h0T[s_]])
                    b.copy("dve", h0Tb[s_][:], pH[:, 0:16], [pH], [h0Tb[s_]])
                    Mf = M0f[s_]
                    build_M("pool", Mf, Ar[:, 0, gi:gi + 1], Ai[:, 0, gi:gi + 1])
                    pX = nxt(pf, "pf")
                    b.mm(pX[:, 0:16], R_all[:, gi, :], U8s[s_][:], True, False, [R_all, U8s[s_]], [pX])
                    b.mm(pX[:, 0:16], Mf[:], h0T[s_][:], False, True, [Mf, h0T[s_]], [pX])
                    b.mm(pX[:, 16:32], T_all[:, gi, :], U8s[s_][:], True, False, [T_all, U8s[s_]], [pX])
                    b.mm(pX[:, 16:32], O_all[:, gi, :, :].rearrange("p t c -> p (t c)"), h0Tb[s_][:], False, True, [O_all, h0Tb[s_]], [pX])
                    b.copy("dve", hoT[s_][:], pX[:, 0:16], [pX], [hoT[s_]])
                    b.copy("dve", Y8s[s_][:], pX[:, 16:32], [pX], [Y8s[s_]])
                    pO = nxt(pf, "pf")
                    b.tr(pO[0:16, 0:128], hoT[s_][:], identf[:], [hoT[s_], identf], [pO])
                    b.copy("dve", st_out[:, gl, :, :].rearrange("b p r -> b r p"), pO[0:16, 0:128].rearrange("b (r p) -> b r p", r=2), [pO], [st_out])
                    ptb3 = nxt(pb, "pb")
                    b.tr(ptb3[0:16, 0:128], Y8s[s_][:], ident[:], [Y8s[s_], ident], [ptb3])
                    b.copy("dve", Yps[:, :, gl * 16:(gl + 1) * 16], ptb3[0:16, 0:128].rearrange("p (t c) -> p t c", t=8), [ptb3], [Yps])
            for j in range(2):
                b.dma(yview[j * 128:(j + 1) * 128, :, c0:c0 + GP * 16], Yp[j][:], [Yp[j]], [], key=f"Yp{j}")
            b.dma(ysv[:, :, c0:c0 + GP * 16], Yps[:], [Yps], [], key="Yps")
            b.dma(g["o_ssm_s"].t[:, half * GP:(half + 1) * GP, :, :].rearrange("b g p r -> b (g p r)"), st_out[:].rearrange("b g p r -> b (g p r)"), [st_out], [], key="st_out")
        pO = nxt(pf, "pf")
        b.tr(pO[0:64, 0:128], hl[:], identf[:], [hl, identf], [pO])
        b.copy("dve", hlo[:, :, :].rearrange("g p r -> g r p"), pO[0:64, 0:128].rearrange("g (r p) -> g r p", r=2), [pO], [hlo])
        b.dma(g["o_ssm_p"].t[:, :, :], hlo[:], [hlo], [], key="hlo")
    P.barrier()


def load_cast_w(cx, b, st, src2d, kchunks, ncols, name, stg, eng_cycle=("pool", "dve", "act")):
    w = cx.sb(st, [128, kchunks, ncols], BF16, name)
    v = src2d.rearrange("(k p) c -> k p c", p=128)
    for k in range(kchunks):
        s = stg[k % len(stg)]
        b.dma(s[:, 0:ncols], v[k], [], [s], key=f"wstg{k % len(stg)}")
        b.copy(eng_cycle[k % len(eng_cycle)], w[:, k, :], s[:, 0:ncols], [s], [w])
    return w


def phase_P(nc, cx, b, P, g):
    ident = g["ident"]
    LT = L_CTX + L_OWN
    NQB = L_OWN // 128
    with contextlib.ExitStack() as st:
        stg = [cx.sb(st, [128, 1024], F32, "stgP") for _ in range(2)]
        wuk = load_cast_w(cx, b, st, g["mla_w_uk"].t, 2, 1024, "wukP", stg)
        wuv = load_cast_w(cx, b, st, g["mla_w_uv"].t, 2, 1024, "wuvP", stg)
        ckvT = cx.sb(st, [128, 2, LT], BF16, "ckvT")
        for k in range(2):
            b.dma(ckvT[:, k, :], g["ckvT_scr"].t[k], [], [ckvT], key="ckvT")
        KT = [cx.sb(st, [96, LT], BF16, "KT") for _ in range(2)]
        for i in range(2):
            b.dma(KT[i][64:96, :], g["krT_scr"].t[:, :], [], [KT[i]], key="KTr")
        Vh = [cx.sb(st, [128, LT // 128, 64], BF16, "Vh") for _ in range(2)]
        qT = [cx.sb(st, [96, L_OWN], BF16, "qTh") for _ in range(2)]
        S_sb = [cx.sb(st, [128, LT], F32, "S_sb") for _ in range(2)]
        P_bf = [cx.sb(st, [128, LT], BF16, "P_bf") for _ in range(2)]
        PT = [cx.sb(st, [128, LT // 128, 128], BF16, "PT") for _ in range(2)]
        sm = [cx.sb(st, [128, 8], F32, "smP") for _ in range(2)]
        o_t = [cx.sb(st, [128, 64], BF16, "o_t") for _ in range(4)]
        tril = cx.sb(st, [128, 128], F32, "tril")
        cbias = cx.sb(st, [128, 1], F32, "cbias")
        b.dma(cbias[:], g["ctx_bias"].t[:, :], [], [cbias], key="cbias")
        b.memset("pool", tril[:], 0.0, [tril])
        P.add("pool", lambda e: e.affine_select(out=tril[:], in_=tril[:], pattern=[[-1, 128]], compare_op=ALU.is_ge,
                                                fill=NEG, base=0, channel_multiplier=1), [tril.r], [tril.r])
        pf = [cx.ps(st, [128, 512], F32, "pfP") for _ in range(4)]
        pb = [cx.ps(st, [128, 1024], BF16, "pbP") for _ in range(2)]
        po = [cx.ps(st, [128, 512], F32, "poP") for _ in range(2)]
        cnt = {"pf": 0, "pb": 0, "po": 0, "ev": 0}

        def nxt(lst, k):
            cnt[k] += 1
            return lst[cnt[k] % len(lst)]

        def ev_eng():
            cnt["ev"] += 1
            return "act" if cnt["ev"] % 2 else "dve"

        MAXENG = "dve"
        work = []

        def head_prep(h):
            hb = h % 2
            b.dma(qT[hb][:, :], g["qT_scr"].t[h], [], [qT[hb]], key=f"qTh{hb}")
            for tg in range(LT // 512):
                pz = nxt(pf, "pf")
                for k in range(2):
                    b.mm(pz[0:64, :], wuk[:, k, h * 64:(h + 1) * 64], ckvT[:, k, tg * 512:(tg + 1) * 512], k == 0, k == 1, [wuk, ckvT], [pz])
                b.copy(ev_eng(), KT[hb][0:64, tg * 512:(tg + 1) * 512], pz[0:64, :], [pz], [KT[hb]])
            for vg in range(LT // 1024):
                pz = nxt(pf, "pf")
                for j in range(8):
                    kt = vg * 8 + j
                    for k in range(2):
                        b.mm(pz[:, j * 64:(j + 1) * 64], ckvT[:, k, kt * 128:(kt + 1) * 128], wuv[:, k, h * 64:(h + 1) * 64], k == 0, k == 1, [ckvT, wuv], [pz])
                b.copy(ev_eng(), Vh[hb][:, vg * 8:(vg + 1) * 8, :].rearrange("p j d -> p (j d)"), pz[:, :], [pz], [Vh[hb]])

        for h in range(NH):
            hb = h % 2
            for j in range(NQB):
                work.append((h, hb, j))
        def part1(it, h, hb, j):
            s_ = it % 2
            nkb = L_CTX // 128 + j + 1
            nk = nkb * 128
            S, sm_ = S_sb[s_], sm[s_]
            for kg in range((nk + 511) // 512):
                n = min(512, nk - kg * 512)
                pz = nxt(pf, "pf")
                b.mm(pz[:, 0:n], qT[hb][:, j * 128:(j + 1) * 128], KT[hb][:, kg * 512:kg * 512 + n], True, True, [qT[hb], KT[hb]], [pz])
                if kg < L_CTX // 512:
                    b.act(S[:, kg * 512:kg * 512 + n], pz[:, 0:n], AF.Identity, [pz, cbias], [S], bias=cbias[:, 0:1], scale=SCALE)
                else:
                    b.ts("dve", S[:, kg * 512:kg * 512 + n], pz[:, 0:n], SCALE, None, ALU.mult, None, [pz], [S])
            b.tt("dve", S[:, nk - 128:nk], S[:, nk - 128:nk], tril[:], ALU.add, [S, tril], [S])
            b.reduce(MAXENG, sm_[:, 0:1], S[:, 0:nk], ALU.max, [S], [sm_])
            b.ts(MAXENG, sm_[:, 1:2], sm_[:, 0:1], -1.0, None, ALU.mult, None, [sm_], [sm_])

        def part2(it, h, hb, j):
            s_ = it % 2
            nkb = L_CTX // 128 + j + 1
            nk = nkb * 128
            S, Pb, PT_, sm_ = S_sb[s_], P_bf[s_], PT[s_], sm[s_]
            b.act(Pb[:, 0:nk], S[:, 0:nk], AF.Exp, [S, sm_], [Pb, sm_], bias=sm_[:, 1:2], scale=1.0, accum=sm_[:, 2:3])
            P.add("dve", lambda e, sm_=sm_: e.reciprocal(out=sm_[:, 3:4], in_=sm_[:, 2:3]), [sm_.r], [sm_.r])
            for kb0 in range(0, nkb, 8):
                nb = min(8, nkb - kb0)
                pt = nxt(pb, "pb")
                for q in range(nb):
                    kb = kb0 + q
                    b.tr(pt[:, q * 128:(q + 1) * 128], Pb[:, kb * 128:(kb + 1) * 128], ident[:], [Pb, ident], [pt])
                b.copy(ev_eng(), PT_[:, kb0:kb0 + nb, :].rearrange("p k q -> p (k q)"), pt[:, 0:nb * 128], [pt], [PT_])
            pov = nxt(po, "po")
            for kb in range(nkb):
                b.mm(pov[:, 0:64], PT_[:, kb, :], Vh[hb][:, kb, :], kb == 0, kb == nkb - 1, [PT_, Vh[hb]], [pov])
            ot = o_t[it % 4]
            b.ts("dve", ot[:], pov[:, 0:64], sm_[:, 3:4], None, ALU.mult, None, [pov, sm_], [ot])
            b.dma(g["o_scr"].t[j * 128:(j + 1) * 128, h * 64:(h + 1) * 64], ot[:], [ot], [], key=f"ot{it % 4}")

        for i, (h, hb, j) in enumerate(work):
            if j == 0:
                head_prep(h)
            part1(i, h, hb, j)
            if i > 0:
                part2(i - 1, *work[i - 1])
        part2(len(work) - 1, *work[-1])
    P.barrier()


def phase_G(nc, cx, b, P, g):
    ident = g["ident"]
    NB = 16
    NSLOT = 8
    NSTEP = 128 // NSLOT
    with contextlib.ExitStack() as st:
        stg = [cx.sb(st, [128, 1024], F32, "stgG") for _ in range(2)]
        wuv = load_cast_w(cx, b, st, g["mla_w_uv"].t, 2, 1024, "wuvG", stg)
        qlT = cx.sb(st, [128, 2, NB, 128], BF16, "qlT")
        qrT = cx.sb(st, [32, NB, 128], BF16, "qrT")
        for k in range(2):
            b.dma(qlT[:, k], g["qlT_scr"].t[k], [], [qlT], key="qlT")
        b.dma(qrT[:], g["qrT_scr"].t[:, :, :], [], [qrT], key="qrT")
        ckvsT = cx.sb(st, [128, 2, 128], BF16, "ckvsT")
        krsT = cx.sb(st, [32, 128], BF16, "krsT")
        ckvs = cx.sb(st, [128, 256], BF16, "ckvs")
        for k in range(2):
            b.dma(ckvsT[:, k, :], g["ckvsT_scr"].t[k], [], [ckvsT], key="ckvsT")
        b.dma(krsT[:], g["krsT_scr"].t[:, :], [], [krsT], key="krsT")
        b.dma(ckvs[:], g["ckvs_scr"].t[:, :], [], [ckvs], key="ckvsG")
        ptab = cx.sb(st, [128, NB], I32, "ptab")
        P.add("sp", lambda e: e.dma_start(out=ptab[:], in_=g["pt_core"].t.rearrange("b n -> n b"), allow_slow_non_contiguous=True), [], [ptab.r], dma=True, key="ptab")
        G_ = [cx.sb(st, [128, NSLOT, 256], F32, "Gf") for _ in range(2)]
        Gr = [cx.sb(st, [128, NSLOT, 32], F32, "Grf") for _ in range(2)]
        Gb = [cx.sb(st, [128, NSLOT, 256], BF16, "Gb") for _ in range(2)]
        Gbk = [cx.sb(st, [128, NSLOT, 32], BF16, "Gbk") for _ in range(2)]
        KTg = [cx.sb(st, [128, 2, NSLOT * 128], BF16, "KTg") for _ in range(2)]
        KrT = [cx.sb(st, [32, NSLOT * 128], BF16, "KrT") for _ in range(2)]
        Pb = [cx.sb(st, [128, NSLOT * 128], BF16, "PbG") for _ in range(2)]
        PTg = [cx.sb(st, [128, NSLOT, 128], BF16, "PTg") for _ in range(2)]
        Snew = cx.sb(st, [128, 128], F32, "Snew")
        msk = cx.sb(st, [128, NB, 128], F32, "mskG")
        b.dma(msk[:], g["smp_mask"].t.rearrange("b r k -> r b k"), [], [msk], key="msk")
        acc = cx.sb(st, [128, 256], F32, "acc")
        sm = [cx.sb(st, [128, 12], F32, "smG") for _ in range(2)]
        ol_bf = cx.sb(st, [128, 256], BF16, "ol_bf")
        olT = cx.sb(st, [128, 2, 128], BF16, "olT")
        oT_s = cx.sb(st, [64, NH, 128], BF16, "oT_s")
        o_smp = cx.sb(st, [128, 1024], BF16, "o_smp")
        pf = [cx.ps(st, [128, 512], F32, "pfG") for _ in range(1)]
        pb = [cx.ps(st, [128, 1024], BF16, "pbG") for _ in range(2)]
        po = [cx.ps(st, [128, 512], F32, "poG") for _ in range(1)]
        Pnew = cx.sb(st, [128, 128], BF16, "Pnew")
        PTnew = cx.sb(st, [128, 128], BF16, "PTnew")
        cnt = {"pf": 0, "pb": 0, "ev": 0}

        def nxt(lst, k):
            cnt[k] += 1
            return lst[cnt[k] % len(lst)]

        def ev_eng():
            cnt["ev"] += 1
            return "act" if cnt["ev"] % 2 else "dve"

        ckv_rows = g["cache_ckv"].t.rearrange("n (s c) -> (n s) c", c=NSLOT * 256)
        kr_rows = g["cache_krope"].t.rearrange("n (s c) -> (n s) c", c=NSLOT * 32)
        ptf = cx.sb(st, [128, NB], F32, "ptf")
        stpf = cx.sb(st, [128, NSTEP], F32, "stpf")
        idx_f = cx.sb(st, [128, NB, NSTEP], F32, "idx_f")
        idx_all = cx.sb(st, [128, NB, NSTEP], I32, "idx_all")
        b.copy("dve", ptf[:], ptab[:], [ptab], [ptf])
        b.ts("dve", ptf[:], ptf[:], float(NSTEP), None, ALU.mult, None, [ptf], [ptf])
        for i in range(NSTEP):
            b.memset("pool", stpf[:, i:i + 1], float(i), [stpf])
        b.tt("dve", idx_f[:], ptf[:].unsqueeze(2).broadcast_to([128, NB, NSTEP]), stpf[:].unsqueeze(1).broadcast_to([128, NB, NSTEP]), ALU.add, [ptf, stpf], [idx_f])
        b.copy("dve", idx_all[:], idx_f[:], [idx_f], [idx_all])
        pS2 = [cx.ps(st, [128, 1024], F32, "pS2") for _ in range(2)]
        smx = [cx.sb(st, [128, 2], F32, "smx") for _ in range(2)]

        def init_sample(bi):
            sm_ = sm[bi % 2]
            pz = pf[0]
            for k in range(2):
                b.mm(pz[:, 0:128], qlT[:, k, bi, :], ckvsT[:, k, :], k == 0, False, [qlT, ckvsT], [pz])
            b.mm(pz[:, 0:128], qrT[:, bi, :], krsT[:, :], False, True, [qrT, krsT], [pz])
            b.tt("dve", Snew[:], pz[:, 0:128], msk[:, bi, :], ALU.add, [pz, msk], [Snew])
            b.reduce("dve", sm_[:, 0:1], Snew[:], ALU.max, [Snew], [sm_])
            b.ts("dve", sm_[:, 1:2], sm_[:, 0:1], -SCALE, None, ALU.mult, None, [sm_], [sm_])
            b.act(Pnew[:], Snew[:], AF.Exp, [Snew, sm_], [Pnew, sm_], bias=sm_[:, 1:2], scale=SCALE, accum=sm_[:, 2:3])
            pt = nxt(pb, "pb")
            b.tr(pt[:, 0:128], Pnew[:], ident[:], [Pnew, ident], [pt])
            b.copy("dve", PTnew[:], pt[:, 0:128], [pt], [PTnew])
            pov = po[0]
            b.mm(pov[:, 0:256], PTnew[:], ckvs[:, :], True, True, [PTnew, ckvs], [pov])
            b.copy("dve", acc[:], pov[:, 0:256], [pov], [acc])

        def stage_x(i, bi, stp):
            q_ = i % 2
            Gf, Grf, Gb_, KT_, KrT_ = G_[q_], Gr[q_], Gb[q_], KTg[q_], KrT[q_]
            P.add("pool", lambda e, Gf=Gf, stp=stp, bi=bi: e.indirect_dma_start(
                out=Gf[:].rearrange("p s c -> p (s c)"), out_offset=None, in_=ckv_rows,
                in_offset=bass.IndirectOffsetOnAxis(ap=idx_all[:, bi, stp:stp + 1], axis=0)), [idx_all.r], [Gf.r], dma=True, key=f"Gf{q_}")
            P.add("pool", lambda e, Grf=Grf, stp=stp, bi=bi: e.indirect_dma_start(
                out=Grf[:].rearrange("p s c -> p (s c)"), out_offset=None, in_=kr_rows,
                in_offset=bass.IndirectOffsetOnAxis(ap=idx_all[:, bi, stp:stp + 1], axis=0)), [idx_all.r], [Grf.r], dma=True, key=f"Grf{q_}")
            Gbk_ = Gbk[q_]
            hs = NSLOT // 2
            b.copy("dve", Gb_[:, 0:hs, :].rearrange("p s c -> p (s c)"), Gf[:, 0:hs, :].rearrange("p s c -> p (s c)"), [Gf], [Gb_])
            b.copy("act", Gb_[:, hs:NSLOT, :].rearrange("p s c -> p (s c)"), Gf[:, hs:NSLOT, :].rearrange("p s c -> p (s c)"), [Gf], [Gb_])
            b.copy("dve", Gbk_[:].rearrange("p s c -> p (s c)"), Grf[:].rearrange("p s c -> p (s c)"), [Grf], [Gbk_])
            for k in range(2):
                pt = nxt(pb, "pb")
                for s in range(NSLOT):
                    b.tr(pt[:, s * 128:(s + 1) * 128], Gb_[:, s, k * 128:(k + 1) * 128], ident[:], [Gb_, ident], [pt])
                b.copy("act" if k else "dve", KT_[:, k, :], pt[:, :], [pt], [KT_])
            pt = nxt(pb, "pb")
            for s in range(NSLOT):
                b.tr(pt[0:32, s * 128:(s + 1) * 128], Gbk_[:, s, :], ident[:], [Gbk_, ident], [pt])
            b.copy("dve", KrT_[:, :], pt[0:32, :], [pt], [KrT_])
            pz = pS2[q_]
            for hf in range(2):
                sl = slice(hf * 512, (hf + 1) * 512)
                for k in range(2):
                    b.mm(pz[:, sl], qlT[:, k, bi, :], KT_[:, k, sl], k == 0, False, [qlT, KT_], [pz])
                b.mm(pz[:, sl], qrT[:, bi, :], KrT_[:, sl], False, True, [qrT, KrT_], [pz])
            b.reduce("dve", smx[q_][:, 0:1], pz[:, :], ALU.max, [pz], [smx[q_]])

        def stage_y(i, bi, stp):
            q_ = i % 2
            sm_ = sm[bi % 2]
            Gb_, Pb_, PT_, pz = Gb[q_], Pb[q_], PTg[q_], pS2[q_]
            b.tt("dve", sm_[:, 6:7], smx[q_][:, 0:1], sm_[:, 0:1], ALU.max, [sm_, smx[q_]], [sm_])
            b.ts("dve", sm_[:, 7:8], sm_[:, 6:7], -SCALE, None, ALU.mult, None, [sm_], [sm_])
            b.act(sm_[:, 8:9], sm_[:, 0:1], AF.Exp, [sm_], [sm_], bias=sm_[:, 7:8], scale=SCALE)
            b.act(Pb_[:, :], pz[:, :], AF.Exp, [pz, sm_], [Pb_, sm_], bias=sm_[:, 7:8], scale=SCALE, accum=sm_[:, 9:10])
            b.stt("dve", sm_[:, 2:3], sm_[:, 2:3], sm_[:, 8:9], sm_[:, 9:10], ALU.mult, ALU.add, [sm_], [sm_])
            b.copy("dve", sm_[:, 0:1], sm_[:, 6:7], [sm_], [sm_])
            pt = nxt(pb, "pb")
            for s in range(NSLOT):
                b.tr(pt[:, s * 128:(s + 1) * 128], Pb_[:, s * 128:(s + 1) * 128], ident[:], [Pb_, ident], [pt])
            b.copy("act", PT_[:].rearrange("p s q -> p (s q)"), pt[:, :], [pt], [PT_])
            pov = po[0]
            for s in range(NSLOT):
                b.mm(pov[:, 0:256], PT_[:, s, :], Gb_[:, s, 0:256], s == 0, s == NSLOT - 1, [PT_, Gb_], [pov])
            b.stt("dve", acc[:], acc[:], sm_[:, 8:9], pov[:, 0:256], ALU.mult, ALU.add, [acc, sm_, pov], [acc])

        def finalize(bi):
            sm_ = sm[bi % 2]
            if g.get("debug"):
                b.dma(g["dbg_sm"].t[bi], sm_[:], [sm_], [], key="dbgsm")
                b.dma(g["dbg_acc"].t[bi], acc[:], [acc], [], key="dbgacc")
            P.add("dve", lambda e, sm_=sm_: e.reciprocal(out=sm_[:, 3:4], in_=sm_[:, 2:3]), [sm_.r], [sm_.r])
            b.ts("dve", ol_bf[:], acc[:], sm_[:, 3:4], None, ALU.mult, None, [acc, sm_], [ol_bf])
            pt = nxt(pb, "pb")
            for k in range(2):
                b.tr(pt[:, k * 128:(k + 1) * 128], ol_bf[:, k * 128:(k + 1) * 128], ident[:], [ol_bf, ident], [pt])
            b.copy("dve", olT[:].rearrange("p k r -> p (k r)"), pt[:, 0:256], [pt], [olT])
            pz = pf[0]
            for h in range(NH):
                for k in range(2):
                    b.mm(pz[0:64, h * 8:(h + 1) * 8], wuv[:, k, h * 64:(h + 1) * 64], olT[:, k, h * 8:(h + 1) * 8], k == 0, k == 1, [wuv, olT], [pz])
            b.copy("dve", oT_s[:, :, bi * 8:(bi + 1) * 8], pz[0:64, 0:128].rearrange("p (h t) -> p h t", h=NH), [pz], [oT_s])

        items = [(bi, stp) for bi in range(NB) for stp in range(NSTEP)]

        def emit_y(i):
            bi, stp = items[i]
            if stp == 0:
                init_sample(bi)
            stage_y(i, bi, stp)
            if stp == NSTEP - 1:
                finalize(bi)

        for i, (bi, stp) in enumerate(items):
            stage_x(i, bi, stp)
            if i > 0:
                emit_y(i - 1)
        emit_y(len(items) - 1)
        for hg in range(2):
            pt = nxt(pb, "pb")
            for hh in range(8):
                b.tr(pt[:, hh * 64:(hh + 1) * 64], oT_s[:, hg * 8 + hh, :], ident[0:64, 0:64], [oT_s, ident], [pt])
            b.copy("dve", o_smp[:, hg * 512:(hg + 1) * 512], pt[:, 0:512], [pt], [o_smp])
        b.dma(g["o_scr"].t[L_OWN:L_OWN + 128, :], o_smp[:], [o_smp], [], key="o_smp")
    P.barrier()


def phase_B(nc, cx, b, P, g):
    ident = g["ident"]
    with contextlib.ExitStack() as st:
        stg = [cx.sb(st, [128, 1024], F32, "stgB") for _ in range(2)]
        wglu = load_cast_w(cx, b, st, g["ssm_w_glu"].t, 8, 1024, "wglu", stg)
        wbs = load_cast_w(cx, b, st, g["w_br_ssm"].t, 8, 1024, "wbs", stg)
        wba = load_cast_w(cx, b, st, g["w_br_attn"].t, 8, 1024, "wba", stg)
        wo = load_cast_w(cx, b, st, g["w_out"].t, 8, 1024, "wo", stg)
        bglu = cx.sb(st, [128, D], F32, "bglu")
        gfin = cx.sb(st, [128, D], F32, "gfin")
        b.dma(bglu[:], bcast_rows(g["ssm_b_glu"].t, D), [], [bglu], key="bglu")
        b.dma(gfin[:], bcast_rows(g["norm_final"].t, D), [], [gfin], key="gfin")
        NB = 2
        ys = [cx.sb(st, [128, D], BF16, "ysB") for _ in range(NB)]
        gt = [cx.sb(st, [128, 4096], BF16, "gtB") for _ in range(NB)]
        ot = [cx.sb(st, [128, D], BF16, "otB") for _ in range(NB)]
        xt = [cx.sb(st, [128, D], F32, "xtB") for _ in range(NB)]
        t1 = cx.sb(st, [128, D], F32, "t1B")
        zg = cx.sb(st, [128, D], F32, "zgB")
        ab = cx.sb(st, [128, D], BF16, "abB")
        aT = [cx.sb(st, [128, 8, 128], BF16, "aTB") for _ in range(2)]
        mg = cx.sb(st, [128, D], F32, "mgB")
        hh = cx.sb(st, [128, D], F32, "hhB")
        yo = [cx.sb(st, [128, D], F32, "yoB") for _ in range(NB)]
        stt_ = [cx.sb(st, [128, 4], F32, "stB") for _ in range(NB)]
        junk = cx.sb(st, [128, D], BF16, "junkB")
        pz = [cx.ps(st, [128, 512], F32, "pzB") for _ in range(4)]
        pT = [cx.ps(st, [128, 1024], BF16, "pTB") for _ in range(2)]
        cnt = {"pz": 0, "pT": 0, "aT": 0}

        def nxt(lst, k):
            cnt[k] += 1
            return lst[cnt[k] % len(lst)]

        def transp(src):
            pt = nxt(pT, "pT")
            for k in range(8):
                b.tr(pt[:, k * 128:(k + 1) * 128], src[:, k * 128:(k + 1) * 128], ident[:], [src, ident], [pt])
            a = nxt(aT, "aT")
            b.copy("act", a[:].rearrange("p k t -> p (k t)"), pt[:], [pt], [a])
            return a

        def linear(a, w, cg):
            p_ = nxt(pz, "pz")
            for k in range(8):
                b.mm(p_[:, :], a[:, k, :], w[:, k, cg * 512:(cg + 1) * 512], k == 0, k == 7, [a, w], [p_])
            return p_

        tiles = [("own", i) for i in range(L_OWN // 128)] + [("smp", 0)]
        for ti, (kind, i) in enumerate(tiles):
            s_ = ti % NB
            r0 = i * 128 if kind == "own" else L_OWN
            X = g["x_own"].t[i * 128:(i + 1) * 128, :] if kind == "own" else g["x_smp"].t[:, :]
            OUT = g["o_y_p"].t[i * 128:(i + 1) * 128, :] if kind == "own" else g["o_y_s"].t[:, :]
            ys_, gt_, ot_, xt_, st_ = ys[s_], gt[s_], ot[s_], xt[s_], stt_[s_]
            b.dma(ys_[:], g["ys_scr"].t[r0:r0 + 128, :], [], [ys_], key=f"ysB{s_}")
            b.dma(gt_[:], g["g_scr"].t[r0:r0 + 128, :], [], [gt_], key=f"gtB{s_}")
            b.dma(ot_[:], g["o_scr"].t[r0:r0 + 128, :], [], [ot_], key=f"otB{s_}")
            b.dma(xt_[:], X, [], [xt_], key=f"xtB{s_}")
            b.tt("pool", t1[:], ys_[:], ys_[:], ALU.mult, [ys_], [t1])
            b.ts("pool", t1[:], t1[:], 0.044715, 1.0, ALU.mult, ALU.add, [t1], [t1])
            b.tt("pool", t1[:], t1[:], ys_[:], ALU.mult, [t1, ys_], [t1])
            b.act(t1[:], t1[:], AF.Sigmoid, [t1], [t1], scale=1.5957691216057308)
            b.tt("dve", zg[:], t1[:], ys_[:], ALU.mult, [t1, ys_], [zg])
            b.copy("pool", ab[:], zg[:], [zg], [ab])
            a = transp(ab)
            for cg in range(2):
                p_ = linear(a, wglu, cg)
                sl = slice(cg * 512, (cg + 1) * 512)
                b.tt("dve", t1[:, sl], p_[:, :], bglu[:, sl], ALU.add, [p_, bglu], [t1])
                b.act(t1[:, sl], t1[:, sl], AF.Sigmoid, [t1], [t1])
                b.tt("dve", t1[:, sl], t1[:, sl], zg[:, sl], ALU.mult, [t1, zg], [t1])
            b.tt("dve", ab[:], t1[:], gt_[:, 0:1024], ALU.mult, [t1, gt_], [ab])
            a = transp(ab)
            for cg in range(2):
                p_ = linear(a, wbs, cg)
                sl = slice(cg * 512, (cg + 1) * 512)
                b.tt("dve", mg[:, sl], p_[:, :], gt_[:, 2048 + cg * 512:2048 + (cg + 1) * 512], ALU.mult, [p_, gt_], [mg])
            b.tt("pool", ab[:], ot_[:], gt_[:, 1024:2048], ALU.mult, [ot_, gt_], [ab])
            a = transp(ab)
            for cg in range(2):
                p_ = linear(a, wba, cg)
                sl = slice(cg * 512, (cg + 1) * 512)
                b.tt("dve", t1[:, sl], p_[:, :], gt_[:, 3072 + cg * 512:3072 + (cg + 1) * 512], ALU.mult, [p_, gt_], [t1])
            b.tt("pool", mg[:], mg[:], t1[:], ALU.add, [mg, t1], [mg])
            b.copy("pool", ab[:], mg[:], [mg], [ab])
            a = transp(ab)
            for cg in range(2):
                p_ = linear(a, wo, cg)
                sl = slice(cg * 512, (cg + 1) * 512)
                b.tt("dve", hh[:, sl], p_[:, :], xt_[:, sl], ALU.add, [p_, xt_], [hh])
            b.act(junk[:], hh[:], AF.Square, [hh], [st_], accum=st_[:, 0:1])
            b.rstd(st_, 0, 1, 2, 1.0 / D)
            b.stt("dve", yo[s_][:], hh[:], st_[:, 2:3], gfin[:], ALU.mult, ALU.mult, [hh, st_, gfin], [yo[s_]])
            b.dma(OUT, yo[s_][:], [yo[s_]], [], key=f"yoB{s_}")
```

```python
import contextlib
import math
import numpy as np
import concourse.bass as bass
import concourse.mybir as mybir
from concourse.bass_utils import run_bass_kernel_spmd

F32 = mybir.dt.float32
BF16 = mybir.dt.bfloat16
I32 = mybir.dt.int32
AF = mybir.ActivationFunctionType
ALU = mybir.AluOpType
AX = mybir.AxisListType

D = 1024
NCOL = 5792
C_US, C_GS, C_CQ, C_KV, C_KR, C_GA, C_MS, C_MA = 0, 1024, 2048, 2432, 2688, 2720, 3744, 4768
NH = 16
SCALE = 96 ** -0.5
EPS = 1e-6
NEG = -1e30
L_OWN = 2048
L_CTX = 2048
PAST = 16384
NPAGE = 128
STAGE = "all"


class Reg:
    __slots__ = ("name", "w", "r")

    def __init__(self, name=""):
        self.name = name
        self.w = None
        self.r = []


class Op:
    __slots__ = ("eng", "fn", "deps", "idx", "dma", "key", "sig", "cnt", "waits")

    def __init__(self, eng, fn, dma, key):
        self.eng, self.fn, self.dma, self.key = eng, fn, dma, key
        self.deps = []
        self.sig = False
        self.cnt = 0
        self.waits = []


class Prog:
    ENGS = ("pe", "act", "dve", "pool", "sp")

    def __init__(self, nc):
        self.nc = nc
        self.ops = {e: [] for e in self.ENGS}
        self.dma_keys = {}

    def add(self, eng, fn, reads=(), writes=(), dma=False, key=None):
        op = Op(eng, fn, dma, key)
        if dma:
            lst = self.dma_keys.setdefault(key, [])
            lst.append(op)
            op.cnt = 16 * len(lst)
            op.sig = True
        for r in reads:
            if r.w is not None:
                op.deps.append((r.w, "raw"))
        for w in writes:
            if w.w is not None:
                op.deps.append((w.w, "waw"))
            for rd in w.r:
                op.deps.append((rd, "war"))
        for r in reads:
            r.r.append(op)
        for w in writes:
            w.w = op
            w.r = []
        op.idx = len(self.ops[eng])
        self.ops[eng].append(op)
        return op

    def barrier(self):
        lasts = []
        for e in self.ENGS:
            comp = [o for o in self.ops[e] if not o.dma]
            if comp:
                lasts.append(comp[-1])
        for key, lst in self.dma_keys.items():
            if lst:
                lasts.append(lst[-1])
        for e in self.ENGS:
            op = Op(e, (lambda eh: eh.nop()), False, None)
            op.deps = [(d, "raw") for d in lasts]
            op.idx = len(self.ops[e])
            self.ops[e].append(op)

    def emit(self):
        nc = self.nc
        for e in self.ENGS:
            for op in self.ops[e]:
                need = {}
                for (d, kind) in op.deps:
                    if d is op:
                        continue
                    if (not d.dma) and (not op.dma) and d.eng == op.eng and kind != "raw":
                        continue
                    k = ("dma", d.key) if d.dma else ("eng", d.eng)
                    if k not in need or (d.dma and need[k].cnt < d.cnt) or ((not d.dma) and need[k].idx < d.idx):
                        need[k] = d
                op.deps = need
        for e in self.ENGS:
            for op in self.ops[e]:
                for k, d in op.deps.items():
                    if not d.dma:
                        d.sig = True
        for e in self.ENGS:
            c = 0
            for op in self.ops[e]:
                if not op.dma and op.sig:
                    c += 1
                    op.cnt = c
        sem_eng, sem_key = {}, {}
        stack = contextlib.ExitStack()
        for e in self.ENGS:
            sem_eng[e] = stack.enter_context(nc.semaphore("se_" + e))
        for key in self.dma_keys:
            sem_key[key] = stack.enter_context(nc.semaphore("sd_" + str(key)))
        for e in self.ENGS:
            seen = {}
            for op in self.ops[e]:
                for k, d in op.deps.items():
                    sem = sem_key[d.key] if d.dma else sem_eng[d.eng]
                    if seen.get(k, 0) >= d.cnt:
                        continue
                    seen[k] = d.cnt
                    op.waits.append((sem, d.cnt))
        nops = sum(len(self.ops[e]) for e in self.ENGS)
        nw = sum(len(op.waits) for e in self.ENGS for op in self.ops[e])
        print(f"[prog] ops={nops} waits={nw} dma_keys={len(self.dma_keys)}", flush=True)

        def run(e_name, eh):
            for op in self.ops[e_name]:
                for (sem, v) in op.waits:
                    eh.wait_ge(sem, v)
                ins = op.fn(eh)
                if op.sig:
                    if op.dma:
                        ins.then_inc(sem_key[op.key], 16)
                    else:
                        ins.then_inc(sem_eng[e_name], 1)
            if e_name == "sp":
                for key, lst in self.dma_keys.items():
                    if lst:
                        eh.wait_ge(sem_key[key], 16 * len(lst))

        with nc.Block() as block:
            @block.tensor
            def _(t):
                run("pe", t)

            @block.scalar
            def _(s):
                run("act", s)

            @block.vector
            def _(v):
                run("dve", v)

            @block.gpsimd
            def _(g):
                run("pool", g)

            @block.sync
            def _(sy):
                run("sp", sy)
        stack.close()


class T:
    __slots__ = ("t", "r")

    def __init__(self, t, name):
        self.t = t
        self.r = Reg(name)

    def __getitem__(self, k):
        return self.t[k]


class Ctx:
    def __init__(self, nc, P):
        self.nc, self.P = nc, P
        self.n = 0

    def sb(self, st, shape, dt, name=None):
        self.n += 1
        name = f"{name or 't'}_{self.n}"
        return T(st.enter_context(self.nc.sbuf_tensor(name, shape, dt)), name)

    def ps(self, st, shape, dt, name=None):
        self.n += 1
        name = f"{name or 'p'}_{self.n}"
        return T(st.enter_context(self.nc.psum_tensor(name, shape, dt)), name)

    def dram(self, name, shape, dt, kind="Internal"):
        return T(self.nc.dram_tensor(name, shape, dt, kind=kind).ap(), name)


def _regs(xs):
    return [x.r if isinstance(x, T) else x for x in xs]


class B:
    def __init__(self, cx):
        self.cx, self.P = cx, cx.P
        self.dq = 0

    def dma(self, out, in_, reads, writes, key, eng="sp"):
        self.P.add(eng, lambda e: e.dma_start(out=out, in_=in_), _regs(reads), _regs(writes), dma=True, key=key)

    def mm(self, out, lhsT, rhs, start, stop, reads, writes):
        self.P.add("pe", lambda e: e.matmul(out=out, lhsT=lhsT, rhs=rhs, start=start, stop=stop), _regs(reads), _regs(writes))

    def tr(self, out, in_, ident, reads, writes):
        self.P.add("pe", lambda e: e.transpose(out=out, in_=in_, identity=ident), _regs(reads), _regs(writes))

    def act(self, out, in_, func, reads, writes, bias=None, scale=None, accum=None, eng="act"):
        kw = {}
        if bias is not None:
            kw["bias"] = bias
        if scale is not None:
            kw["scale"] = scale
        if accum is not None:
            kw["accum_out"] = accum
        self.P.add("act", lambda e: e.activation(out=out, in_=in_, func=func, **kw), _regs(reads), _regs(writes))

    def copy(self, eng, out, in_, reads, writes):
        if eng == "act":
            self.P.add("act", lambda e: e.activation(out=out, in_=in_, func=AF.Copy), _regs(reads), _regs(writes))
        else:
            self.P.add(eng, lambda e: e.tensor_copy(out=out, in_=in_), _regs(reads), _regs(writes))

    def tt(self, eng, out, in0, in1, op, reads, writes):
        self.P.add(eng, lambda e: e.tensor_tensor(out=out, in0=in0, in1=in1, op=op), _regs(reads), _regs(writes))

    def ts(self, eng, out, in0, s1, s2, op0, op1, reads, writes):
        if op1 is None:
            self.P.add(eng, lambda e: e.tensor_scalar(out=out, in0=in0, scalar1=s1, scalar2=None, op0=op0), _regs(reads), _regs(writes))
        else:
            self.P.add(eng, lambda e: e.tensor_scalar(out=out, in0=in0, scalar1=s1, scalar2=s2, op0=op0, op1=op1), _regs(reads), _regs(writes))

    def stt(self, eng, out, in0, scalar, in1, op0, op1, reads, writes):
        self.P.add(eng, lambda e: e.scalar_tensor_tensor(out=out, in0=in0, scalar=scalar, in1=in1, op0=op0, op1=op1), _regs(reads), _regs(writes))

    def rstd(self, st_, c_in, c_tmp, c_out, inv_n):
        self.ts("dve", st_[:, c_tmp:c_tmp + 1], st_[:, c_in:c_in + 1], inv_n, EPS, ALU.mult, ALU.add, [st_], [st_])
        self.act(st_[:, c_tmp:c_tmp + 1], st_[:, c_tmp:c_tmp + 1], AF.Sqrt, [st_], [st_])
        self.P.add("dve", lambda e: e.reciprocal(out=st_[:, c_out:c_out + 1], in_=st_[:, c_tmp:c_tmp + 1]), [st_.r], [st_.r])

    def memset(self, eng, ap, val, writes):
        self.P.add(eng, lambda e: e.memset(ap, val), [], _regs(writes))

    def reduce(self, eng, out, in_, op, reads, writes):
        self.P.add(eng, lambda e: e.tensor_reduce(out=out, in_=in_, axis=AX.X, op=op), _regs(reads), _regs(writes))


def bcast_rows(ap1d, n, p=128):
    return ap1d.rearrange("(o n) -> o n", o=1).broadcast_to([p, n])


def build(debug=False, phases=("A",)):
    nc = bass.Bass("TRN2", target_bir_lowering=False)
    P = Prog(nc)
    cx = Ctx(nc, P)
    b = B(cx)
    kio = "ExternalOutput" if debug else "Internal"

    def din(name, shape, dt=F32):
        return T(nc.dram_tensor(name, shape, dt, kind="ExternalInput").ap(), name)

    def dout(name, shape, dt=F32):
        return T(nc.dram_tensor(name, shape, dt, kind="ExternalOutput").ap(), name)

    x_ctx = din("x_ctx", [L_CTX, D])
    x_own = din("x_own", [L_OWN, D])
    x_smp = din("x_smp", [128, D])
    cs_ctx = din("cs_ctx", [L_CTX, 32])
    cs_own = din("cs_own", [L_OWN, 32])
    cs_smp = din("cs_smp", [128, 32])
    norm_in = din("norm_in", [D])
    w_in = din("w_in", [D, NCOL])
    mla_q_norm = din("mla_q_norm", [384])
    mla_kv_norm = din("mla_kv_norm", [256])
    mla_w_uq = din("mla_w_uq", [384, 1536])
    mla_w_uk = din("mla_w_uk", [256, 1024])
    ssm_lambda_re = din("ssm_lambda_re", [64, 64])
    ssm_lambda_im = din("ssm_lambda_im", [64, 64])
    ssm_log_dt = din("ssm_log_dt", [64])
    ssm_b_re = din("ssm_b_re", [64, 64, 16])
    ssm_b_im = din("ssm_b_im", [64, 64, 16])
    ssm_c_re = din("ssm_c_re", [64, 16, 64])
    ssm_c_im = din("ssm_c_im", [64, 16, 64])
    ssm_d = din("ssm_d", [D])
    state_s = din("state_s", [16, 64, 64, 2])
    mla_w_uv = din("mla_w_uv", [256, 1024])
    ctx_bias = din("ctx_bias", [128, 1])
    if "G" in phases:
        smp_mask = din("smp_mask", [16, 128, 128])
        pt_core = din("pt_core", [16, 128], I32)
        cache_ckv = din("cache_ckv", [20480, 128 * 256])
        cache_krope = din("cache_krope", [20480, 128 * 32])
    if "B" in phases:
        ssm_w_glu = din("ssm_w_glu", [D, D])
        ssm_b_glu = din("ssm_b_glu", [D])
        w_br_ssm = din("w_br_ssm", [D, D])
        w_br_attn = din("w_br_attn", [D, D])
        w_out = din("w_out", [D, D])
        norm_final = din("norm_final", [D])
        o_y_p = dout("o_y_p", [L_OWN, D])
        o_y_s = dout("o_y_s", [128, D])
    o_ssm_p = dout("o_ssm_p", [64, 64, 2])
    o_ssm_s = dout("o_ssm_s", [16, 64, 64, 2])
    o_ckv_p = dout("o_ckv_p", [L_OWN, 256])
    o_kr_p = dout("o_kr_p", [L_OWN, 32])
    o_ckv_s = dout("o_ckv_s", [128, 256])
    o_kr_s = dout("o_kr_s", [128, 32])
    NTOK = L_CTX + L_OWN + 128
    u_scr = cx.dram("u_scr", [NTOK, D], BF16, kio)
    g_scr = cx.dram("g_scr", [L_OWN + 128, 4096], BF16, kio)
    o_scr = cx.dram("o_scr", [L_OWN + 128, D], BF16, kio)
    ys_scr = cx.dram("ys_scr", [L_OWN + 128, D], BF16, kio)
    qT_scr = cx.dram("qT_scr", [NH, 96, L_OWN], BF16, kio)
    ckvT_scr = cx.dram("ckvT_scr", [2, 128, L_CTX + L_OWN], BF16, kio)
    krT_scr = cx.dram("krT_scr", [32, L_CTX + L_OWN], BF16, kio)
    qlT_scr = cx.dram("qlT_scr", [2, 128, 16, 128], BF16, kio)
    qrT_scr = cx.dram("qrT_scr", [32, 16, 128], BF16, kio)
    ckvs_scr = cx.dram("ckvs_scr", [128, 256], BF16, kio)
    ckvsT_scr = cx.dram("ckvsT_scr", [2, 128, 128], BF16, kio)
    krsT_scr = cx.dram("krsT_scr", [32, 128], BF16, kio)

    if debug:
        dbg_sm = dout("dbg_sm", [16, 128, 12])
        dbg_acc = dout("dbg_acc", [16, 128, 256])
        dbg_L = dout("dbg_L", [128, 2304])
        dbg_T = dout("dbg_T", [128, 8192], BF16)
        dbg_R = dout("dbg_R", [128, 8192], BF16)
        dbg_O = dout("dbg_O", [128, 8192], BF16)
    with contextlib.ExitStack() as top:
        ident = cx.sb(top, [128, 128], BF16, "ident")
        identf = cx.sb(top, [128, 128], F32, "identf")
        b.memset("pool", identf[:], 0.0, [identf])
        P.add("pool", lambda e: e.affine_select(out=identf[:], in_=identf[:], pattern=[[-1, 128]], compare_op=ALU.not_equal,
                                                fill=1.0, base=0, channel_multiplier=1), [identf.r], [identf.r])
        b.copy("dve", ident[:], identf[:], [identf], [ident])

        if "A" in phases:
            phase_A(nc, cx, b, P, locals())
        if "S" in phases:
            phase_S(nc, cx, b, P, locals())
        if "P" in phases:
            phase_P(nc, cx, b, P, locals())
        if "G" in phases:
            phase_G(nc, cx, b, P, locals())
        if "B" in phases:
            phase_B(nc, cx, b, P, locals())
    P.emit()
    return nc


def phase_A(nc, cx, b, P, g):
    ident = g["ident"]
    x_ctx, x_own, x_smp = g["x_ctx"], g["x_own"], g["x_smp"]
    cs_ctx, cs_own, cs_smp = g["cs_ctx"], g["cs_own"], g["cs_smp"]
    w_in, norm_in = g["w_in"], g["norm_in"]
    with contextlib.ExitStack() as st:
        w_sb = cx.sb(st, [128, 8, NCOL], BF16, "w_in_sb")
        wuq_sb = cx.sb(st, [128, 3, 1536], BF16, "wuq")
        wukT_sb = cx.sb(st, [64, NH, 256], BF16, "wukT")
        gin = cx.sb(st, [128, D], F32, "gin")
        gq = cx.sb(st, [128, 384], F32, "gq")
        gkv = cx.sb(st, [128, 256], F32, "gkv")
        pz = [cx.ps(st, [128, 512], F32, "pz") for _ in range(4)]
        pT = [cx.ps(st, [128, 1024], BF16, "pT") for _ in range(2)]
        pq = [cx.ps(st, [128, 512], F32, "pq") for _ in range(2)]
        cnt = {"pz": 0, "pT": 0, "pq": 0, "cast": 0}

        def nxt(lst, k):
            cnt[k] += 1
            return lst[cnt[k] % len(lst)]

        cast_engs = ["pool", "dve", "act"]

        def cast(out, in_, reads, writes):
            cnt["cast"] += 1
            b.copy(cast_engs[cnt["cast"] % 3], out, in_, reads, writes)

        st0 = contextlib.ExitStack()
        stg = [cx.sb(st0, [128, 2896], F32, "stg") for _ in range(2)]
        w_v = w_in.t.rearrange("(k p) c -> k p c", p=128)
        for k in range(8):
            for h in range(2):
                s = stg[h]
                b.dma(s[:, :], w_v[k][:, h * 2896:(h + 1) * 2896], [], [s], key=f"stg{h}")
                cast(w_sb[:, k, h * 2896:(h + 1) * 2896], s[:, :], [s], [w_sb])
        wq_v = g["mla_w_uq"].t.rearrange("(k p) c -> k p c", p=128)
        for k in range(3):
            s = stg[k % 2]
            b.dma(s[:, 0:1536], wq_v[k], [], [s], key=f"stg{k % 2}")
            cast(wuq_sb[:, k, :], s[:, 0:1536], [s], [wuq_sb])
        wk_v = g["mla_w_uk"].t.rearrange("(k p) c -> k p c", p=128)
        wk_bf = cx.sb(st0, [128, 2, 1024], BF16, "wk_bf")
        for k in range(2):
            s = stg[k % 2]
            b.dma(s[:, 0:1024], wk_v[k], [], [s], key=f"stg{k % 2}")
            cast(wk_bf[:, k, :], s[:, 0:1024], [s], [wk_bf])
        for k in range(2):
            for hg in range(2):
                pt = nxt(pT, "pT")
                for hh in range(8):
                    h = hg * 8 + hh
                    b.tr(pt[0:64, hh * 128:(hh + 1) * 128], wk_bf[:, k, h * 64:(h + 1) * 64], ident[:], [wk_bf, ident], [pt])
                b.copy("dve", wukT_sb[:, hg * 8:(hg + 1) * 8, k * 128:(k + 1) * 128],
                       pt[0:64, :].rearrange("p (h c) -> p h c", h=8), [pt], [wukT_sb])
        st0.close()
        P.barrier()
        b.dma(gin[:], bcast_rows(norm_in.t, D), [], [gin], key="gin")
        b.dma(gq[:], bcast_rows(g["mla_q_norm"].t, 384), [], [gq], key="gq")
        b.dma(gkv[:], bcast_rows(g["mla_kv_norm"].t, 256), [], [gkv], key="gkv")

        NB = 2
        xt = [cx.sb(st, [128, D], F32, "xt") for _ in range(NB)]
        cs = [cx.sb(st, [128, 32], F32, "cs") for _ in range(NB)]
        junk1 = cx.sb(st, [128, D], BF16, "junk")
        junk = [junk1, junk1]
        stat = [cx.sb(st, [128, 8], F32, "stat") for _ in range(NB)]
        xn = [cx.sb(st, [128, D], BF16, "xn") for _ in range(NB)]
        xnT = [cx.sb(st, [128, 8, 128], BF16, "xnT") for _ in range(NB)]
        u_t = [cx.sb(st, [128, D], BF16, "u_t") for _ in range(NB)]
        g_t = [cx.sb(st, [128, 2048], BF16, "g_t") for _ in range(NB)]
        cqn = [cx.sb(st, [128, 384], BF16, "cqn") for _ in range(NB)]
        cqnT = [cx.sb(st, [128, 3, 128], BF16, "cqnT") for _ in range(NB)]
        q_t1 = cx.sb(st, [128, NH, 96], F32, "q_t")
        q_t = [q_t1, q_t1]
        qb_t = [cx.sb(st, [128, NH, 96], BF16, "qb_t") for _ in range(NB)]
        qr_tmp1 = cx.sb(st, [128, 4, NH, 16], F32, "qr_tmp")
        qr_tmp = [qr_tmp1, qr_tmp1]
        qT_t = [cx.sb(st, [96, NH, 128], BF16, "qT_t") for _ in range(NB)]
        ckv_t = [cx.sb(st, [128, 256], F32, "ckv_t") for _ in range(NB)]
        ckvb_t = [cx.sb(st, [128, 288], BF16, "ckvb_t") for _ in range(NB)]
        kr_t = [cx.sb(st, [128, 32], F32, "kr_t") for _ in range(NB)]
        kr_tmp = [cx.sb(st, [128, 64], F32, "kr_tmp") for _ in range(NB)]
        kvT_t = [cx.sb(st, [128, 3, 128], BF16, "kvT_t") for _ in range(NB)]
        qlT_t = cx.sb(st, [128, 2, 16, NH, 8], BF16, "qlT_t")

        tiles = [("ctx", i) for i in range(L_CTX // 128)] + [("own", i) for i in range(L_OWN // 128)] + [("smp", 0)]
        for ti, (kind, i) in enumerate(tiles):
            s_ = ti % NB
            X, CS = {"ctx": (x_ctx, cs_ctx), "own": (x_own, cs_own), "smp": (x_smp, cs_smp)}[kind]
            tok0 = {"ctx": 0, "own": L_CTX, "smp": L_CTX + L_OWN}[kind] + i * 128
            full = kind != "ctx"
            xt_, st_, xn_, xnT_ = xt[s_], stat[s_], xn[s_], xnT[s_]
            b.dma(xt_[:], X.t[i * 128:(i + 1) * 128, :], [], [xt_], key=f"xt{s_}")
            b.dma(cs[s_][:], CS.t[i * 128:(i + 1) * 128, :], [], [cs[s_]], key=f"cs{s_}")
            b.act(junk[s_][:], xt_[:], AF.Square, [xt_], [st_], accum=st_[:, 0:1])
            b.rstd(st_, 0, 1, 2, 1.0 / D)
            b.stt("dve", xn_[:], xt_[:], st_[:, 2:3], gin[:], ALU.mult, ALU.mult, [xt_, st_, gin], [xn_])
            pt = nxt(pT, "pT")
            for k in range(8):
                b.tr(pt[:, k * 128:(k + 1) * 128], xn_[:, k * 128:(k + 1) * 128], ident[:], [xn_, ident], [pt])
            b.copy("act", xnT_[:].rearrange("p k t -> p (k t)"), pt[:], [pt], [xnT_])

            def proj(c0, n):
                pzt = nxt(pz, "pz")
                for k in range(8):
                    b.mm(pzt[:, 0:n], xnT_[:, k, :], w_sb[:, k, c0:c0 + n], k == 0, k == 7, [xnT_, w_sb], [pzt])
                return pzt

            for hf in range(2):
                pzt = proj(C_US + hf * 512, 512)
                b.copy("dve" if hf else "act", u_t[s_][:, hf * 512:(hf + 1) * 512], pzt[:, :], [pzt], [u_t[s_]])
            b.dma(g["u_scr"].t[tok0:tok0 + 128, :], u_t[s_][:], [u_t[s_]], [], key=f"uo{s_}")
            pzt = proj(C_KV, 288)
            b.act(junk[s_][:, 0:256], pzt[:, 0:256], AF.Square, [pzt], [st_], accum=st_[:, 3:4])
            b.rstd(st_, 3, 4, 5, 1.0 / 256)
            b.stt("dve", ckv_t[s_][:], pzt[:, 0:256], st_[:, 5:6], gkv[:], ALU.mult, ALU.mult, [pzt, st_, gkv], [ckv_t[s_]])
            kt = kr_tmp[s_]
            c_, s2_ = cs[s_][:, 0:16], cs[s_][:, 16:32]
            b.tt("dve", kt[:, 0:16], pzt[:, 256:272], c_, ALU.mult, [pzt, cs[s_]], [kt])
            b.tt("dve", kt[:, 16:32], pzt[:, 272:288], s2_, ALU.mult, [pzt, cs[s_]], [kt])
            b.tt("dve", kt[:, 32:48], pzt[:, 256:272], s2_, ALU.mult, [pzt, cs[s_]], [kt])
            b.tt("dve", kt[:, 48:64], pzt[:, 272:288], c_, ALU.mult, [pzt, cs[s_]], [kt])
            b.tt("dve", kr_t[s_][:, 0:16], kt[:, 0:16], kt[:, 16:32], ALU.subtract, [kt], [kr_t[s_]])
            b.tt("dve", kr_t[s_][:, 16:32], kt[:, 32:48], kt[:, 48:64], ALU.add, [kt], [kr_t[s_]])
            if kind == "own":
                b.dma(g["o_ckv_p"].t[i * 128:(i + 1) * 128, :], ckv_t[s_][:], [ckv_t[s_]], [], key=f"ckvo{s_}")
                b.dma(g["o_kr_p"].t[i * 128:(i + 1) * 128, :], kr_t[s_][:], [kr_t[s_]], [], key=f"kro{s_}")
            if kind == "smp":
                b.dma(g["o_ckv_s"].t[:, :], ckv_t[s_][:], [ckv_t[s_]], [], key=f"ckvo{s_}")
                b.dma(g["o_kr_s"].t[:, :], kr_t[s_][:], [kr_t[s_]], [], key=f"kro{s_}")
            cb = ckvb_t[s_]
            b.copy("pool", cb[:, 0:256], ckv_t[s_][:], [ckv_t[s_]], [cb])
            b.copy("pool", cb[:, 256:288], kr_t[s_][:], [kr_t[s_]], [cb])
            pt = nxt(pT, "pT")
            b.tr(pt[:, 0:128], cb[:, 0:128], ident[:], [cb, ident], [pt])
            b.tr(pt[:, 128:256], cb[:, 128:256], ident[:], [cb, ident], [pt])
            b.tr(pt[0:32, 256:384], cb[:, 256:288], ident[:], [cb, ident], [pt])
            kvT = kvT_t[s_]
            b.copy("act", kvT[:, 0:2, :].rearrange("p k t -> p (k t)"), pt[:, 0:256], [pt], [kvT])
            b.copy("act", kvT[0:32, 2, :], pt[0:32, 256:384], [pt], [kvT])
            if kind == "smp":
                b.dma(g["ckvs_scr"].t[:, :], cb[:, 0:256], [cb], [], key="ckvs")
                for k in range(2):
                    b.dma(g["ckvsT_scr"].t[k], kvT[:, k, :], [kvT], [], key=f"kvTo{s_}")
                b.dma(g["krsT_scr"].t[:, :], kvT[0:32, 2, :], [kvT], [], key=f"kvTo{s_}")
            else:
                p0 = tok0
                for k in range(2):
                    b.dma(g["ckvT_scr"].t[k, :, p0:p0 + 128], kvT[:, k, :], [kvT], [], key=f"kvTo{s_}")
                b.dma(g["krT_scr"].t[:, p0:p0 + 128], kvT[0:32, 2, :], [kvT], [], key=f"kvTo{s_}")
            if not full:
                continue
            gr0 = i * 128 if kind == "own" else L_OWN
            glist = ((C_GS, 0, AF.Silu), (C_GS + 512, 512, AF.Silu), (C_GA, 1024, AF.Silu), (C_GA + 512, 1536, AF.Silu),
                     (C_MS, 2048, AF.Sigmoid), (C_MS + 512, 2560, AF.Sigmoid), (C_MA, 3072, AF.Sigmoid), (C_MA + 512, 3584, AF.Sigmoid))
            for hf in range(2):
                gt = g_t[hf]
                for (c0, o0, fn) in glist[hf * 4:(hf + 1) * 4]:
                    pzt = proj(c0, 512)
                    b.act(gt[:, o0 - hf * 2048:o0 - hf * 2048 + 512], pzt[:, :], fn, [pzt], [gt])
                b.dma(g["g_scr"].t[gr0:gr0 + 128, hf * 2048:(hf + 1) * 2048], gt[:], [gt], [], key=f"go{hf}")
            pzt = proj(C_CQ, 384)
            b.act(junk[s_][:, 0:384], pzt[:, 0:384], AF.Square, [pzt], [st_], accum=st_[:, 6:7])
            b.rstd(st_, 6, 7, 6, 1.0 / 384)
            b.stt("dve", cqn[s_][:], pzt[:, 0:384], st_[:, 6:7], gq[:], ALU.mult, ALU.mult, [pzt, st_, gq], [cqn[s_]])
            pt = nxt(pT, "pT")
            for k in range(3):
                b.tr(pt[:, k * 128:(k + 1) * 128], cqn[s_][:, k * 128:(k + 1) * 128], ident[:], [cqn[s_], ident], [pt])
            b.copy("act", cqnT[s_][:].rearrange("p k t -> p (k t)"), pt[:, 0:384], [pt], [cqnT[s_]])
            qv = q_t[s_][:].rearrange("p h d -> p (h d)")
            for cg in range(3):
                pqt = nxt(pq, "pq")
                for k in range(3):
                    b.mm(pqt[:, :], cqnT[s_][:, k, :], wuq_sb[:, k, cg * 512:(cg + 1) * 512], k == 0, k == 2, [cqnT[s_], wuq_sb], [pqt])
                b.copy("act" if cg % 2 else "dve", qv[:, cg * 512:(cg + 1) * 512], pqt[:, :], [pqt], [q_t[s_]])
            q3 = q_t[s_]
            tm = qr_tmp[s_]
            cb3 = cs[s_][:, 0:16].unsqueeze(1).broadcast_to([128, NH, 16])
            sb3 = cs[s_][:, 16:32].unsqueeze(1).broadcast_to([128, NH, 16])
            b.tt("pool", tm[:, 0], q3[:, :, 64:80], cb3, ALU.mult, [q3, cs[s_]], [tm])
            b.tt("pool", tm[:, 1], q3[:, :, 80:96], sb3, ALU.mult, [q3, cs[s_]], [tm])
            b.tt("pool", tm[:, 2], q3[:, :, 64:80], sb3, ALU.mult, [q3, cs[s_]], [tm])
            b.tt("pool", tm[:, 3], q3[:, :, 80:96], cb3, ALU.mult, [q3, cs[s_]], [tm])
            qb = qb_t[s_]
            b.copy("pool", qb[:, :, 0:64], q3[:, :, 0:64], [q3], [qb])
            b.tt("dve", qb[:, :, 64:80], tm[:, 0], tm[:, 1], ALU.subtract, [tm], [qb])
            b.tt("dve", qb[:, :, 80:96], tm[:, 2], tm[:, 3], ALU.add, [tm], [qb])
            qT = qT_t[s_]
            for hg in range(2):
                pt = nxt(pT, "pT")
                for hh in range(8):
                    b.tr(pt[0:96, hh * 128:(hh + 1) * 128], qb[:, hg * 8 + hh, :], ident[:], [qb, ident], [pt])
                b.copy("act" if hg else "dve", qT[:, hg * 8:(hg + 1) * 8, :].rearrange("p h t -> p (h t)"), pt[0:96, :], [pt], [qT])
            if kind == "own":
                b.dma(g["qT_scr"].t[:, :, i * 128:(i + 1) * 128].rearrange("h r t -> r h t"), qT[:, :, :], [qT], [], key=f"qTo{s_}")
            else:
                for hg in range(4):
                    for k in range(2):
                        pqt = nxt(pq, "pq")
                        for hh in range(4):
                            h = hg * 4 + hh
                            b.mm(pqt[:, hh * 128:(hh + 1) * 128], wukT_sb[:, h, k * 128:(k + 1) * 128], qT[0:64, h, :], True, True, [wukT_sb, qT], [pqt])
                        b.copy("dve", qlT_t[:, k, :, hg * 4:(hg + 1) * 4, :].rearrange("p b h t -> p h b t"),
                               pqt[:, :].rearrange("p (h b t) -> p h b t", h=4, b=16), [pqt], [qlT_t])
                for k in range(2):
                    b.dma(g["qlT_scr"].t[k], qlT_t[:, k].rearrange("p b h t -> p b (h t)"), [qlT_t], [], key="qlo")
                for h in range(NH):
                    b.dma(g["qrT_scr"].t[:, :, h * 8:(h + 1) * 8], qT[64:96, h, :].rearrange("p (b t) -> p b t", b=16), [qT], [], key="qro")
    P.barrier()


def rope_tables(pos):
    half = 16
    inv = (10000.0 ** (-np.arange(half, dtype=np.float32) * np.float32(2.0 / 32))).astype(np.float32)
    ang = pos.astype(np.float32)[:, None] * inv[None, :]
    return np.concatenate([np.cos(ang), np.sin(ang)], axis=1).astype(np.float32)


_NC_CACHE = {}


def make_in_maps(inp, phases=("A", "S", "P", "G", "B")):
    maps = []
    r_ = np.arange(128)
    hq, tq = r_ // 8, r_ % 8
    bk, tk = r_ // 8, r_ % 8
    smp_mask = np.where((bk[None, None, :] == np.arange(16)[:, None, None]) & (tk[None, None, :] <= tq[None, :, None]), 0.0, NEG).astype(np.float32)
    cck = np.asarray(inp["cache_ckv"], np.float32).reshape(20480, 128 * 256) if "G" in phases else None
    ckr = np.asarray(inp["cache_krope"], np.float32).reshape(20480, 128 * 32) if "G" in phases else None
    xp = np.asarray(inp["x_prompt"], np.float32)
    xs = np.asarray(inp["x_sample"], np.float32)
    for c in range(8):
        bi, h = c // 2, c % 2
        m = {}
        m["x_own"] = np.ascontiguousarray(xp[bi, h * L_OWN:(h + 1) * L_OWN])
        m["x_ctx"] = np.ascontiguousarray(xp[bi, 0:L_CTX]) if h == 1 else np.zeros((L_CTX, D), np.float32)
        m["x_smp"] = np.ascontiguousarray(xs[16 * c:16 * c + 16].reshape(128, D))
        m["cs_ctx"] = rope_tables(np.arange(L_CTX))
        m["cs_own"] = rope_tables(h * L_OWN + np.arange(L_OWN))
        m["cs_smp"] = rope_tables(np.tile(PAST + np.arange(8), 16))
        m["state_s"] = np.ascontiguousarray(np.asarray(inp["state_ssm"], np.float32)[16 * c:16 * c + 16])
        for k in ("norm_in", "w_in", "mla_q_norm", "mla_kv_norm", "ssm_lambda_re", "ssm_lambda_im", "ssm_log_dt",
                  "ssm_b_re", "ssm_b_im", "ssm_c_re", "ssm_c_im", "ssm_d"):
            m[k] = np.asarray(inp[k], np.float32)
        m["mla_w_uq"] = np.asarray(inp["mla_w_uq"], np.float32).reshape(384, 1536)
        m["mla_w_uk"] = np.asarray(inp["mla_w_uk"], np.float32).reshape(256, 1024)
        m["mla_w_uv"] = np.asarray(inp["mla_w_uv"], np.float32).reshape(256, 1024)
        m["ctx_bias"] = np.full((128, 1), 0.0 if h == 1 else NEG, np.float32)
        if "G" in phases:
            m["smp_mask"] = smp_mask
            m["pt_core"] = np.ascontiguousarray(np.asarray(inp["page_table"], np.int32)[16 * c:16 * c + 16])
            m["cache_ckv"] = cck
            m["cache_krope"] = ckr
        if "B" in phases:
            for k in ("ssm_w_glu", "ssm_b_glu", "w_br_ssm", "w_br_attn", "w_out", "norm_final"):
                m[k] = np.asarray(inp[k], np.float32)
        maps.append(m)
    return maps


def kernel(**inp):
    if "nc" not in _NC_CACHE:
        _NC_CACHE["nc"] = build(phases=("A", "S", "P", "G", "B"))
    nc = _NC_CACHE["nc"]
    maps = make_in_maps(inp)
    res = run_bass_kernel_spmd(nc, maps, core_ids=list(range(8)))
    R = res.results
    B_, L_ = 4, 4096
    y_p = np.zeros((B_, L_, D), np.float32)
    y_s = np.zeros((128, 8, D), np.float32)
    ckv_p = np.zeros((B_, L_, 256), np.float32)
    kr_p = np.zeros((B_, L_, 32), np.float32)
    ssm_p = np.zeros((B_, 64, 64, 2), np.float32)
    ckv_s = np.zeros((128, 8, 256), np.float32)
    kr_s = np.zeros((128, 8, 32), np.float32)
    ssm_s = np.zeros((128, 64, 64, 2), np.float32)
    for c in range(8):
        bi, h = c // 2, c % 2
        r = R[c]
        ckv_p[bi, h * L_OWN:(h + 1) * L_OWN] = r["o_ckv_p"]
        kr_p[bi, h * L_OWN:(h + 1) * L_OWN] = r["o_kr_p"]
        ckv_s[16 * c:16 * c + 16] = r["o_ckv_s"].reshape(16, 8, 256)
        kr_s[16 * c:16 * c + 16] = r["o_kr_s"].reshape(16, 8, 32)
        if "o_y_p" in r:
            y_p[bi, h * L_OWN:(h + 1) * L_OWN] = r["o_y_p"]
            y_s[16 * c:16 * c + 16] = r["o_y_s"].reshape(16, 8, D)
        if "o_ssm_s" in r:
            ssm_s[16 * c:16 * c + 16] = r["o_ssm_s"]
            if h == 1:
                ssm_p[bi] = r["o_ssm_p"]
    return (y_p, y_s, ckv_p, kr_p, ssm_p, ckv_s, kr_s, ssm_s)


def phase_S(nc, cx, b, P, g):
    ident, identf = g["ident"], g["identf"]
    NG = 64
    GP = 16
    with contextlib.ExitStack() as st:
        swapf = cx.sb(st, [128, 128], F32, "swapf")
        sgn = cx.sb(st, [128, 1], F32, "sgn")
        b.memset("pool", swapf[:], 0.0, [swapf])
        for base in (64, -64):
            P.add("pool", lambda e, base=base: e.affine_select(out=swapf[:], in_=swapf[:], pattern=[[-1, 128]], compare_op=ALU.not_equal,
                                                               fill=1.0, base=base, channel_multiplier=1), [swapf.r], [swapf.r])
        b.memset("pool", sgn[0:64, :], 1.0, [sgn])
        b.memset("pool", sgn[64:128, :], -1.0, [sgn])
        pf = [cx.ps(st, [128, 512], F32, "pf") for _ in range(3)]
        pb = [cx.ps(st, [128, 1024], BF16, "pb") for _ in range(2)]
        cnt = {"pf": 0, "pb": 0}

        def nxt(lst, k):
            cnt[k] += 1
            return lst[cnt[k] % len(lst)]

        lre = cx.sb(st, [128, NG], F32, "lre")
        lim = cx.sb(st, [128, NG], F32, "lim")
        dt = cx.sb(st, [128, NG], F32, "dt")
        wk = cx.sb(st, [128, 12, NG], F32, "wk")
        Lr = cx.sb(st, [128, 9, NG], F32, "Lr")
        Li = cx.sb(st, [128, 9, NG], F32, "Li")
        Ar = cx.sb(st, [128, 9, NG], F32, "Ar")
        Ai = cx.sb(st, [128, 9, NG], F32, "Ai")
        Aiu = cx.sb(st, [128, 9, NG], F32, "Aiu")
        Lis = cx.sb(st, [128, 9, NG], F32, "Lis")
        O_all = cx.sb(st, [128, NG, 8, 16], BF16, "O_all")
        Dv = cx.sb(st, [128, NG], F32, "Dv")
        T_all = cx.sb(st, [128, NG, 128], BF16, "T_all")
        R_all = cx.sb(st, [128, NG, 128], BF16, "R_all")
        pre = contextlib.ExitStack()
        ld = cx.sb(pre, [64, 2, 128], F32, "ld")
        for j, nm in enumerate(("ssm_lambda_re", "ssm_lambda_im")):
            for hf in range(2):
                b.dma(ld[:, j, hf * 64:(hf + 1) * 64], g[nm].t[:, :], [], [ld], key="ld")
        for j, dst in enumerate((lre, lim)):
            pt = nxt(pf, "pf")
            b.tr(pt[:, 0:64], ld[:, j, :], identf[0:64, 0:64], [ld, identf], [pt])
            b.copy("dve", dst[:], pt[:, 0:64], [pt], [dst])
        b.dma(dt[:], bcast_rows(g["ssm_log_dt"].t, NG), [], [dt], key="dt")
        b.act(dt[:], dt[:], AF.Exp, [dt], [dt])
        a_, th, mag, r1, r2, sn, cs_ = (wk[:, i, :] for i in range(7))
        b.tt("dve", a_, lre[:], dt[:], ALU.mult, [lre, dt], [wk])
        b.tt("dve", th, lim[:], dt[:], ALU.mult, [lim, dt], [wk])
        b.act(mag, a_, AF.Exp, [wk], [wk])
        b.ts("dve", r1, th, 1.0 / 64, None, ALU.mult, None, [wk], [wk])
        b.ts("dve", r2, th, 1.0 / 64, 0.5 * math.pi, ALU.mult, ALU.add, [wk], [wk])
        b.act(sn, r1, AF.Sin, [wk], [wk])
        b.act(cs_, r2, AF.Sin, [wk], [wk])
        for _ in range(6):
            b.tt("dve", r1, cs_, cs_, ALU.mult, [wk], [wk])
            b.tt("dve", r2, sn, sn, ALU.mult, [wk], [wk])
            b.tt("dve", wk[:, 7, :], cs_, sn, ALU.mult, [wk], [wk])
            b.tt("dve", cs_, r1, r2, ALU.subtract, [wk], [wk])
            b.ts("dve", sn, wk[:, 7, :], 2.0, None, ALU.mult, None, [wk], [wk])
        b.memset("dve", Lr[:, 0, :], 1.0, [Lr])
        b.memset("dve", Li[:, 0, :], 0.0, [Li])
        b.tt("dve", Lr[:, 1, :], mag, cs_, ALU.mult, [wk], [Lr])
        b.tt("dve", Li[:, 1, :], mag, sn, ALU.mult, [wk], [Li])
        t0, t1 = wk[:, 7, :], wk[:, 8, :]

        def cmul(or_, oi_, ar, ai, br, bi, regs_w):
            b.tt("dve", t0, ar, br, ALU.mult, [Lr, Li, Ar, Aiu, wk], [wk])
            b.tt("dve", t1, ai, bi, ALU.mult, [Lr, Li, Ar, Aiu, wk], [wk])
            b.tt("dve", or_, t0, t1, ALU.subtract, [wk], regs_w)
            b.tt("dve", t0, ar, bi, ALU.mult, [Lr, Li, Ar, Aiu, wk], [wk])
            b.tt("dve", t1, ai, br, ALU.mult, [Lr, Li, Ar, Aiu, wk], [wk])
            b.tt("dve", oi_, t0, t1, ALU.add, [wk], regs_w)

        for e in range(1, 8):
            cmul(Lr[:, e + 1, :], Li[:, e + 1, :], Lr[:, e, :], Li[:, e, :], Lr[:, 1, :], Li[:, 1, :], [Lr, Li])
        b.copy("dve", Ar[:, 0, :], Lr[:, 8, :], [Lr], [Ar])
        b.copy("dve", Aiu[:, 0, :], Li[:, 8, :], [Li], [Aiu])
        for l in range(8):
            cmul(Ar[:, l + 1, :], Aiu[:, l + 1, :], Ar[:, l, :], Aiu[:, l, :], Ar[:, l, :], Aiu[:, l, :], [Ar, Aiu])
        b.ts("dve", Ai[:].rearrange("p l g -> p (l g)"), Aiu[:].rearrange("p l g -> p (l g)"), sgn[:, 0:1], None, ALU.mult, None, [Aiu, sgn], [Ai])
        cr, ci, den, nr = wk[:, 9, :], wk[:, 10, :], wk[:, 11, :], wk[:, 0, :]
        b.ts("dve", nr, Lr[:, 1, :], -1.0, None, ALU.add, None, [Lr], [wk])
        b.tt("dve", t0, lre[:], lre[:], ALU.mult, [lre], [wk])
        b.tt("dve", t1, lim[:], lim[:], ALU.mult, [lim], [wk])
        b.tt("dve", den, t0, t1, ALU.add, [wk], [wk])
        P.add("dve", lambda e: e.reciprocal(out=den, in_=den), [wk.r], [wk.r])
        b.tt("dve", t0, nr, lre[:], ALU.mult, [wk, lre], [wk])
        b.tt("dve", t1, Li[:, 1, :], lim[:], ALU.mult, [Li, lim], [wk])
        b.tt("dve", cr, t0, t1, ALU.add, [wk], [wk])
        b.tt("dve", cr, cr, den, ALU.mult, [wk], [wk])
        b.tt("dve", t0, Li[:, 1, :], lre[:], ALU.mult, [Li, lre], [wk])
        b.tt("dve", t1, nr, lim[:], ALU.mult, [wk, lim], [wk])
        b.tt("dve", ci, t0, t1, ALU.subtract, [wk], [wk])
        b.tt("dve", ci, ci, den, ALU.mult, [wk], [wk])
        cis = wk[:, 1, :]
        b.ts("dve", cis, ci, sgn[:, 0:1], None, ALU.mult, None, [wk, sgn], [wk])
        b.ts("dve", Lis[:].rearrange("p l g -> p (l g)"), Li[:].rearrange("p l g -> p (l g)"), sgn[:, 0:1], None, ALU.mult, None, [Li, sgn], [Lis])

        Bx = cx.sb(pre, [128, NG, 16], F32, "Bx")
        By = cx.sb(pre, [128, NG, 16], F32, "By")
        bre = g["ssm_b_re"].t.rearrange("g p c -> p g c")
        bim = g["ssm_b_im"].t.rearrange("g p c -> p g c")
        b.dma(Bx[0:64], bre, [], [Bx], key="Bx")
        b.dma(Bx[64:128], bim, [], [Bx], key="Bx")
        b.dma(By[0:64], bim, [], [By], key="By")
        b.dma(By[64:128], bre, [], [By], key="By")
        BS = cx.sb(pre, [128, NG, 16], F32, "BS")
        BSp = cx.sb(pre, [128, NG, 16], F32, "BSp")
        tmpB = cx.sb(pre, [128, NG, 16], F32, "tmpB")

        def bc(ap2d):
            return ap2d.unsqueeze(2).broadcast_to([128, NG, 16])

        b.tt("pool", tmpB[:], By[:], bc(cis), ALU.mult, [By, wk], [tmpB])
        b.tt("pool", BS[:], Bx[:], bc(cr), ALU.mult, [Bx, wk], [BS])
        b.tt("pool", BS[:], BS[:], tmpB[:], ALU.subtract, [BS, tmpB], [BS])
        b.tt("pool", tmpB[:], Bx[:], bc(cis), ALU.mult, [Bx, wk], [tmpB])
        b.tt("pool", BSp[:], By[:], bc(cr), ALU.mult, [By, wk], [BSp])
        b.tt("pool", BSp[:], BSp[:], tmpB[:], ALU.add, [BSp, tmpB], [BSp])
        GT = cx.sb(pre, [128, NG, 15, 16], BF16, "GT")
        b.memset("pool", GT[:, :, 8:15, :], 0.0, [GT])
        tmpG = cx.sb(pre, [128, NG, 16], F32, "tmpG")
        for e in range(8):
            b.tt("pool", tmpB[:], BSp[:], bc(Lis[:, e, :]), ALU.mult, [BSp, Lis], [tmpB])
            b.tt("dve", tmpG[:], BS[:], bc(Lr[:, e, :]), ALU.mult, [BS, Lr], [tmpG])
            b.tt("dve", GT[:, :, 7 - e, :], tmpG[:], tmpB[:], ALU.subtract, [tmpG, tmpB], [GT])
        Cx = cx.sb(pre, [128, NG, 16], F32, "Cx")
        Cy = cx.sb(pre, [128, NG, 16], F32, "Cy")
        Cxb = cx.sb(pre, [128, NG, 16], BF16, "Cxb")
        cin = [cx.sb(pre, [128, 128], F32, "cin") for _ in range(2)]
        cre = g["ssm_c_re"].t.rearrange("g c p -> (g c) p")
        cim = g["ssm_c_im"].t.rearrange("g c p -> (g c) p")
        q = 0
        for j in range(8):
            for which in range(2):
                ci_ = cin[q % 2]
                q += 1
                a0, a1 = (cre, cim) if which == 0 else (cim, cre)
                b.dma(ci_[:, 0:64], a0[j * 128:(j + 1) * 128, :], [], [ci_], key=f"cin{q % 2}")
                b.dma(ci_[:, 64:128], a1[j * 128:(j + 1) * 128, :], [], [ci_], key=f"cin{q % 2}")
                pt = nxt(pf, "pf")
                b.tr(pt[:, 0:128], ci_[:], identf[:], [ci_, identf], [pt])
                dst = Cx if which == 0 else Cy
                dv = dst[:, j * 8:(j + 1) * 8, :].rearrange("p g c -> p (g c)")
                if which == 0:
                    b.ts("dve", dv, pt[:, 0:128], sgn[:, 0:1], None, ALU.mult, None, [pt, sgn], [dst])
                else:
                    b.copy("dve", dv, pt[:, 0:128], [pt], [dst])
        b.copy("pool", Cxb[:], Cx[:], [Cx], [Cxb])
        for t in range(8):
            b.tt("pool", tmpB[:], Cy[:], bc(Li[:, t + 1, :]), ALU.mult, [Cy, Li], [tmpB])
            b.tt("dve", tmpG[:], Cx[:], bc(Lr[:, t + 1, :]), ALU.mult, [Cx, Lr], [tmpG])
            b.tt("dve", O_all[:, :, t, :], tmpG[:], tmpB[:], ALU.subtract, [tmpG, tmpB], [O_all])
        dsrc = g["ssm_d"].t.rearrange("(g c) -> c g", c=16)
        for s in range(8):
            P.add("sp", lambda e, s=s: e.dma_start(out=Dv[s * 16:(s + 1) * 16, :], in_=dsrc, allow_slow_non_contiguous=True), [], [Dv.r], dma=True, key="Dv")
        for gi in range(NG):
            pt = nxt(pf, "pf")
            for t in range(8):
                b.mm(pt[:, t * 16:(t + 1) * 16], GT[:, gi, 7 - t:15 - t, :].rearrange("p s c -> p (s c)"), Cxb[:, gi, :], True, True, [GT, Cxb], [pt])
            b.stt("dve", T_all[:, gi, :], identf[:], Dv[:, gi:gi + 1], pt[:, 0:128], ALU.mult, ALU.add, [identf, Dv, pt], [T_all])
            if gi % 8 == 0:
                ptb = nxt(pb, "pb")
            b.tr(ptb[:, (gi % 8) * 128:(gi % 8 + 1) * 128], GT[:, gi, 0:8, :].rearrange("p s c -> p (s c)"), ident[:], [GT, ident], [ptb])
            if gi % 8 == 7:
                b.copy("act", R_all[:, gi - 7:gi + 1, :].rearrange("p g s -> p (g s)"), ptb[:], [ptb], [R_all])

        if g.get("debug"):
            b.dma(g["dbg_L"].t[:, 0:576], Lr[:].rearrange("p l g -> p (l g)"), [Lr], [], key="dbg")
            b.dma(g["dbg_L"].t[:, 576:1152], Li[:].rearrange("p l g -> p (l g)"), [Li], [], key="dbg")
            b.dma(g["dbg_L"].t[:, 1152:1728], Ar[:].rearrange("p l g -> p (l g)"), [Ar], [], key="dbg")
            b.dma(g["dbg_L"].t[:, 1728:2304], Ai[:].rearrange("p l g -> p (l g)"), [Ai], [], key="dbg")
            b.dma(g["dbg_T"].t[:, :], T_all[:].rearrange("p g c -> p (g c)"), [T_all], [], key="dbg")
            b.dma(g["dbg_R"].t[:, :], R_all[:].rearrange("p g c -> p (g c)"), [R_all], [], key="dbg")
            b.dma(g["dbg_O"].t[:, :], O_all[:].rearrange("p g t c -> p (g t c)"), [O_all], [], key="dbg")
        pre.close()
        P.barrier()
        if STAGE == "pre":
            return
        Mtmp = [cx.sb(st, [128, 128], F32, "Mtmp") for _ in range(2)]
        mt = [0]

        def build_M(eng, M, sr, si):
            if eng == "dve":
                b.ts(eng, M[:], identf[:], sr, None, ALU.mult, None, [identf, Ar, Ai], [M])
                b.stt(eng, M[:], swapf[:], si, M[:], ALU.mult, ALU.add, [swapf, Ar, Ai, M], [M])
            else:
                tmp = Mtmp[mt[0] % 2]
                mt[0] += 1
                b.ts(eng, tmp[:], swapf[:], si, None, ALU.mult, None, [swapf, Ar, Ai], [tmp])
                b.stt("dve", M[:], identf[:], sr, tmp[:], ALU.mult, ALU.add, [identf, Ar, Ai, tmp], [M])

        NCH = (L_CTX + L_OWN) // 8
        NOWN = L_OWN // 8
        uview = g["u_scr"].t[0:L_CTX + L_OWN, :].rearrange("(k s) c -> k s c", s=8)
        yview = g["ys_scr"].t[0:L_OWN, :].rearrange("(k s) c -> k s c", s=8)
        usv = g["u_scr"].t[L_CTX + L_OWN:L_CTX + L_OWN + 128, :].rearrange("(k s) c -> k s c", s=8)
        ysv = g["ys_scr"].t[L_OWN:L_OWN + 128, :].rearrange("(k s) c -> k s c", s=8)
        Up = [cx.sb(st, [128, GP, 8, 16], BF16, "Up") for _ in range(4)]
        Ups = cx.sb(st, [16, GP, 8, 16], BF16, "Ups")
        Uraw = [cx.sb(st, [128, 8, GP * 16], BF16, "Uraw") for _ in range(2)]
        Uraw_s = cx.sb(st, [16, 8, GP * 16], BF16, "Uraw_s")
        Yp = [cx.sb(st, [128, 8, GP * 16], BF16, "Yp") for _ in range(2)]
        Yps = cx.sb(st, [16, 8, GP * 16], BF16, "Yps")
        U8 = [cx.sb(st, [128, NCH], BF16, "U8") for _ in range(2)]
        U8s = [cx.sb(st, [128, 16], BF16, "U8s") for _ in range(2)]
        Ssb = [cx.sb(st, [128, NCH], BF16, "Ssb") for _ in range(2)]
        Msb = [cx.sb(st, [128, 128], BF16, "Msb") for _ in range(4)]
        Hsb = [cx.sb(st, [128, NOWN], BF16, "Hsb") for _ in range(2)]
        M0f = [cx.sb(st, [128, 128], F32, "M0f") for _ in range(2)]
        Y8 = [cx.sb(st, [128, NOWN], BF16, "Y8") for _ in range(2)]
        Y8s = [cx.sb(st, [128, 16], BF16, "Y8s") for _ in range(2)]
        hl = cx.sb(st, [128, NG], F32, "hl")
        hlo = cx.sb(st, [64, 64, 2], F32, "hlo")
        st_in = cx.sb(st, [16, GP, 64, 2], F32, "st_in")
        st_out = cx.sb(st, [16, GP, 64, 2], F32, "st_out")
        st_r = cx.sb(st, [16, GP, 2, 64], F32, "st_r")
        h0T = [cx.sb(st, [128, 16], F32, "h0T") for _ in range(2)]
        h0Tb = [cx.sb(st, [128, 16], BF16, "h0Tb") for _ in range(2)]
        hoT = [cx.sb(st, [128, 16], F32, "hoT") for _ in range(2)]
        mcnt = 0
        for half in range(NG // GP):
            c0 = half * GP * 16
            for j in range(4):
                raw = Uraw[j % 2]
                b.dma(raw[:], uview[j * 128:(j + 1) * 128, :, c0:c0 + GP * 16], [], [raw], key=f"Uraw{j % 2}")
                b.copy("pool" if j % 2 else "act", Up[j][:], raw[:].rearrange("k s (g c) -> k g s c", c=16), [raw], [Up[j]])
            b.dma(Uraw_s[:], usv[:, :, c0:c0 + GP * 16], [], [Uraw_s], key="Ups")
            b.copy("pool", Ups[:], Uraw_s[:].rearrange("k s (g c) -> k g s c", c=16), [Uraw_s], [Ups])
            b.dma(st_in[:].rearrange("b g p r -> b (g p r)"), g["state_s"].t[:, half * GP:(half + 1) * GP, :, :].rearrange("b g p r -> b (g p r)"), [], [st_in], key="st_in")
            b.copy("pool", st_r[:], st_in[:].rearrange("b g p r -> b g r p"), [st_in], [st_r])
            for gl in range(GP):
                gi = half * GP + gl
                s_ = gi % 2
                ptb = nxt(pb, "pb")
                for j in range(4):
                    b.tr(ptb[:, j * 128:(j + 1) * 128], Up[j][:, gl, :, :].rearrange("k s c -> k (s c)"), ident[:], [Up[j], ident], [ptb])
                b.tr(ptb[:, 512:528], Ups[:, gl, :, :].rearrange("k s c -> k (s c)"), ident[0:16, 0:16], [Ups, ident], [ptb])
                b.copy("act", U8[s_][:], ptb[:, 0:512], [ptb], [U8[s_]])
                b.copy("dve", U8s[s_][:], ptb[:, 512:528], [ptb], [U8s[s_]])
                if STAGE == "m1":
                    continue
                pS = nxt(pf, "pf")
                b.mm(pS[:, :], R_all[:, gi, :], U8[s_][:], True, False, [R_all, U8[s_]], [pS])
                for l in range(9):
                    d = 1 << l
                    b.copy("act" if l % 2 else "dve", Ssb[s_][:], pS[:, :], [pS], [Ssb[s_]])
                    M = Msb[mcnt % 4]
                    mcnt += 1
                    build_M("pool" if l % 3 else "dve", M, Ar[:, l, gi:gi + 1], Ai[:, l, gi:gi + 1])
                    b.mm(pS[:, d:NCH], M[:], Ssb[s_][:, 0:NCH - d], False, l == 8, [M, Ssb[s_]], [pS])
                b.copy("dve", hl[:, gi:gi + 1], pS[:, NCH - 1:NCH], [pS], [hl])
                if STAGE == "m2":
                    continue
                pY = nxt(pf, "pf")
                b.mm(pY[:, 0:NOWN], T_all[:, gi, :], U8[s_][:, NCH - NOWN:NCH], True, False, [T_all, U8[s_]], [pY])
                b.copy("act", Hsb[s_][:], pS[:, NCH - NOWN - 1:NCH - 1], [pS], [Hsb[s_]])
                b.mm(pY[:, 0:NOWN], O_all[:, gi, :, :].rearrange("p t c -> p (t c)"), Hsb[s_][:], False, True, [O_all, Hsb[s_]], [pY])
                b.copy("dve", Y8[s_][:], pY[:, 0:NOWN], [pY], [Y8[s_]])
                if STAGE == "m3":
                    continue
                ptb2 = nxt(pb, "pb")
                for j in range(2):
                    b.tr(ptb2[:, j * 128:(j + 1) * 128], Y8[s_][:, j * 128:(j + 1) * 128], ident[:], [Y8[s_], ident], [ptb2])
                for j in range(2):
                    b.copy("dve", Yp[j][:, :, gl * 16:(gl + 1) * 16],
                           ptb2[:, j * 128:(j + 1) * 128].rearrange("p (t c) -> p t c", t=8), [ptb2], [Yp[j]])
                if STAGE != "nosmp":
                    pH = nxt(pf, "pf")
                    b.tr(pH[:, 0:16], st_r[:, gl, :, :].rearrange("b r p -> b (r p)"), identf[0:16, 0:16], [st_r, identf], [pH])
                    b.copy("dve", h0T[s_][:], pH[:, 0:16], [pH], [h0T[s_]])
                    b.copy("dve", h0Tb[s_][:], pH[:, 0:16], [pH], [h0Tb[s_]])
                    Mf = M0f[s_]
                    build_M("pool", Mf, Ar[:, 0, gi:gi + 1], Ai[:, 0, gi:gi + 1])
                    pX = nxt(pf, "pf")
                    b.mm(pX[:, 0:16], R_all[:, gi, :], U8s[s_][:], True, False, [R_all, U8s[s_]], [pX])
                    b.mm(pX[:, 0:16], Mf[:], h0T[s_][:], False, True, [Mf, h0T[s_]], [pX])
                    b.mm(pX[:, 16:32], T_all[:, gi, :], U8s[s_][:], True, False, [T_all, U8s[s_]], [pX])
                    b.mm(pX[:, 16:32], O_all[:, gi, :, :].rearrange("p t c -> p (t c)"), h0Tb[s_][:], False, True, [O_all, h0Tb[s_]], [pX])
                    b.copy("dve", hoT[s_][:], pX[:, 0:16], [pX], [hoT[s_]])
                    b.copy("dve", Y8s[s_][:], pX[:, 16:32], [pX], [Y8s[s_]])
                    pO = nxt(pf, "pf")
                    b.tr(pO[0:16, 0:128], hoT[s_][:], identf[:], [hoT[s_], identf], [pO])
                    b.copy("dve", st_out[:, gl, :, :].rearrange("b p r -> b r p"), pO[0:16, 0:128].rearrange("b (r p) -> b r p", r=2), [pO], [st_out])
                    ptb3 = nxt(pb, "pb")
                    b.tr(ptb3[0:16, 0:128], Y8s[s_][:], ident[:], [Y8s[s_], ident], [ptb3])
                    b.copy("dve", Yps[:, :, gl * 16:(gl + 1) * 16], ptb3[0:16, 0:128].rearrange("p (t c) -> p t c", t=8), [ptb3], [Yps])
            for j in range(2):
                b.dma(yview[j * 128:(j + 1) * 128, :, c0:c0 + GP * 16], Yp[j][:], [Yp[j]], [], key=f"Yp{j}")
            b.dma(ysv[:, :, c0:c0 + GP * 16], Yps[:], [Yps], [], key="Yps")
            b.dma(g["o_ssm_s"].t[:, half * GP:(half + 1) * GP, :, :].rearrange("b g p r -> b (g p r)"), st_out[:].rearrange("b g p r -> b (g p r)"), [st_out], [], key="st_out")
        pO = nxt(pf, "pf")
        b.tr(pO[0:64, 0:128], hl[:], identf[:], [hl, identf], [pO])
        b.copy("dve", hlo[:, :, :].rearrange("g p r -> g r p"), pO[0:64, 0:128].rearrange("g (r p) -> g r p", r=2), [pO], [hlo])
        b.dma(g["o_ssm_p"].t[:, :, :], hlo[:], [hlo], [], key="hlo")
    P.barrier()


def load_cast_w(cx, b, st, src2d, kchunks, ncols, name, stg, eng_cycle=("pool", "dve", "act")):
    w = cx.sb(st, [128, kchunks, ncols], BF16, name)
    v = src2d.rearrange("(k p) c -> k p c", p=128)
    for k in range(kchunks):
        s = stg[k % len(stg)]
        b.dma(s[:, 0:ncols], v[k], [], [s], key=f"wstg{k % len(stg)}")
        b.copy(eng_cycle[k % len(eng_cycle)], w[:, k, :], s[:, 0:ncols], [s], [w])
    return w


def phase_P(nc, cx, b, P, g):
    ident = g["ident"]
    LT = L_CTX + L_OWN
    NQB = L_OWN // 128
    with contextlib.ExitStack() as st:
        stg = [cx.sb(st, [128, 1024], F32, "stgP") for _ in range(2)]
        wuk = load_cast_w(cx, b, st, g["mla_w_uk"].t, 2, 1024, "wukP", stg)
        wuv = load_cast_w(cx, b, st, g["mla_w_uv"].t, 2, 1024, "wuvP", stg)
        ckvT = cx.sb(st, [128, 2, LT], BF16, "ckvT")
        for k in range(2):
            b.dma(ckvT[:, k, :], g["ckvT_scr"].t[k], [], [ckvT], key="ckvT")
        KT = [cx.sb(st, [96, LT], BF16, "KT") for _ in range(2)]
        for i in range(2):
            b.dma(KT[i][64:96, :], g["krT_scr"].t[:, :], [], [KT[i]], key="KTr")
        Vh = [cx.sb(st, [128, LT // 128, 64], BF16, "Vh") for _ in range(2)]
        qT = [cx.sb(st, [96, L_OWN], BF16, "qTh") for _ in range(2)]
        S_sb = [cx.sb(st, [128, LT], F32, "S_sb") for _ in range(2)]
        P_bf = [cx.sb(st, [128, LT], BF16, "P_bf") for _ in range(2)]
        PT = [cx.sb(st, [128, LT // 128, 128], BF16, "PT") for _ in range(2)]
        sm = [cx.sb(st, [128, 8], F32, "smP") for _ in range(2)]
        o_t = [cx.sb(st, [128, 64], BF16, "o_t") for _ in range(4)]
        tril = cx.sb(st, [128, 128], F32, "tril")
        cbias = cx.sb(st, [128, 1], F32, "cbias")
        b.dma(cbias[:], g["ctx_bias"].t[:, :], [], [cbias], key="cbias")
        b.memset("pool", tril[:], 0.0, [tril])
        P.add("pool", lambda e: e.affine_select(out=tril[:], in_=tril[:], pattern=[[-1, 128]], compare_op=ALU.is_ge,
                                                fill=NEG, base=0, channel_multiplier=1), [tril.r], [tril.r])
        pf = [cx.ps(st, [128, 512], F32, "pfP") for _ in range(4)]
        pb = [cx.ps(st, [128, 1024], BF16, "pbP") for _ in range(2)]
        po = [cx.ps(st, [128, 512], F32, "poP") for _ in range(2)]
        cnt = {"pf": 0, "pb": 0, "po": 0, "ev": 0}

        def nxt(lst, k):
            cnt[k] += 1
            return lst[cnt[k] % len(lst)]

        def ev_eng():
            cnt["ev"] += 1
            return "act" if cnt["ev"] % 2 else "dve"

        MAXENG = "dve"
        work = []

        def head_prep(h):
            hb = h % 2
            b.dma(qT[hb][:, :], g["qT_scr"].t[h], [], [qT[hb]], key=f"qTh{hb}")
            for tg in range(LT // 512):
                pz = nxt(pf, "pf")
                for k in range(2):
                    b.mm(pz[0:64, :], wuk[:, k, h * 64:(h + 1) * 64], ckvT[:, k, tg * 512:(tg + 1) * 512], k == 0, k == 1, [wuk, ckvT], [pz])
                b.copy(ev_eng(), KT[hb][0:64, tg * 512:(tg + 1) * 512], pz[0:64, :], [pz], [KT[hb]])
            for vg in range(LT // 1024):
                pz = nxt(pf, "pf")
                for j in range(8):
                    kt = vg * 8 + j
                    for k in range(2):
                        b.mm(pz[:, j * 64:(j + 1) * 64], ckvT[:, k, kt * 128:(kt + 1) * 128], wuv[:, k, h * 64:(h + 1) * 64], k == 0, k == 1, [ckvT, wuv], [pz])
                b.copy(ev_eng(), Vh[hb][:, vg * 8:(vg + 1) * 8, :].rearrange("p j d -> p (j d)"), pz[:, :], [pz], [Vh[hb]])

        for h in range(NH):
            hb = h % 2
            for j in range(NQB):
                work.append((h, hb, j))
        def part1(it, h, hb, j):
            s_ = it % 2
            nkb = L_CTX // 128 + j + 1
            nk = nkb * 128
            S, sm_ = S_sb[s_], sm[s_]
            for kg in range((nk + 511) // 512):
                n = min(512, nk - kg * 512)
                pz = nxt(pf, "pf")
                b.mm(pz[:, 0:n], qT[hb][:, j * 128:(j + 1) * 128], KT[hb][:, kg * 512:kg * 512 + n], True, True, [qT[hb], KT[hb]], [pz])
                if kg < L_CTX // 512:
                    b.act(S[:, kg * 512:kg * 512 + n], pz[:, 0:n], AF.Identity, [pz, cbias], [S], bias=cbias[:, 0:1], scale=SCALE)
                else:
                    b.ts("dve", S[:, kg * 512:kg * 512 + n], pz[:, 0:n], SCALE, None, ALU.mult, None, [pz], [S])
            b.tt("dve", S[:, nk - 128:nk], S[:, nk - 128:nk], tril[:], ALU.add, [S, tril], [S])
            b.reduce(MAXENG, sm_[:, 0:1], S[:, 0:nk], ALU.max, [S], [sm_])
            b.ts(MAXENG, sm_[:, 1:2], sm_[:, 0:1], -1.0, None, ALU.mult, None, [sm_], [sm_])

        def part2(it, h, hb, j):
            s_ = it % 2
            nkb = L_CTX // 128 + j + 1
            nk = nkb * 128
            S, Pb, PT_, sm_ = S_sb[s_], P_bf[s_], PT[s_], sm[s_]
            b.act(Pb[:, 0:nk], S[:, 0:nk], AF.Exp, [S, sm_], [Pb, sm_], bias=sm_[:, 1:2], scale=1.0, accum=sm_[:, 2:3])
            P.add("dve", lambda e, sm_=sm_: e.reciprocal(out=sm_[:, 3:4], in_=sm_[:, 2:3]), [sm_.r], [sm_.r])
            for kb0 in range(0, nkb, 8):
                nb = min(8, nkb - kb0)
                pt = nxt(pb, "pb")
                for q in range(nb):
                    kb = kb0 + q
                    b.tr(pt[:, q * 128:(q + 1) * 128], Pb[:, kb * 128:(kb + 1) * 128], ident[:], [Pb, ident], [pt])
                b.copy(ev_eng(), PT_[:, kb0:kb0 + nb, :].rearrange("p k q -> p (k q)"), pt[:, 0:nb * 128], [pt], [PT_])
            pov = nxt(po, "po")
            for kb in range(nkb):
                b.mm(pov[:, 0:64], PT_[:, kb, :], Vh[hb][:, kb, :], kb == 0, kb == nkb - 1, [PT_, Vh[hb]], [pov])
            ot = o_t[it % 4]
            b.ts("dve", ot[:], pov[:, 0:64], sm_[:, 3:4], None, ALU.mult, None, [pov, sm_], [ot])
            b.dma(g["o_scr"].t[j * 128:(j + 1) * 128, h * 64:(h + 1) * 64], ot[:], [ot], [], key=f"ot{it % 4}")

        for i, (h, hb, j) in enumerate(work):
            if j == 0:
                head_prep(h)
            part1(i, h, hb, j)
            if i > 0:
                part2(i - 1, *work[i - 1])
        part2(len(work) - 1, *work[-1])
    P.barrier()


def phase_G(nc, cx, b, P, g):
    ident = g["ident"]
    NB = 16
    NSLOT = 8
    NSTEP = 128 // NSLOT
    with contextlib.ExitStack() as st:
        stg = [cx.sb(st, [128, 1024], F32, "stgG") for _ in range(2)]
        wuv = load_cast_w(cx, b, st, g["mla_w_uv"].t, 2, 1024, "wuvG", stg)
        qlT = cx.sb(st, [128, 2, NB, 128], BF16, "qlT")
        qrT = cx.sb(st, [32, NB, 128], BF16, "qrT")
        for k in range(2):
            b.dma(qlT[:, k], g["qlT_scr"].t[k], [], [qlT], key="qlT")
        b.dma(qrT[:], g["qrT_scr"].t[:, :, :], [], [qrT], key="qrT")
        ckvsT = cx.sb(st, [128, 2, 128], BF16, "ckvsT")
        krsT = cx.sb(st, [32, 128], BF16, "krsT")
        ckvs = cx.sb(st, [128, 256], BF16, "ckvs")
        for k in range(2):
            b.dma(ckvsT[:, k, :], g["ckvsT_scr"].t[k], [], [ckvsT], key="ckvsT")
        b.dma(krsT[:], g["krsT_scr"].t[:, :], [], [krsT], key="krsT")
        b.dma(ckvs[:], g["ckvs_scr"].t[:, :], [], [ckvs], key="ckvsG")
        ptab = cx.sb(st, [128, NB], I32, "ptab")
        P.add("sp", lambda e: e.dma_start(out=ptab[:], in_=g["pt_core"].t.rearrange("b n -> n b"), allow_slow_non_contiguous=True), [], [ptab.r], dma=True, key="ptab")
        G_ = [cx.sb(st, [128, NSLOT, 256], F32, "Gf") for _ in range(2)]
        Gr = [cx.sb(st, [128, NSLOT, 32], F32, "Grf") for _ in range(2)]
        Gb = [cx.sb(st, [128, NSLOT, 256], BF16, "Gb") for _ in range(2)]
        Gbk = [cx.sb(st, [128, NSLOT, 32], BF16, "Gbk") for _ in range(2)]
        KTg = [cx.sb(st, [128, 2, NSLOT * 128], BF16, "KTg") for _ in range(2)]
        KrT = [cx.sb(st, [32, NSLOT * 128], BF16, "KrT") for _ in range(2)]
        Pb = [cx.sb(st, [128, NSLOT * 128], BF16, "PbG") for _ in range(2)]
        PTg = [cx.sb(st, [128, NSLOT, 128], BF16, "PTg") for _ in range(2)]
        Snew = cx.sb(st, [128, 128], F32, "Snew")
        msk = cx.sb(st, [128, NB, 128], F32, "mskG")
        b.dma(msk[:], g["smp_mask"].t.rearrange("b r k -> r b k"), [], [msk], key="msk")
        acc = cx.sb(st, [128, 256], F32, "acc")
        sm = [cx.sb(st, [128, 12], F32, "smG") for _ in range(2)]
        ol_bf = cx.sb(st, [128, 256], BF16, "ol_bf")
        olT = cx.sb(st, [128, 2, 128], BF16, "olT")
        oT_s = cx.sb(st, [64, NH, 128], BF16, "oT_s")
        o_smp = cx.sb(st, [128, 1024], BF16, "o_smp")
        pf = [cx.ps(st, [128, 512], F32, "pfG") for _ in range(1)]
        pb = [cx.ps(st, [128, 1024], BF16, "pbG") for _ in range(2)]
        po = [cx.ps(st, [128, 512], F32, "poG") for _ in range(1)]
        Pnew = cx.sb(st, [128, 128], BF16, "Pnew")
        PTnew = cx.sb(st, [128, 128], BF16, "PTnew")
        cnt = {"pf": 0, "pb": 0, "ev": 0}

        def nxt(lst, k):
            cnt[k] += 1
            return lst[cnt[k] % len(lst)]

        def ev_eng():
            cnt["ev"] += 1
            return "act" if cnt["ev"] % 2 else "dve"

        ckv_rows = g["cache_ckv"].t.rearrange("n (s c) -> (n s) c", c=NSLOT * 256)
        kr_rows = g["cache_krope"].t.rearrange("n (s c) -> (n s) c", c=NSLOT * 32)
        ptf = cx.sb(st, [128, NB], F32, "ptf")
        stpf = cx.sb(st, [128, NSTEP], F32, "stpf")
        idx_f = cx.sb(st, [128, NB, NSTEP], F32, "idx_f")
        idx_all = cx.sb(st, [128, NB, NSTEP], I32, "idx_all")
        b.copy("dve", ptf[:], ptab[:], [ptab], [ptf])
        b.ts("dve", ptf[:], ptf[:], float(NSTEP), None, ALU.mult, None, [ptf], [ptf])
        for i in range(NSTEP):
            b.memset("pool", stpf[:, i:i + 1], float(i), [stpf])
        b.tt("dve", idx_f[:], ptf[:].unsqueeze(2).broadcast_to([128, NB, NSTEP]), stpf[:].unsqueeze(1).broadcast_to([128, NB, NSTEP]), ALU.add, [ptf, stpf], [idx_f])
        b.copy("dve", idx_all[:], idx_f[:], [idx_f], [idx_all])
        pS2 = [cx.ps(st, [128, 1024], F32, "pS2") for _ in range(2)]
        smx = [cx.sb(st, [128, 2], F32, "smx") for _ in range(2)]

        def init_sample(bi):
            sm_ = sm[bi % 2]
            pz = pf[0]
            for k in range(2):
                b.mm(pz[:, 0:128], qlT[:, k, bi, :], ckvsT[:, k, :], k == 0, False, [qlT, ckvsT], [pz])
            b.mm(pz[:, 0:128], qrT[:, bi, :], krsT[:, :], False, True, [qrT, krsT], [pz])
            b.tt("dve", Snew[:], pz[:, 0:128], msk[:, bi, :], ALU.add, [pz, msk], [Snew])
            b.reduce("dve", sm_[:, 0:1], Snew[:], ALU.max, [Snew], [sm_])
            b.ts("dve", sm_[:, 1:2], sm_[:, 0:1], -SCALE, None, ALU.mult, None, [sm_], [sm_])
            b.act(Pnew[:], Snew[:], AF.Exp, [Snew, sm_], [Pnew, sm_], bias=sm_[:, 1:2], scale=SCALE, accum=sm_[:, 2:3])
            pt = nxt(pb, "pb")
            b.tr(pt[:, 0:128], Pnew[:], ident[:], [Pnew, ident], [pt])
            b.copy("dve", PTnew[:], pt[:, 0:128], [pt], [PTnew])
            pov = po[0]
            b.mm(pov[:, 0:256], PTnew[:], ckvs[:, :], True, True, [PTnew, ckvs], [pov])
            b.copy("dve", acc[:], pov[:, 0:256], [pov], [acc])

        def stage_x(i, bi, stp):
            q_ = i % 2
            Gf, Grf, Gb_, KT_, KrT_ = G_[q_], Gr[q_], Gb[q_], KTg[q_], KrT[q_]
            P.add("pool", lambda e, Gf=Gf, stp=stp, bi=bi: e.indirect_dma_start(
                out=Gf[:].rearrange("p s c -> p (s c)"), out_offset=None, in_=ckv_rows,
                in_offset=bass.IndirectOffsetOnAxis(ap=idx_all[:, bi, stp:stp + 1], axis=0)), [idx_all.r], [Gf.r], dma=True, key=f"Gf{q_}")
            P.add("pool", lambda e, Grf=Grf, stp=stp, bi=bi: e.indirect_dma_start(
                out=Grf[:].rearrange("p s c -> p (s c)"), out_offset=None, in_=kr_rows,
                in_offset=bass.IndirectOffsetOnAxis(ap=idx_all[:, bi, stp:stp + 1], axis=0)), [idx_all.r], [Grf.r], dma=True, key=f"Grf{q_}")
            Gbk_ = Gbk[q_]
            hs = NSLOT // 2
            b.copy("dve", Gb_[:, 0:hs, :].rearrange("p s c -> p (s c)"), Gf[:, 0:hs, :].rearrange("p s c -> p (s c)"), [Gf], [Gb_])
            b.copy("act", Gb_[:, hs:NSLOT, :].rearrange("p s c -> p (s c)"), Gf[:, hs:NSLOT, :].rearrange("p s c -> p (s c)"), [Gf], [Gb_])
            b.copy("dve", Gbk_[:].rearrange("p s c -> p (s c)"), Grf[:].rearrange("p s c -> p (s c)"), [Grf], [Gbk_])
            for k in range(2):
                pt = nxt(pb, "pb")
                for s in range(NSLOT):
                    b.tr(pt[:, s * 128:(s + 1) * 128], Gb_[:, s, k * 128:(k + 1) * 128], ident[:], [Gb_, ident], [pt])
                b.copy("act" if k else "dve", KT_[:, k, :], pt[:, :], [pt], [KT_])
            pt = nxt(pb, "pb")
            for s in range(NSLOT):
                b.tr(pt[0:32, s * 128:(s + 1) * 128], Gbk_[:, s, :], ident[:], [Gbk_, ident], [pt])
            b.copy("dve", KrT_[:, :], pt[0:32, :], [pt], [KrT_])
            pz = pS2[q_]
            for hf in range(2):
                sl = slice(hf * 512, (hf + 1) * 512)
                for k in range(2):
                    b.mm(pz[:, sl], qlT[:, k, bi, :], KT_[:, k, sl], k == 0, False, [qlT, KT_], [pz])
                b.mm(pz[:, sl], qrT[:, bi, :], KrT_[:, sl], False, True, [qrT, KrT_], [pz])
            b.reduce("dve", smx[q_][:, 0:1], pz[:, :], ALU.max, [pz], [smx[q_]])

        def stage_y(i, bi, stp):
            q_ = i % 2
            sm_ = sm[bi % 2]
            Gb_, Pb_, PT_, pz = Gb[q_], Pb[q_], PTg[q_], pS2[q_]
            b.tt("dve", sm_[:, 6:7], smx[q_][:, 0:1], sm_[:, 0:1], ALU.max, [sm_, smx[q_]], [sm_])
            b.ts("dve", sm_[:, 7:8], sm_[:, 6:7], -SCALE, None, ALU.mult, None, [sm_], [sm_])
            b.act(sm_[:, 8:9], sm_[:, 0:1], AF.Exp, [sm_], [sm_], bias=sm_[:, 7:8], scale=SCALE)
            b.act(Pb_[:, :], pz[:, :], AF.Exp, [pz, sm_], [Pb_, sm_], bias=sm_[:, 7:8], scale=SCALE, accum=sm_[:, 9:10])
            b.stt("dve", sm_[:, 2:3], sm_[:, 2:3], sm_[:, 8:9], sm_[:, 9:10], ALU.mult, ALU.add, [sm_], [sm_])
            b.copy("dve", sm_[:, 0:1], sm_[:, 6:7], [sm_], [sm_])
            pt = nxt(pb, "pb")
            for s in range(NSLOT):
                b.tr(pt[:, s * 128:(s + 1) * 128], Pb_[:, s * 128:(s + 1) * 128], ident[:], [Pb_, ident], [pt])
            b.copy("act", PT_[:].rearrange("p s q -> p (s q)"), pt[:, :], [pt], [PT_])
            pov = po[0]
            for s in range(NSLOT):
                b.mm(pov[:, 0:256], PT_[:, s, :], Gb_[:, s, 0:256], s == 0, s == NSLOT - 1, [PT_, Gb_], [pov])
            b.stt("dve", acc[:], acc[:], sm_[:, 8:9], pov[:, 0:256], ALU.mult, ALU.add, [acc, sm_, pov], [acc])

        def finalize(bi):
            sm_ = sm[bi % 2]
            if g.get("debug"):
                b.dma(g["dbg_sm"].t[bi], sm_[:], [sm_], [], key="dbgsm")
                b.dma(g["dbg_acc"].t[bi], acc[:], [acc], [], key="dbgacc")
            P.add("dve", lambda e, sm_=sm_: e.reciprocal(out=sm_[:, 3:4], in_=sm_[:, 2:3]), [sm_.r], [sm_.r])
            b.ts("dve", ol_bf[:], acc[:], sm_[:, 3:4], None, ALU.mult, None, [acc, sm_], [ol_bf])
            pt = nxt(pb, "pb")
            for k in range(2):
                b.tr(pt[:, k * 128:(k + 1) * 128], ol_bf[:, k * 128:(k + 1) * 128], ident[:], [ol_bf, ident], [pt])
            b.copy("dve", olT[:].rearrange("p k r -> p (k r)"), pt[:, 0:256], [pt], [olT])
            pz = pf[0]
            for h in range(NH):
                for k in range(2):
                    b.mm(pz[0:64, h * 8:(h + 1) * 8], wuv[:, k, h * 64:(h + 1) * 64], olT[:, k, h * 8:(h + 1) * 8], k == 0, k == 1, [wuv, olT], [pz])
            b.copy("dve", oT_s[:, :, bi * 8:(bi + 1) * 8], pz[0:64, 0:128].rearrange("p (h t) -> p h t", h=NH), [pz], [oT_s])

        items = [(bi, stp) for bi in range(NB) for stp in range(NSTEP)]

        def emit_y(i):
            bi, stp = items[i]
            if stp == 0:
                init_sample(bi)
            stage_y(i, bi, stp)
            if stp == NSTEP - 1:
                finalize(bi)

        for i, (bi, stp) in enumerate(items):
            stage_x(i, bi, stp)
            if i > 0:
                emit_y(i - 1)
        emit_y(len(items) - 1)
        for hg in range(2):
            pt = nxt(pb, "pb")
            for hh in range(8):
                b.tr(pt[:, hh * 64:(hh + 1) * 64], oT_s[:, hg * 8 + hh, :], ident[0:64, 0:64], [oT_s, ident], [pt])
            b.copy("dve", o_smp[:, hg * 512:(hg + 1) * 512], pt[:, 0:512], [pt], [o_smp])
        b.dma(g["o_scr"].t[L_OWN:L_OWN + 128, :], o_smp[:], [o_smp], [], key="o_smp")
    P.barrier()


def phase_B(nc, cx, b, P, g):
    ident = g["ident"]
    with contextlib.ExitStack() as st:
        stg = [cx.sb(st, [128, 1024], F32, "stgB") for _ in range(2)]
        wglu = load_cast_w(cx, b, st, g["ssm_w_glu"].t, 8, 1024, "wglu", stg)
        wbs = load_cast_w(cx, b, st, g["w_br_ssm"].t, 8, 1024, "wbs", stg)
        wba = load_cast_w(cx, b, st, g["w_br_attn"].t, 8, 1024, "wba", stg)
        wo = load_cast_w(cx, b, st, g["w_out"].t, 8, 1024, "wo", stg)
        bglu = cx.sb(st, [128, D], F32, "bglu")
        gfin = cx.sb(st, [128, D], F32, "gfin")
        b.dma(bglu[:], bcast_rows(g["ssm_b_glu"].t, D), [], [bglu], key="bglu")
        b.dma(gfin[:], bcast_rows(g["norm_final"].t, D), [], [gfin], key="gfin")
        NB = 2
        ys = [cx.sb(st, [128, D], BF16, "ysB") for _ in range(NB)]
        gt = [cx.sb(st, [128, 4096], BF16, "gtB") for _ in range(NB)]
        ot = [cx.sb(st, [128, D], BF16, "otB") for _ in range(NB)]
        xt = [cx.sb(st, [128, D], F32, "xtB") for _ in range(NB)]
        t1 = cx.sb(st, [128, D], F32, "t1B")
        zg = cx.sb(st, [128, D], F32, "zgB")
        ab = cx.sb(st, [128, D], BF16, "abB")
        aT = [cx.sb(st, [128, 8, 128], BF16, "aTB") for _ in range(2)]
        mg = cx.sb(st, [128, D], F32, "mgB")
        hh = cx.sb(st, [128, D], F32, "hhB")
        yo = [cx.sb(st, [128, D], F32, "yoB") for _ in range(NB)]
        stt_ = [cx.sb(st, [128, 4], F32, "stB") for _ in range(NB)]
        junk = cx.sb(st, [128, D], BF16, "junkB")
        pz = [cx.ps(st, [128, 512], F32, "pzB") for _ in range(4)]
        pT = [cx.ps(st, [128, 1024], BF16, "pTB") for _ in range(2)]
        cnt = {"pz": 0, "pT": 0, "aT": 0}

        def nxt(lst, k):
            cnt[k] += 1
            return lst[cnt[k] % len(lst)]

        def transp(src):
            pt = nxt(pT, "pT")
            for k in range(8):
                b.tr(pt[:, k * 128:(k + 1) * 128], src[:, k * 128:(k + 1) * 128], ident[:], [src, ident], [pt])
            a = nxt(aT, "aT")
            b.copy("act", a[:].rearrange("p k t -> p (k t)"), pt[:], [pt], [a])
            return a

        def linear(a, w, cg):
            p_ = nxt(pz, "pz")
            for k in range(8):
                b.mm(p_[:, :], a[:, k, :], w[:, k, cg * 512:(cg + 1) * 512], k == 0, k == 7, [a, w], [p_])
            return p_

        tiles = [("own", i) for i in range(L_OWN // 128)] + [("smp", 0)]
        for ti, (kind, i) in enumerate(tiles):
            s_ = ti % NB
            r0 = i * 128 if kind == "own" else L_OWN
            X = g["x_own"].t[i * 128:(i + 1) * 128, :] if kind == "own" else g["x_smp"].t[:, :]
            OUT = g["o_y_p"].t[i * 128:(i + 1) * 128, :] if kind == "own" else g["o_y_s"].t[:, :]
            ys_, gt_, ot_, xt_, st_ = ys[s_], gt[s_], ot[s_], xt[s_], stt_[s_]
            b.dma(ys_[:], g["ys_scr"].t[r0:r0 + 128, :], [], [ys_], key=f"ysB{s_}")
            b.dma(gt_[:], g["g_scr"].t[r0:r0 + 128, :], [], [gt_], key=f"gtB{s_}")
            b.dma(ot_[:], g["o_scr"].t[r0:r0 + 128, :], [], [ot_], key=f"otB{s_}")
            b.dma(xt_[:], X, [], [xt_], key=f"xtB{s_}")
            b.tt("pool", t1[:], ys_[:], ys_[:], ALU.mult, [ys_], [t1])
            b.ts("pool", t1[:], t1[:], 0.044715, 1.0, ALU.mult, ALU.add, [t1], [t1])
            b.tt("pool", t1[:], t1[:], ys_[:], ALU.mult, [t1, ys_], [t1])
            b.act(t1[:], t1[:], AF.Sigmoid, [t1], [t1], scale=1.5957691216057308)
            b.tt("dve", zg[:], t1[:], ys_[:], ALU.mult, [t1, ys_], [zg])
            b.copy("pool", ab[:], zg[:], [zg], [ab])
            a = transp(ab)
            for cg in range(2):
                p_ = linear(a, wglu, cg)
                sl = slice(cg * 512, (cg + 1) * 512)
                b.tt("dve", t1[:, sl], p_[:, :], bglu[:, sl], ALU.add, [p_, bglu], [t1])
                b.act(t1[:, sl], t1[:, sl], AF.Sigmoid, [t1], [t1])
                b.tt("dve", t1[:, sl], t1[:, sl], zg[:, sl], ALU.mult, [t1, zg], [t1])
            b.tt("dve", ab[:], t1[:], gt_[:, 0:1024], ALU.mult, [t1, gt_], [ab])
            a = transp(ab)
            for cg in range(2):
                p_ = linear(a, wbs, cg)
                sl = slice(cg * 512, (cg + 1) * 512)
                b.tt("dve", mg[:, sl], p_[:, :], gt_[:, 2048 + cg * 512:2048 + (cg + 1) * 512], ALU.mult, [p_, gt_], [mg])
            b.tt("pool", ab[:], ot_[:], gt_[:, 1024:2048], ALU.mult, [ot_, gt_], [ab])
            a = transp(ab)
            for cg in range(2):
                p_ = linear(a, wba, cg)
                sl = slice(cg * 512, (cg + 1) * 512)
                b.tt("dve", t1[:, sl], p_[:, :], gt_[:, 3072 + cg * 512:3072 + (cg + 1) * 512], ALU.mult, [p_, gt_], [t1])
            b.tt("pool", mg[:], mg[:], t1[:], ALU.add, [mg, t1], [mg])
            b.copy("pool", ab[:], mg[:], [mg], [ab])
            a = transp(ab)
            for cg in range(2):
                p_ = linear(a, wo, cg)
                sl = slice(cg * 512, (cg + 1) * 512)
                b.tt("dve", hh[:, sl], p_[:, :], xt_[:, sl], ALU.add, [p_, xt_], [hh])
            b.act(junk[:], hh[:], AF.Square, [hh], [st_], accum=st_[:, 0:1])
            b.rstd(st_, 0, 1, 2, 1.0 / D)
            b.stt("dve", yo[s_][:], hh[:], st_[:, 2:3], gfin[:], ALU.mult, ALU.mult, [hh, st_, gfin], [yo[s_]])
            b.dma(OUT, yo[s_][:], [yo[s_]], [], key=f"yoB{s_}")
```

```python
import contextlib
import math
import numpy as np
import concourse.bass as bass
import concourse.mybir as mybir
from concourse.bass_utils import run_bass_kernel_spmd

F32 = mybir.dt.float32
BF16 = mybir.dt.bfloat16
I32 = mybir.dt.int32
AF = mybir.ActivationFunctionType
ALU = mybir.AluOpType
AX = mybir.AxisListType

D = 1024
NCOL = 5792
C_US, C_GS, C_CQ, C_KV, C_KR, C_GA, C_MS, C_MA = 0, 1024, 2048, 2432, 2688, 2720, 3744, 4768
NH = 16
SCALE = 96 ** -0.5
EPS = 1e-6
NEG = -1e30
L_OWN = 2048
L_CTX = 2048
PAST = 16384
NPAGE = 128
STAGE = "all"


class Reg:
    __slots__ = ("name", "w", "r")

    def __init__(self, name=""):
        self.name = name
        self.w = None
        self.r = []


class Op:
    __slots__ = ("eng", "fn", "deps", "idx", "dma", "key", "sig", "cnt", "waits")

    def __init__(self, eng, fn, dma, key):
        self.eng, self.fn, self.dma, self.key = eng, fn, dma, key
        self.deps = []
        self.sig = False
        self.cnt = 0
        self.waits = []


class Prog:
    ENGS = ("pe", "act", "dve", "pool", "sp")

    def __init__(self, nc):
        self.nc = nc
        self.ops = {e: [] for e in self.ENGS}
        self.dma_keys = {}

    def add(self, eng, fn, reads=(), writes=(), dma=False, key=None):
        op = Op(eng, fn, dma, key)
        if dma:
            lst = self.dma_keys.setdefault(key, [])
            lst.append(op)
            op.cnt = 16 * len(lst)
            op.sig = True
        for r in reads:
            if r.w is not None:
                op.deps.append((r.w, "raw"))
        for w in writes:
            if w.w is not None:
                op.deps.append((w.w, "waw"))
            for rd in w.r:
                op.deps.append((rd, "war"))
        for r in reads:
            r.r.append(op)
        for w in writes:
            w.w = op
            w.r = []
        op.idx = len(self.ops[eng])
        self.ops[eng].append(op)
        return op

    def barrier(self):
        lasts = []
        for e in self.ENGS:
            comp = [o for o in self.ops[e] if not o.dma]
            if comp:
                lasts.append(comp[-1])
        for key, lst in self.dma_keys.items():
            if lst:
                lasts.append(lst[-1])
        for e in self.ENGS:
            op = Op(e, (lambda eh: eh.nop()), False, None)
            op.deps = [(d, "raw") for d in lasts]
            op.idx = len(self.ops[e])
            self.ops[e].append(op)

    def emit(self):
        nc = self.nc
        for e in self.ENGS:
            for op in self.ops[e]:
                need = {}
                for (d, kind) in op.deps:
                    if d is op:
                        continue
                    if (not d.dma) and (not op.dma) and d.eng == op.eng and kind != "raw":
                        continue
                    k = ("dma", d.key) if d.dma else ("eng", d.eng)
                    if k not in need or (d.dma and need[k].cnt < d.cnt) or ((not d.dma) and need[k].idx < d.idx):
                        need[k] = d
                op.deps = need
        for e in self.ENGS:
            for op in self.ops[e]:
                for k, d in op.deps.items():
                    if not d.dma:
                        d.sig = True
        for e in self.ENGS:
            c = 0
            for op in self.ops[e]:
                if not op.dma and op.sig:
                    c += 1
                    op.cnt = c
        sem_eng, sem_key = {}, {}
        stack = contextlib.ExitStack()
        for e in self.ENGS:
            sem_eng[e] = stack.enter_context(nc.semaphore("se_" + e))
        for key in self.dma_keys:
            sem_key[key] = stack.enter_context(nc.semaphore("sd_" + str(key)))
        for e in self.ENGS:
            seen = {}
            for op in self.ops[e]:
                for k, d in op.deps.items():
                    sem = sem_key[d.key] if d.dma else sem_eng[d.eng]
                    if seen.get(k, 0) >= d.cnt:
                        continue
                    seen[k] = d.cnt
                    op.waits.append((sem, d.cnt))
        nops = sum(len(self.ops[e]) for e in self.ENGS)
        nw = sum(len(op.waits) for e in self.ENGS for op in self.ops[e])
        print(f"[prog] ops={nops} waits={nw} dma_keys={len(self.dma_keys)}", flush=True)

        def run(e_name, eh):
            for op in self.ops[e_name]:
                for (sem, v) in op.waits:
                    eh.wait_ge(sem, v)
                ins = op.fn(eh)
                if op.sig:
                    if op.dma:
                        ins.then_inc(sem_key[op.key], 16)
                    else:
                        ins.then_inc(sem_eng[e_name], 1)
            if e_name == "sp":
                for key, lst in self.dma_keys.items():
                    if lst:
                        eh.wait_ge(sem_key[key], 16 * len(lst))

        with nc.Block() as block:
            @block.tensor
            def _(t):
                run("pe", t)

            @block.scalar
            def _(s):
                run("act", s)

            @block.vector
            def _(v):
                run("dve", v)

            @block.gpsimd
            def _(g):
                run("pool", g)

            @block.sync
            def _(sy):
                run("sp", sy)
        stack.close()


class T:
    __slots__ = ("t", "r")

    def __init__(self, t, name):
        self.t = t
        self.r = Reg(name)

    def __getitem__(self, k):
        return self.t[k]


class Ctx:
    def __init__(self, nc, P):
        self.nc, self.P = nc, P
        self.n = 0

    def sb(self, st, shape, dt, name=None):
        self.n += 1
        name = f"{name or 't'}_{self.n}"
        return T(st.enter_context(self.nc.sbuf_tensor(name, shape, dt)), name)

    def ps(self, st, shape, dt, name=None):
        self.n += 1
        name = f"{name or 'p'}_{self.n}"
        return T(st.enter_context(self.nc.psum_tensor(name, shape, dt)), name)

    def dram(self, name, shape, dt, kind="Internal"):
        return T(self.nc.dram_tensor(name, shape, dt, kind=kind).ap(), name)


def _regs(xs):
    return [x.r if isinstance(x, T) else x for x in xs]


class B:
    def __init__(self, cx):
        self.cx, self.P = cx, cx.P
        self.dq = 0

    def dma(self, out, in_, reads, writes, key, eng="sp"):
        self.P.add(eng, lambda e: e.dma_start(out=out, in_=in_), _regs(reads), _regs(writes), dma=True, key=key)

    def mm(self, out, lhsT, rhs, start, stop, reads, writes):
        self.P.add("pe", lambda e: e.matmul(out=out, lhsT=lhsT, rhs=rhs, start=start, stop=stop), _regs(reads), _regs(writes))

    def tr(self, out, in_, ident, reads, writes):
        self.P.add("pe", lambda e: e.transpose(out=out, in_=in_, identity=ident), _regs(reads), _regs(writes))

    def act(self, out, in_, func, reads, writes, bias=None, scale=None, accum=None, eng="act"):
        kw = {}
        if bias is not None:
            kw["bias"] = bias
        if scale is not None:
            kw["scale"] = scale
        if accum is not None:
            kw["accum_out"] = accum
        self.P.add("act", lambda e: e.activation(out=out, in_=in_, func=func, **kw), _regs(reads), _regs(writes))

    def copy(self, eng, out, in_, reads, writes):
        if eng == "act":
            self.P.add("act", lambda e: e.activation(out=out, in_=in_, func=AF.Copy), _regs(reads), _regs(writes))
        else:
            self.P.add(eng, lambda e: e.tensor_copy(out=out, in_=in_), _regs(reads), _regs(writes))

    def tt(self, eng, out, in0, in1, op, reads, writes):
        self.P.add(eng, lambda e: e.tensor_tensor(out=out, in0=in0, in1=in1, op=op), _regs(reads), _regs(writes))

    def ts(self, eng, out, in0, s1, s2, op0, op1, reads, writes):
        if op1 is None:
            self.P.add(eng, lambda e: e.tensor_scalar(out=out, in0=in0, scalar1=s1, scalar2=None, op0=op0), _regs(reads), _regs(writes))
        else:
            self.P.add(eng, lambda e: e.tensor_scalar(out=out, in0=in0, scalar1=s1, scalar2=s2, op0=op0, op1=op1), _regs(reads), _regs(writes))

    def stt(self, eng, out, in0, scalar, in1, op0, op1, reads, writes):
        self.P.add(eng, lambda e: e.scalar_tensor_tensor(out=out, in0=in0, scalar=scalar, in1=in1, op0=op0, op1=op1), _regs(reads), _regs(writes))

    def rstd(self, st_, c_in, c_tmp, c_out, inv_n):
        self.ts("dve", st_[:, c_tmp:c_tmp + 1], st_[:, c_in:c_in + 1], inv_n, EPS, ALU.mult, ALU.add, [st_], [st_])
        self.act(st_[:, c_tmp:c_tmp + 1], st_[:, c_tmp:c_tmp + 1], AF.Sqrt, [st_], [st_])
        self.P.add("dve", lambda e: e.reciprocal(out=st_[:, c_out:c_out + 1], in_=st_[:, c_tmp:c_tmp + 1]), [st_.r], [st_.r])

    def memset(self, eng, ap, val, writes):
        self.P.add(eng, lambda e: e.memset(ap, val), [], _regs(writes))

    def reduce(self, eng, out, in_, op, reads, writes):
        self.P.add(eng, lambda e: e.tensor_reduce(out=out, in_=in_, axis=AX.X, op=op), _regs(reads), _regs(writes))


def bcast_rows(ap1d, n, p=128):
    return ap1d.rearrange("(o n) -> o n", o=1).broadcast_to([p, n])


def build(debug=False, phases=("A",)):
    nc = bass.Bass("TRN2", target_bir_lowering=False)
    P = Prog(nc)
    cx = Ctx(nc, P)
    b = B(cx)
    kio = "ExternalOutput" if debug else "Internal"

    def din(name, shape, dt=F32):
        return T(nc.dram_tensor(name, shape, dt, kind="ExternalInput").ap(), name)

    def dout(name, shape, dt=F32):
        return T(nc.dram_tensor(name, shape, dt, kind="ExternalOutput").ap(), name)

    x_ctx = din("x_ctx", [L_CTX, D])
    x_own = din("x_own", [L_OWN, D])
    x_smp = din("x_smp", [128, D])
    cs_ctx = din("cs_ctx", [L_CTX, 32])
    cs_own = din("cs_own", [L_OWN, 32])
    cs_smp = din("cs_smp", [128, 32])
    norm_in = din("norm_in", [D])
    w_in = din("w_in", [D, NCOL])
    mla_q_norm = din("mla_q_norm", [384])
    mla_kv_norm = din("mla_kv_norm", [256])
    mla_w_uq = din("mla_w_uq", [384, 1536])
    mla_w_uk = din("mla_w_uk", [256, 1024])
    ssm_lambda_re = din("ssm_lambda_re", [64, 64])
    ssm_lambda_im = din("ssm_lambda_im", [64, 64])
    ssm_log_dt = din("ssm_log_dt", [64])
    ssm_b_re = din("ssm_b_re", [64, 64, 16])
    ssm_b_im = din("ssm_b_im", [64, 64, 16])
    ssm_c_re = din("ssm_c_re", [64, 16, 64])
    ssm_c_im = din("ssm_c_im", [64, 16, 64])
    ssm_d = din("ssm_d", [D])
    state_s = din("state_s", [16, 64, 64, 2])
    mla_w_uv = din("mla_w_uv", [256, 1024])
    ctx_bias = din("ctx_bias", [128, 1])
    if "G" in phases:
        smp_mask = din("smp_mask", [16, 128, 128])
        pt_core = din("pt_core", [16, 128], I32)
        cache_ckv = din("cache_ckv", [20480, 128 * 256])
        cache_krope = din("cache_krope", [20480, 128 * 32])
    if "B" in phases:
        ssm_w_glu = din("ssm_w_glu", [D, D])
        ssm_b_glu = din("ssm_b_glu", [D])
        w_br_ssm = din("w_br_ssm", [D, D])
        w_br_attn = din("w_br_attn", [D, D])
        w_out = din("w_out", [D, D])
        norm_final = din("norm_final", [D])
        o_y_p = dout("o_y_p", [L_OWN, D])
        o_y_s = dout("o_y_s", [128, D])
    o_ssm_p = dout("o_ssm_p", [64, 64, 2])
    o_ssm_s = dout("o_ssm_s", [16, 64, 64, 2])
    o_ckv_p = dout("o_ckv_p", [L_OWN, 256])
    o_kr_p = dout("o_kr_p", [L_OWN, 32])
    o_ckv_s = dout("o_ckv_s", [128, 256])
    o_kr_s = dout("o_kr_s", [128, 32])
    NTOK = L_CTX + L_OWN + 128
    u_scr = cx.dram("u_scr", [NTOK, D], BF16, kio)
    g_scr = cx.dram("g_scr", [L_OWN + 128, 4096], BF16, kio)
    o_scr = cx.dram("o_scr", [L_OWN + 128, D], BF16, kio)
    ys_scr = cx.dram("ys_scr", [L_OWN + 128, D], BF16, kio)
    qT_scr = cx.dram("qT_scr", [NH, 96, L_OWN], BF16, kio)
    ckvT_scr = cx.dram("ckvT_scr", [2, 128, L_CTX + L_OWN], BF16, kio)
    krT_scr = cx.dram("krT_scr", [32, L_CTX + L_OWN], BF16, kio)
    qlT_scr = cx.dram("qlT_scr", [2, 128, 16, 128], BF16, kio)
    qrT_scr = cx.dram("qrT_scr", [32, 16, 128], BF16, kio)
    ckvs_scr = cx.dram("ckvs_scr", [128, 256], BF16, kio)
    ckvsT_scr = cx.dram("ckvsT_scr", [2, 128, 128], BF16, kio)
    krsT_scr = cx.dram("krsT_scr", [32, 128], BF16, kio)

    if debug:
        dbg_sm = dout("dbg_sm", [16, 128, 12])
        dbg_acc = dout("dbg_acc", [16, 128, 256])
        dbg_L = dout("dbg_L", [128, 2304])
        dbg_T = dout("dbg_T", [128, 8192], BF16)
        dbg_R = dout("dbg_R", [128, 8192], BF16)
        dbg_O = dout("dbg_O", [128, 8192], BF16)
    with contextlib.ExitStack() as top:
        ident = cx.sb(top, [128, 128], BF16, "ident")
        identf = cx.sb(top, [128, 128], F32, "identf")
        b.memset("pool", identf[:], 0.0, [identf])
        P.add("pool", lambda e: e.affine_select(out=identf[:], in_=identf[:], pattern=[[-1, 128]], compare_op=ALU.not_equal,
                                                fill=1.0, base=0, channel_multiplier=1), [identf.r], [identf.r])
        b.copy("dve", ident[:], identf[:], [identf], [ident])

        if "A" in phases:
            phase_A(nc, cx, b, P, locals())
        if "S" in phases:
            phase_S(nc, cx, b, P, locals())
        if "P" in phases:
            phase_P(nc, cx, b, P, locals())
        if "G" in phases:
            phase_G(nc, cx, b, P, locals())
        if "B" in phases:
            phase_B(nc, cx, b, P, locals())
    P.emit()
    return nc


def phase_A(nc, cx, b, P, g):
    ident = g["ident"]
    x_ctx, x_own, x_smp = g["x_ctx"], g["x_own"], g["x_smp"]
    cs_ctx, cs_own, cs_smp = g["cs_ctx"], g["cs_own"], g["cs_smp"]
    w_in, norm_in = g["w_in"], g["norm_in"]
    with contextlib.ExitStack() as st:
        w_sb = cx.sb(st, [128, 8, NCOL], BF16, "w_in_sb")
        wuq_sb = cx.sb(st, [128, 3, 1536], BF16, "wuq")
        wukT_sb = cx.sb(st, [64, NH, 256], BF16, "wukT")
        gin = cx.sb(st, [128, D], F32, "gin")
        gq = cx.sb(st, [128, 384], F32, "gq")
        gkv = cx.sb(st, [128, 256], F32, "gkv")
        pz = [cx.ps(st, [128, 512], F32, "pz") for _ in range(4)]
        pT = [cx.ps(st, [128, 1024], BF16, "pT") for _ in range(2)]
        pq = [cx.ps(st, [128, 512], F32, "pq") for _ in range(2)]
        cnt = {"pz": 0, "pT": 0, "pq": 0, "cast": 0}

        def nxt(lst, k):
            cnt[k] += 1
            return lst[cnt[k] % len(lst)]

        cast_engs = ["pool", "dve", "act"]

        def cast(out, in_, reads, writes):
            cnt["cast"] += 1
            b.copy(cast_engs[cnt["cast"] % 3], out, in_, reads, writes)

        st0 = contextlib.ExitStack()
        stg = [cx.sb(st0, [128, 2896], F32, "stg") for _ in range(2)]
        w_v = w_in.t.rearrange("(k p) c -> k p c", p=128)
        for k in range(8):
            for h in range(2):
                s = stg[h]
                b.dma(s[:, :], w_v[k][:, h * 2896:(h + 1) * 2896], [], [s], key=f"stg{h}")
                cast(w_sb[:, k, h * 2896:(h + 1) * 2896], s[:, :], [s], [w_sb])
        wq_v = g["mla_w_uq"].t.rearrange("(k p) c -> k p c", p=128)
        for k in range(3):
            s = stg[k % 2]
            b.dma(s[:, 0:1536], wq_v[k], [], [s], key=f"stg{k % 2}")
            cast(wuq_sb[:, k, :], s[:, 0:1536], [s], [wuq_sb])
        wk_v = g["mla_w_uk"].t.rearrange("(k p) c -> k p c", p=128)
        wk_bf = cx.sb(st0, [128, 2, 1024], BF16, "wk_bf")
        for k in range(2):
            s = stg[k % 2]
            b.dma(s[:, 0:1024], wk_v[k], [], [s], key=f"stg{k % 2}")
            cast(wk_bf[:, k, :], s[:, 0:1024], [s], [wk_bf])
        for k in range(2):
            for hg in range(2):
                pt = nxt(pT, "pT")
                for hh in range(8):
                    h = hg * 8 + hh
                    b.tr(pt[0:64, hh * 128:(hh + 1) * 128], wk_bf[:, k, h * 64:(h + 1) * 64], ident[:], [wk_bf, ident], [pt])
                b.copy("dve", wukT_sb[:, hg * 8:(hg + 1) * 8, k * 128:(k + 1) * 128],
                       pt[0:64, :].rearrange("p (h c) -> p h c", h=8), [pt], [wukT_sb])
        st0.close()
        P.barrier()
        b.dma(gin[:], bcast_rows(norm_in.t, D), [], [gin], key="gin")
        b.dma(gq[:], bcast_rows(g["mla_q_norm"].t, 384), [], [gq], key="gq")
        b.dma(gkv[:], bcast_rows(g["mla_kv_norm"].t, 256), [], [gkv], key="gkv")

        NB = 2
        xt = [cx.sb(st, [128, D], F32, "xt") for _ in range(NB)]
        cs = [cx.sb(st, [128, 32], F32, "cs") for _ in range(NB)]
        junk1 = cx.sb(st, [128, D], BF16, "junk")
        junk = [junk1, junk1]
        stat = [cx.sb(st, [128, 8], F32, "stat") for _ in range(NB)]
        xn = [cx.sb(st, [128, D], BF16, "xn") for _ in range(NB)]
        xnT = [cx.sb(st, [128, 8, 128], BF16, "xnT") for _ in range(NB)]
        u_t = [cx.sb(st, [128, D], BF16, "u_t") for _ in range(NB)]
        g_t = [cx.sb(st, [128, 2048], BF16, "g_t") for _ in range(NB)]
        cqn = [cx.sb(st, [128, 384], BF16, "cqn") for _ in range(NB)]
        cqnT = [cx.sb(st, [128, 3, 128], BF16, "cqnT") for _ in range(NB)]
        q_t1 = cx.sb(st, [128, NH, 96], F32, "q_t")
        q_t = [q_t1, q_t1]
        qb_t = [cx.sb(st, [128, NH, 96], BF16, "qb_t") for _ in range(NB)]
        qr_tmp1 = cx.sb(st, [128, 4, NH, 16], F32, "qr_tmp")
        qr_tmp = [qr_tmp1, qr_tmp1]
        qT_t = [cx.sb(st, [96, NH, 128], BF16, "qT_t") for _ in range(NB)]
        ckv_t = [cx.sb(st, [128, 256], F32, "ckv_t") for _ in range(NB)]
        ckvb_t = [cx.sb(st, [128, 288], BF16, "ckvb_t") for _ in range(NB)]
        kr_t = [cx.sb(st, [128, 32], F32, "kr_t") for _ in range(NB)]
        kr_tmp = [cx.sb(st, [128, 64], F32, "kr_tmp") for _ in range(NB)]
        kvT_t = [cx.sb(st, [128, 3, 128], BF16, "kvT_t") for _ in range(NB)]
        qlT_t = cx.sb(st, [128, 2, 16, NH, 8], BF16, "qlT_t")

        tiles = [("ctx", i) for i in range(L_CTX // 128)] + [("own", i) for i in range(L_OWN // 128)] + [("smp", 0)]
        def loads_A(tj):
            kd, ii = tiles[tj]
            sj = tj % NB
            Xj, CSj = {"ctx": (x_ctx, cs_ctx), "own": (x_own, cs_own), "smp": (x_smp, cs_smp)}[kd]
            b.dma(xt[sj][:], Xj.t[ii * 128:(ii + 1) * 128, :], [], [xt[sj]], key=f"xt{sj}")
            b.dma(cs[sj][:], CSj.t[ii * 128:(ii + 1) * 128, :], [], [cs[sj]], key=f"cs{sj}")

        for ti, (kind, i) in enumerate(tiles):
            s_ = ti % NB
            X, CS = {"ctx": (x_ctx, cs_ctx), "own": (x_own, cs_own), "smp": (x_smp, cs_smp)}[kind]
            tok0 = {"ctx": 0, "own": L_CTX, "smp": L_CTX + L_OWN}[kind] + i * 128
            full = kind != "ctx"
            xt_, st_, xn_, xnT_ = xt[s_], stat[s_], xn[s_], xnT[s_]
            if ti == 0:
                loads_A(0)
            if ti + 1 < len(tiles):
                loads_A(ti + 1)
            b.act(junk[s_][:], xt_[:], AF.Square, [xt_], [st_], accum=st_[:, 0:1])
            b.rstd(st_, 0, 1, 2, 1.0 / D)
            b.stt("dve", xn_[:], xt_[:], st_[:, 2:3], gin[:], ALU.mult, ALU.mult, [xt_, st_, gin], [xn_])
            pt = nxt(pT, "pT")
            for k in range(8):
                b.tr(pt[:, k * 128:(k + 1) * 128], xn_[:, k * 128:(k + 1) * 128], ident[:], [xn_, ident], [pt])
            b.copy("act", xnT_[:].rearrange("p k t -> p (k t)"), pt[:], [pt], [xnT_])

            def proj(c0, n):
                pzt = nxt(pz, "pz")
                for k in range(8):
                    b.mm(pzt[:, 0:n], xnT_[:, k, :], w_sb[:, k, c0:c0 + n], k == 0, k == 7, [xnT_, w_sb], [pzt])
                return pzt

            for hf in range(2):
                pzt = proj(C_US + hf * 512, 512)
                b.copy("dve" if hf else "act", u_t[s_][:, hf * 512:(hf + 1) * 512], pzt[:, :], [pzt], [u_t[s_]])
            b.dma(g["u_scr"].t[tok0:tok0 + 128, :], u_t[s_][:], [u_t[s_]], [], key=f"uo{s_}")
            pzt = proj(C_KV, 288)
            b.act(junk[s_][:, 0:256], pzt[:, 0:256], AF.Square, [pzt], [st_], accum=st_[:, 3:4])
            b.rstd(st_, 3, 4, 5, 1.0 / 256)
            b.stt("dve", ckv_t[s_][:], pzt[:, 0:256], st_[:, 5:6], gkv[:], ALU.mult, ALU.mult, [pzt, st_, gkv], [ckv_t[s_]])
            kt = kr_tmp[s_]
            c_, s2_ = cs[s_][:, 0:16], cs[s_][:, 16:32]
            b.tt("dve", kt[:, 0:16], pzt[:, 256:272], c_, ALU.mult, [pzt, cs[s_]], [kt])
            b.tt("dve", kt[:, 16:32], pzt[:, 272:288], s2_, ALU.mult, [pzt, cs[s_]], [kt])
            b.tt("dve", kt[:, 32:48], pzt[:, 256:272], s2_, ALU.mult, [pzt, cs[s_]], [kt])
            b.tt("dve", kt[:, 48:64], pzt[:, 272:288], c_, ALU.mult, [pzt, cs[s_]], [kt])
            b.tt("dve", kr_t[s_][:, 0:16], kt[:, 0:16], kt[:, 16:32], ALU.subtract, [kt], [kr_t[s_]])
            b.tt("dve", kr_t[s_][:, 16:32], kt[:, 32:48], kt[:, 48:64], ALU.add, [kt], [kr_t[s_]])
            if kind == "own":
                b.dma(g["o_ckv_p"].t[i * 128:(i + 1) * 128, :], ckv_t[s_][:], [ckv_t[s_]], [], key=f"ckvo{s_}")
                b.dma(g["o_kr_p"].t[i * 128:(i + 1) * 128, :], kr_t[s_][:], [kr_t[s_]], [], key=f"kro{s_}")
            if kind == "smp":
                b.dma(g["o_ckv_s"].t[:, :], ckv_t[s_][:], [ckv_t[s_]], [], key=f"ckvo{s_}")
                b.dma(g["o_kr_s"].t[:, :], kr_t[s_][:], [kr_t[s_]], [], key=f"kro{s_}")
            cb = ckvb_t[s_]
            b.copy("pool", cb[:, 0:256], ckv_t[s_][:], [ckv_t[s_]], [cb])
            b.copy("pool", cb[:, 256:288], kr_t[s_][:], [kr_t[s_]], [cb])
            pt = nxt(pT, "pT")
            b.tr(pt[:, 0:128], cb[:, 0:128], ident[:], [cb, ident], [pt])
            b.tr(pt[:, 128:256], cb[:, 128:256], ident[:], [cb, ident], [pt])
            b.tr(pt[0:32, 256:384], cb[:, 256:288], ident[:], [cb, ident], [pt])
            kvT = kvT_t[s_]
            b.copy("act", kvT[:, 0:2, :].rearrange("p k t -> p (k t)"), pt[:, 0:256], [pt], [kvT])
            b.copy("act", kvT[0:32, 2, :], pt[0:32, 256:384], [pt], [kvT])
            if kind == "smp":
                b.dma(g["ckvs_scr"].t[:, :], cb[:, 0:256], [cb], [], key="ckvs")
                for k in range(2):
                    b.dma(g["ckvsT_scr"].t[k], kvT[:, k, :], [kvT], [], key=f"kvTo{s_}")
                b.dma(g["krsT_scr"].t[:, :], kvT[0:32, 2, :], [kvT], [], key=f"kvTo{s_}")
            else:
                p0 = tok0
                for k in range(2):
                    b.dma(g["ckvT_scr"].t[k, :, p0:p0 + 128], kvT[:, k, :], [kvT], [], key=f"kvTo{s_}")
                b.dma(g["krT_scr"].t[:, p0:p0 + 128], kvT[0:32, 2, :], [kvT], [], key=f"kvTo{s_}")
            if not full:
                continue
            gr0 = i * 128 if kind == "own" else L_OWN
            glist = ((C_GS, 0, AF.Silu), (C_GS + 512, 512, AF.Silu), (C_GA, 1024, AF.Silu), (C_GA + 512, 1536, AF.Silu),
                     (C_MS, 2048, AF.Sigmoid), (C_MS + 512, 2560, AF.Sigmoid), (C_MA, 3072, AF.Sigmoid), (C_MA + 512, 3584, AF.Sigmoid))
            for hf in range(2):
                gt = g_t[hf]
                for (c0, o0, fn) in glist[hf * 4:(hf + 1) * 4]:
                    pzt = proj(c0, 512)
                    b.act(gt[:, o0 - hf * 2048:o0 - hf * 2048 + 512], pzt[:, :], fn, [pzt], [gt])
                b.dma(g["g_scr"].t[gr0:gr0 + 128, hf * 2048:(hf + 1) * 2048], gt[:], [gt], [], key=f"go{hf}")
            pzt = proj(C_CQ, 384)
            b.act(junk[s_][:, 0:384], pzt[:, 0:384], AF.Square, [pzt], [st_], accum=st_[:, 6:7])
            b.rstd(st_, 6, 7, 6, 1.0 / 384)
            b.stt("dve", cqn[s_][:], pzt[:, 0:384], st_[:, 6:7], gq[:], ALU.mult, ALU.mult, [pzt, st_, gq], [cqn[s_]])
            pt = nxt(pT, "pT")
            for k in range(3):
                b.tr(pt[:, k * 128:(k + 1) * 128], cqn[s_][:, k * 128:(k + 1) * 128], ident[:], [cqn[s_], ident], [pt])
            b.copy("act", cqnT[s_][:].rearrange("p k t -> p (k t)"), pt[:, 0:384], [pt], [cqnT[s_]])
            qv = q_t[s_][:].rearrange("p h d -> p (h d)")
            for cg in range(3):
                pqt = nxt(pq, "pq")
                for k in range(3):
                    b.mm(pqt[:, :], cqnT[s_][:, k, :], wuq_sb[:, k, cg * 512:(cg + 1) * 512], k == 0, k == 2, [cqnT[s_], wuq_sb], [pqt])
                b.copy("act" if cg % 2 else "dve", qv[:, cg * 512:(cg + 1) * 512], pqt[:, :], [pqt], [q_t[s_]])
            q3 = q_t[s_]
            tm = qr_tmp[s_]
            cb3 = cs[s_][:, 0:16].unsqueeze(1).broadcast_to([128, NH, 16])
            sb3 = cs[s_][:, 16:32].unsqueeze(1).broadcast_to([128, NH, 16])
            b.tt("pool", tm[:, 0], q3[:, :, 64:80], cb3, ALU.mult, [q3, cs[s_]], [tm])
            b.tt("pool", tm[:, 1], q3[:, :, 80:96], sb3, ALU.mult, [q3, cs[s_]], [tm])
            b.tt("pool", tm[:, 2], q3[:, :, 64:80], sb3, ALU.mult, [q3, cs[s_]], [tm])
            b.tt("pool", tm[:, 3], q3[:, :, 80:96], cb3, ALU.mult, [q3, cs[s_]], [tm])
            qb = qb_t[s_]
            b.copy("pool", qb[:, :, 0:64], q3[:, :, 0:64], [q3], [qb])
            b.tt("dve", qb[:, :, 64:80], tm[:, 0], tm[:, 1], ALU.subtract, [tm], [qb])
            b.tt("dve", qb[:, :, 80:96], tm[:, 2], tm[:, 3], ALU.add, [tm], [qb])
            qT = qT_t[s_]
            for hg in range(2):
                pt = nxt(pT, "pT")
                for hh in range(8):
                    b.tr(pt[0:96, hh * 128:(hh + 1) * 128], qb[:, hg * 8 + hh, :], ident[:], [qb, ident], [pt])
                b.copy("act" if hg else "dve", qT[:, hg * 8:(hg + 1) * 8, :].rearrange("p h t -> p (h t)"), pt[0:96, :], [pt], [qT])
            if kind == "own":
                b.dma(g["qT_scr"].t[:, :, i * 128:(i + 1) * 128].rearrange("h r t -> r h t"), qT[:, :, :], [qT], [], key=f"qTo{s_}")
            else:
                for hg in range(4):
                    for k in range(2):
                        pqt = nxt(pq, "pq")
                        for hh in range(4):
                            h = hg * 4 + hh
                            b.mm(pqt[:, hh * 128:(hh + 1) * 128], wukT_sb[:, h, k * 128:(k + 1) * 128], qT[0:64, h, :], True, True, [wukT_sb, qT], [pqt])
                        b.copy("dve", qlT_t[:, k, :, hg * 4:(hg + 1) * 4, :].rearrange("p b h t -> p h b t"),
                               pqt[:, :].rearrange("p (h b t) -> p h b t", h=4, b=16), [pqt], [qlT_t])
                for k in range(2):
                    b.dma(g["qlT_scr"].t[k], qlT_t[:, k].rearrange("p b h t -> p b (h t)"), [qlT_t], [], key="qlo")
                for h in range(NH):
                    b.dma(g["qrT_scr"].t[:, :, h * 8:(h + 1) * 8], qT[64:96, h, :].rearrange("p (b t) -> p b t", b=16), [qT], [], key="qro")
    P.barrier()


def rope_tables(pos):
    half = 16
    inv = (10000.0 ** (-np.arange(half, dtype=np.float32) * np.float32(2.0 / 32))).astype(np.float32)
    ang = pos.astype(np.float32)[:, None] * inv[None, :]
    return np.concatenate([np.cos(ang), np.sin(ang)], axis=1).astype(np.float32)


_NC_CACHE = {}


def make_in_maps(inp, phases=("A", "S", "P", "G", "B")):
    maps = []
    r_ = np.arange(128)
    hq, tq = r_ // 8, r_ % 8
    bk, tk = r_ // 8, r_ % 8
    smp_mask = np.where((bk[None, None, :] == np.arange(16)[:, None, None]) & (tk[None, None, :] <= tq[None, :, None]), 0.0, NEG).astype(np.float32)
    cck = np.asarray(inp["cache_ckv"], np.float32).reshape(20480, 128 * 256) if "G" in phases else None
    ckr = np.asarray(inp["cache_krope"], np.float32).reshape(20480, 128 * 32) if "G" in phases else None
    xp = np.asarray(inp["x_prompt"], np.float32)
    xs = np.asarray(inp["x_sample"], np.float32)
    for c in range(8):
        bi, h = c // 2, c % 2
        m = {}
        m["x_own"] = np.ascontiguousarray(xp[bi, h * L_OWN:(h + 1) * L_OWN])
        m["x_ctx"] = np.ascontiguousarray(xp[bi, 0:L_CTX]) if h == 1 else np.zeros((L_CTX, D), np.float32)
        m["x_smp"] = np.ascontiguousarray(xs[16 * c:16 * c + 16].reshape(128, D))
        m["cs_ctx"] = rope_tables(np.arange(L_CTX))
        m["cs_own"] = rope_tables(h * L_OWN + np.arange(L_OWN))
        m["cs_smp"] = rope_tables(np.tile(PAST + np.arange(8), 16))
        m["state_s"] = np.ascontiguousarray(np.asarray(inp["state_ssm"], np.float32)[16 * c:16 * c + 16])
        for k in ("norm_in", "w_in", "mla_q_norm", "mla_kv_norm", "ssm_lambda_re", "ssm_lambda_im", "ssm_log_dt",
                  "ssm_b_re", "ssm_b_im", "ssm_c_re", "ssm_c_im", "ssm_d"):
            m[k] = np.asarray(inp[k], np.float32)
        m["mla_w_uq"] = np.asarray(inp["mla_w_uq"], np.float32).reshape(384, 1536)
        m["mla_w_uk"] = np.asarray(inp["mla_w_uk"], np.float32).reshape(256, 1024)
        m["mla_w_uv"] = np.asarray(inp["mla_w_uv"], np.float32).reshape(256, 1024)
        m["ctx_bias"] = np.full((128, 1), 0.0 if h == 1 else NEG, np.float32)
        if "G" in phases:
            m["smp_mask"] = smp_mask
            m["pt_core"] = np.ascontiguousarray(np.asarray(inp["page_table"], np.int32)[16 * c:16 * c + 16])
            m["cache_ckv"] = cck
            m["cache_krope"] = ckr
        if "B" in phases:
            for k in ("ssm_w_glu", "ssm_b_glu", "w_br_ssm", "w_br_attn", "w_out", "norm_final"):
                m[k] = np.asarray(inp[k], np.float32)
        maps.append(m)
    return maps


def kernel(**inp):
    if "nc" not in _NC_CACHE:
        _NC_CACHE["nc"] = build(phases=("A", "S", "P", "G", "B"))
    nc = _NC_CACHE["nc"]
    maps = make_in_maps(inp)
    res = run_bass_kernel_spmd(nc, maps, core_ids=list(range(8)))
    R = res.results
    B_, L_ = 4, 4096
    y_p = np.zeros((B_, L_, D), np.float32)
    y_s = np.zeros((128, 8, D), np.float32)
    ckv_p = np.zeros((B_, L_, 256), np.float32)
    kr_p = np.zeros((B_, L_, 32), np.float32)
    ssm_p = np.zeros((B_, 64, 64, 2), np.float32)
    ckv_s = np.zeros((128, 8, 256), np.float32)
    kr_s = np.zeros((128, 8, 32), np.float32)
    ssm_s = np.zeros((128, 64, 64, 2), np.float32)
    for c in range(8):
        bi, h = c // 2, c % 2
        r = R[c]
        ckv_p[bi, h * L_OWN:(h + 1) * L_OWN] = r["o_ckv_p"]
        kr_p[bi, h * L_OWN:(h + 1) * L_OWN] = r["o_kr_p"]
        ckv_s[16 * c:16 * c + 16] = r["o_ckv_s"].reshape(16, 8, 256)
        kr_s[16 * c:16 * c + 16] = r["o_kr_s"].reshape(16, 8, 32)
        if "o_y_p" in r:
            y_p[bi, h * L_OWN:(h + 1) * L_OWN] = r["o_y_p"]
            y_s[16 * c:16 * c + 16] = r["o_y_s"].reshape(16, 8, D)
        if "o_ssm_s" in r:
            ssm_s[16 * c:16 * c + 16] = r["o_ssm_s"]
            if h == 1:
                ssm_p[bi] = r["o_ssm_p"]
    return (y_p, y_s, ckv_p, kr_p, ssm_p, ckv_s, kr_s, ssm_s)


def phase_S(nc, cx, b, P, g):
    ident, identf = g["ident"], g["identf"]
    NG = 64
    GP = 16
    with contextlib.ExitStack() as st:
        swapf = cx.sb(st, [128, 128], F32, "swapf")
        sgn = cx.sb(st, [128, 1], F32, "sgn")
        b.memset("pool", swapf[:], 0.0, [swapf])
        for base in (64, -64):
            P.add("pool", lambda e, base=base: e.affine_select(out=swapf[:], in_=swapf[:], pattern=[[-1, 128]], compare_op=ALU.not_equal,
                                                               fill=1.0, base=base, channel_multiplier=1), [swapf.r], [swapf.r])
        b.memset("pool", sgn[0:64, :], 1.0, [sgn])
        b.memset("pool", sgn[64:128, :], -1.0, [sgn])
        pf = [cx.ps(st, [128, 512], F32, "pf") for _ in range(3)]
        pb = [cx.ps(st, [128, 1024], BF16, "pb") for _ in range(2)]
        cnt = {"pf": 0, "pb": 0}

        def nxt(lst, k):
            cnt[k] += 1
            return lst[cnt[k] % len(lst)]

        lre = cx.sb(st, [128, NG], F32, "lre")
        lim = cx.sb(st, [128, NG], F32, "lim")
        dt = cx.sb(st, [128, NG], F32, "dt")
        wk = cx.sb(st, [128, 12, NG], F32, "wk")
        Lr = cx.sb(st, [128, 9, NG], F32, "Lr")
        Li = cx.sb(st, [128, 9, NG], F32, "Li")
        Ar = cx.sb(st, [128, 9, NG], F32, "Ar")
        Ai = cx.sb(st, [128, 9, NG], F32, "Ai")
        Aiu = cx.sb(st, [128, 9, NG], F32, "Aiu")
        Lis = cx.sb(st, [128, 9, NG], F32, "Lis")
        O_all = cx.sb(st, [128, NG, 8, 16], BF16, "O_all")
        Dv = cx.sb(st, [128, NG], F32, "Dv")
        T_all = cx.sb(st, [128, NG, 128], BF16, "T_all")
        R_all = cx.sb(st, [128, NG, 128], BF16, "R_all")
        pre = contextlib.ExitStack()
        ld = cx.sb(pre, [64, 2, 128], F32, "ld")
        for j, nm in enumerate(("ssm_lambda_re", "ssm_lambda_im")):
            for hf in range(2):
                b.dma(ld[:, j, hf * 64:(hf + 1) * 64], g[nm].t[:, :], [], [ld], key="ld")
        for j, dst in enumerate((lre, lim)):
            pt = nxt(pf, "pf")
            b.tr(pt[:, 0:64], ld[:, j, :], identf[0:64, 0:64], [ld, identf], [pt])
            b.copy("dve", dst[:], pt[:, 0:64], [pt], [dst])
        b.dma(dt[:], bcast_rows(g["ssm_log_dt"].t, NG), [], [dt], key="dt")
        b.act(dt[:], dt[:], AF.Exp, [dt], [dt])
        a_, th, mag, r1, r2, sn, cs_ = (wk[:, i, :] for i in range(7))
        b.tt("dve", a_, lre[:], dt[:], ALU.mult, [lre, dt], [wk])
        b.tt("dve", th, lim[:], dt[:], ALU.mult, [lim, dt], [wk])
        b.act(mag, a_, AF.Exp, [wk], [wk])
        b.ts("dve", r1, th, 1.0 / 64, None, ALU.mult, None, [wk], [wk])
        b.ts("dve", r2, th, 1.0 / 64, 0.5 * math.pi, ALU.mult, ALU.add, [wk], [wk])
        b.act(sn, r1, AF.Sin, [wk], [wk])
        b.act(cs_, r2, AF.Sin, [wk], [wk])
        for _ in range(6):
            b.tt("dve", r1, cs_, cs_, ALU.mult, [wk], [wk])
            b.tt("dve", r2, sn, sn, ALU.mult, [wk], [wk])
            b.tt("dve", wk[:, 7, :], cs_, sn, ALU.mult, [wk], [wk])
            b.tt("dve", cs_, r1, r2, ALU.subtract, [wk], [wk])
            b.ts("dve", sn, wk[:, 7, :], 2.0, None, ALU.mult, None, [wk], [wk])
        b.memset("dve", Lr[:, 0, :], 1.0, [Lr])
        b.memset("dve", Li[:, 0, :], 0.0, [Li])
        b.tt("dve", Lr[:, 1, :], mag, cs_, ALU.mult, [wk], [Lr])
        b.tt("dve", Li[:, 1, :], mag, sn, ALU.mult, [wk], [Li])
        t0, t1 = wk[:, 7, :], wk[:, 8, :]

        def cmul(or_, oi_, ar, ai, br, bi, regs_w):
            b.tt("dve", t0, ar, br, ALU.mult, [Lr, Li, Ar, Aiu, wk], [wk])
            b.tt("dve", t1, ai, bi, ALU.mult, [Lr, Li, Ar, Aiu, wk], [wk])
            b.tt("dve", or_, t0, t1, ALU.subtract, [wk], regs_w)
            b.tt("dve", t0, ar, bi, ALU.mult, [Lr, Li, Ar, Aiu, wk], [wk])
            b.tt("dve", t1, ai, br, ALU.mult, [Lr, Li, Ar, Aiu, wk], [wk])
            b.tt("dve", oi_, t0, t1, ALU.add, [wk], regs_w)

        for e in range(1, 8):
            cmul(Lr[:, e + 1, :], Li[:, e + 1, :], Lr[:, e, :], Li[:, e, :], Lr[:, 1, :], Li[:, 1, :], [Lr, Li])
        b.copy("dve", Ar[:, 0, :], Lr[:, 8, :], [Lr], [Ar])
        b.copy("dve", Aiu[:, 0, :], Li[:, 8, :], [Li], [Aiu])
        for l in range(8):
            cmul(Ar[:, l + 1, :], Aiu[:, l + 1, :], Ar[:, l, :], Aiu[:, l, :], Ar[:, l, :], Aiu[:, l, :], [Ar, Aiu])
        b.ts("dve", Ai[:].rearrange("p l g -> p (l g)"), Aiu[:].rearrange("p l g -> p (l g)"), sgn[:, 0:1], None, ALU.mult, None, [Aiu, sgn], [Ai])
        cr, ci, den, nr = wk[:, 9, :], wk[:, 10, :], wk[:, 11, :], wk[:, 0, :]
        b.ts("dve", nr, Lr[:, 1, :], -1.0, None, ALU.add, None, [Lr], [wk])
        b.tt("dve", t0, lre[:], lre[:], ALU.mult, [lre], [wk])
        b.tt("dve", t1, lim[:], lim[:], ALU.mult, [lim], [wk])
        b.tt("dve", den, t0, t1, ALU.add, [wk], [wk])
        P.add("dve", lambda e: e.reciprocal(out=den, in_=den), [wk.r], [wk.r])
        b.tt("dve", t0, nr, lre[:], ALU.mult, [wk, lre], [wk])
        b.tt("dve", t1, Li[:, 1, :], lim[:], ALU.mult, [Li, lim], [wk])
        b.tt("dve", cr, t0, t1, ALU.add, [wk], [wk])
        b.tt("dve", cr, cr, den, ALU.mult, [wk], [wk])
        b.tt("dve", t0, Li[:, 1, :], lre[:], ALU.mult, [Li, lre], [wk])
        b.tt("dve", t1, nr, lim[:], ALU.mult, [wk, lim], [wk])
        b.tt("dve", ci, t0, t1, ALU.subtract, [wk], [wk])
        b.tt("dve", ci, ci, den, ALU.mult, [wk], [wk])
        cis = wk[:, 1, :]
        b.ts("dve", cis, ci, sgn[:, 0:1], None, ALU.mult, None, [wk, sgn], [wk])
        b.ts("dve", Lis[:].rearrange("p l g -> p (l g)"), Li[:].rearrange("p l g -> p (l g)"), sgn[:, 0:1], None, ALU.mult, None, [Li, sgn], [Lis])

        Bx = cx.sb(pre, [128, NG, 16], F32, "Bx")
        By = cx.sb(pre, [128, NG, 16], F32, "By")
        bre = g["ssm_b_re"].t.rearrange("g p c -> p g c")
        bim = g["ssm_b_im"].t.rearrange("g p c -> p g c")
        b.dma(Bx[0:64], bre, [], [Bx], key="Bx")
        b.dma(Bx[64:128], bim, [], [Bx], key="Bx")
        b.dma(By[0:64], bim, [], [By], key="By")
        b.dma(By[64:128], bre, [], [By], key="By")
        BS = cx.sb(pre, [128, NG, 16], F32, "BS")
        BSp = cx.sb(pre, [128, NG, 16], F32, "BSp")
        tmpB = cx.sb(pre, [128, NG, 16], F32, "tmpB")

        def bc(ap2d):
            return ap2d.unsqueeze(2).broadcast_to([128, NG, 16])

        b.tt("pool", tmpB[:], By[:], bc(cis), ALU.mult, [By, wk], [tmpB])
        b.tt("pool", BS[:], Bx[:], bc(cr), ALU.mult, [Bx, wk], [BS])
        b.tt("pool", BS[:], BS[:], tmpB[:], ALU.subtract, [BS, tmpB], [BS])
        b.tt("pool", tmpB[:], Bx[:], bc(cis), ALU.mult, [Bx, wk], [tmpB])
        b.tt("pool", BSp[:], By[:], bc(cr), ALU.mult, [By, wk], [BSp])
        b.tt("pool", BSp[:], BSp[:], tmpB[:], ALU.add, [BSp, tmpB], [BSp])
        GT = cx.sb(pre, [128, NG, 15, 16], BF16, "GT")
        b.memset("pool", GT[:, :, 8:15, :], 0.0, [GT])
        tmpG = cx.sb(pre, [128, NG, 16], F32, "tmpG")
        for e in range(8):
            b.tt("pool", tmpB[:], BSp[:], bc(Lis[:, e, :]), ALU.mult, [BSp, Lis], [tmpB])
            b.tt("dve", tmpG[:], BS[:], bc(Lr[:, e, :]), ALU.mult, [BS, Lr], [tmpG])
            b.tt("dve", GT[:, :, 7 - e, :], tmpG[:], tmpB[:], ALU.subtract, [tmpG, tmpB], [GT])
        Cx = cx.sb(pre, [128, NG, 16], F32, "Cx")
        Cy = cx.sb(pre, [128, NG, 16], F32, "Cy")
        Cxb = cx.sb(pre, [128, NG, 16], BF16, "Cxb")
        cin = [cx.sb(pre, [128, 128], F32, "cin") for _ in range(2)]
        cre = g["ssm_c_re"].t.rearrange("g c p -> (g c) p")
        cim = g["ssm_c_im"].t.rearrange("g c p -> (g c) p")
        q = 0
        for j in range(8):
            for which in range(2):
                ci_ = cin[q % 2]
                q += 1
                a0, a1 = (cre, cim) if which == 0 else (cim, cre)
                b.dma(ci_[:, 0:64], a0[j * 128:(j + 1) * 128, :], [], [ci_], key=f"cin{q % 2}")
                b.dma(ci_[:, 64:128], a1[j * 128:(j + 1) * 128, :], [], [ci_], key=f"cin{q % 2}")
                pt = nxt(pf, "pf")
                b.tr(pt[:, 0:128], ci_[:], identf[:], [ci_, identf], [pt])
                dst = Cx if which == 0 else Cy
                dv = dst[:, j * 8:(j + 1) * 8, :].rearrange("p g c -> p (g c)")
                if which == 0:
                    b.ts("dve", dv, pt[:, 0:128], sgn[:, 0:1], None, ALU.mult, None, [pt, sgn], [dst])
                else:
                    b.copy("dve", dv, pt[:, 0:128], [pt], [dst])
        b.copy("pool", Cxb[:], Cx[:], [Cx], [Cxb])
        for t in range(8):
            b.tt("pool", tmpB[:], Cy[:], bc(Li[:, t + 1, :]), ALU.mult, [Cy, Li], [tmpB])
            b.tt("dve", tmpG[:], Cx[:], bc(Lr[:, t + 1, :]), ALU.mult, [Cx, Lr], [tmpG])
            b.tt("dve", O_all[:, :, t, :], tmpG[:], tmpB[:], ALU.subtract, [tmpG, tmpB], [O_all])
        dsrc = g["ssm_d"].t.rearrange("(g c) -> c g", c=16)
        for s in range(8):
            P.add("sp", lambda e, s=s: e.dma_start(out=Dv[s * 16:(s + 1) * 16, :], in_=dsrc, allow_slow_non_contiguous=True), [], [Dv.r], dma=True, key="Dv")
        for gi in range(NG):
            pt = nxt(pf, "pf")
            for t in range(8):
                b.mm(pt[:, t * 16:(t + 1) * 16], GT[:, gi, 7 - t:15 - t, :].rearrange("p s c -> p (s c)"), Cxb[:, gi, :], True, True, [GT, Cxb], [pt])
            b.stt("dve", T_all[:, gi, :], identf[:], Dv[:, gi:gi + 1], pt[:, 0:128], ALU.mult, ALU.add, [identf, Dv, pt], [T_all])
            if gi % 8 == 0:
                ptb = nxt(pb, "pb")
            b.tr(ptb[:, (gi % 8) * 128:(gi % 8 + 1) * 128], GT[:, gi, 0:8, :].rearrange("p s c -> p (s c)"), ident[:], [GT, ident], [ptb])
            if gi % 8 == 7:
                b.copy("act", R_all[:, gi - 7:gi + 1, :].rearrange("p g s -> p (g s)"), ptb[:], [ptb], [R_all])

        if g.get("debug"):
            b.dma(g["dbg_L"].t[:, 0:576], Lr[:].rearrange("p l g -> p (l g)"), [Lr], [], key="dbg")
            b.dma(g["dbg_L"].t[:, 576:1152], Li[:].rearrange("p l g -> p (l g)"), [Li], [], key="dbg")
            b.dma(g["dbg_L"].t[:, 1152:1728], Ar[:].rearrange("p l g -> p (l g)"), [Ar], [], key="dbg")
            b.dma(g["dbg_L"].t[:, 1728:2304], Ai[:].rearrange("p l g -> p (l g)"), [Ai], [], key="dbg")
            b.dma(g["dbg_T"].t[:, :], T_all[:].rearrange("p g c -> p (g c)"), [T_all], [], key="dbg")
            b.dma(g["dbg_R"].t[:, :], R_all[:].rearrange("p g c -> p (g c)"), [R_all], [], key="dbg")
            b.dma(g["dbg_O"].t[:, :], O_all[:].rearrange("p g t c -> p (g t c)"), [O_all], [], key="dbg")
        pre.close()
        P.barrier()
        if STAGE == "pre":
            return
        Mtmp = [cx.sb(st, [128, 128], F32, "Mtmp") for _ in range(2)]
        mt = [0]

        def build_M(eng, M, sr, si):
            if eng == "dve":
                b.ts(eng, M[:], identf[:], sr, None, ALU.mult, None, [identf, Ar, Ai], [M])
                b.stt(eng, M[:], swapf[:], si, M[:], ALU.mult, ALU.add, [swapf, Ar, Ai, M], [M])
            else:
                tmp = Mtmp[mt[0] % 2]
                mt[0] += 1
                b.ts(eng, tmp[:], swapf[:], si, None, ALU.mult, None, [swapf, Ar, Ai], [tmp])
                b.stt("dve", M[:], identf[:], sr, tmp[:], ALU.mult, ALU.add, [identf, Ar, Ai, tmp], [M])

        NCH = (L_CTX + L_OWN) // 8
        NOWN = L_OWN // 8
        uview = g["u_scr"].t[0:L_CTX + L_OWN, :].rearrange("(k s) c -> k s c", s=8)
        yview = g["ys_scr"].t[0:L_OWN, :].rearrange("(k s) c -> k s c", s=8)
        usv = g["u_scr"].t[L_CTX + L_OWN:L_CTX + L_OWN + 128, :].rearrange("(k s) c -> k s c", s=8)
        ysv = g["ys_scr"].t[L_OWN:L_OWN + 128, :].rearrange("(k s) c -> k s c", s=8)
        Up = [cx.sb(st, [128, GP, 8, 16], BF16, "Up") for _ in range(4)]
        Ups = cx.sb(st, [16, GP, 8, 16], BF16, "Ups")
        Uraw = [cx.sb(st, [128, 8, GP * 16], BF16, "Uraw") for _ in range(2)]
        Uraw_s = cx.sb(st, [16, 8, GP * 16], BF16, "Uraw_s")
        Yp = [cx.sb(st, [128, 8, GP * 16], BF16, "Yp") for _ in range(2)]
        Yps = cx.sb(st, [16, 8, GP * 16], BF16, "Yps")
        U8 = [cx.sb(st, [128, NCH], BF16, "U8") for _ in range(2)]
        U8s = [cx.sb(st, [128, 16], BF16, "U8s") for _ in range(2)]
        Ssb = [cx.sb(st, [128, NCH], BF16, "Ssb") for _ in range(2)]
        Msb = [cx.sb(st, [128, 128], BF16, "Msb") for _ in range(4)]
        Hsb = [cx.sb(st, [128, NOWN], BF16, "Hsb") for _ in range(2)]
        M0f = [cx.sb(st, [128, 128], F32, "M0f") for _ in range(2)]
        Y8 = [cx.sb(st, [128, NOWN], BF16, "Y8") for _ in range(2)]
        Y8s = [cx.sb(st, [128, 16], BF16, "Y8s") for _ in range(2)]
        hl = cx.sb(st, [128, NG], F32, "hl")
        hlo = cx.sb(st, [64, 64, 2], F32, "hlo")
        st_in = cx.sb(st, [16, GP, 64, 2], F32, "st_in")
        st_out = cx.sb(st, [16, GP, 64, 2], F32, "st_out")
        st_r = cx.sb(st, [16, GP, 2, 64], F32, "st_r")
        h0T = [cx.sb(st, [128, 16], F32, "h0T") for _ in range(2)]
        h0Tb = [cx.sb(st, [128, 16], BF16, "h0Tb") for _ in range(2)]
        hoT = [cx.sb(st, [128, 16], F32, "hoT") for _ in range(2)]
        mcnt = 0
        for half in range(NG // GP):
            c0 = half * GP * 16
            for j in range(4):
                raw = Uraw[j % 2]
                b.dma(raw[:], uview[j * 128:(j + 1) * 128, :, c0:c0 + GP * 16], [], [raw], key=f"Uraw{j % 2}")
                b.copy("pool" if j % 2 else "act", Up[j][:], raw[:].rearrange("k s (g c) -> k g s c", c=16), [raw], [Up[j]])
            b.dma(Uraw_s[:], usv[:, :, c0:c0 + GP * 16], [], [Uraw_s], key="Ups")
            b.copy("pool", Ups[:], Uraw_s[:].rearrange("k s (g c) -> k g s c", c=16), [Uraw_s], [Ups])
            b.dma(st_in[:].rearrange("b g p r -> b (g p r)"), g["state_s"].t[:, half * GP:(half + 1) * GP, :, :].rearrange("b g p r -> b (g p r)"), [], [st_in], key="st_in")
            b.copy("pool", st_r[:], st_in[:].rearrange("b g p r -> b g r p"), [st_in], [st_r])
            for gl in range(GP):
                gi = half * GP + gl
                s_ = gi % 2
                ptb = nxt(pb, "pb")
                for j in range(4):
                    b.tr(ptb[:, j * 128:(j + 1) * 128], Up[j][:, gl, :, :].rearrange("k s c -> k (s c)"), ident[:], [Up[j], ident], [ptb])
                b.tr(ptb[:, 512:528], Ups[:, gl, :, :].rearrange("k s c -> k (s c)"), ident[0:16, 0:16], [Ups, ident], [ptb])
                b.copy("act", U8[s_][:], ptb[:, 0:512], [ptb], [U8[s_]])
                b.copy("dve", U8s[s_][:], ptb[:, 512:528], [ptb], [U8s[s_]])
                if STAGE == "m1":
                    continue
                pS = nxt(pf, "pf")
                b.mm(pS[:, :], R_all[:, gi, :], U8[s_][:], True, False, [R_all, U8[s_]], [pS])
                for l in range(9):
                    d = 1 << l
                    b.copy("act" if l % 2 else "dve", Ssb[s_][:], pS[:, :], [pS], [Ssb[s_]])
                    M = Msb[mcnt % 4]
                    mcnt += 1
                    build_M("pool" if l % 3 else "dve", M, Ar[:, l, gi:gi + 1], Ai[:, l, gi:gi + 1])
                    b.mm(pS[:, d:NCH], M[:], Ssb[s_][:, 0:NCH - d], False, l == 8, [M, Ssb[s_]], [pS])
                b.copy("dve", hl[:, gi:gi + 1], pS[:, NCH - 1:NCH], [pS], [hl])
                if STAGE == "m2":
                    continue
                pY = nxt(pf, "pf")
                b.mm(pY[:, 0:NOWN], T_all[:, gi, :], U8[s_][:, NCH - NOWN:NCH], True, False, [T_all, U8[s_]], [pY])
                b.copy("act", Hsb[s_][:], pS[:, NCH - NOWN - 1:NCH - 1], [pS], [Hsb[s_]])
                b.mm(pY[:, 0:NOWN], O_all[:, gi, :, :].rearrange("p t c -> p (t c)"), Hsb[s_][:], False, True, [O_all, Hsb[s_]], [pY])
                b.copy("dve", Y8[s_][:], pY[:, 0:NOWN], [pY], [Y8[s_]])
                if STAGE == "m3":
                    continue
                ptb2 = nxt(pb, "pb")
                for j in range(2):
                    b.tr(ptb2[:, j * 128:(j + 1) * 128], Y8[s_][:, j * 128:(j + 1) * 128], ident[:], [Y8[s_], ident], [ptb2])
                for j in range(2):
                    b.copy("dve", Yp[j][:, :, gl * 16:(gl + 1) * 16],
                           ptb2[:, j * 128:(j + 1) * 128].rearrange("p (t c) -> p t c", t=8), [ptb2], [Yp[j]])
                if STAGE != "nosmp":
                    pH = nxt(pf, "pf")
                    b.tr(pH[:, 0:16], st_r[:, gl, :, :].rearrange("b r p -> b (r p)"), identf[0:16, 0:16], [st_r, identf], [pH])
                    b.copy("dve", h0T[s_][:], pH[:, 0:16], [pH], [h0T[s_]])
                    b.copy("dve", h0Tb[s_][:], pH[:, 0:16], [pH], [h0Tb[s_]])
                    Mf = M0f[s_]
                    build_M("pool", Mf, Ar[:, 0, gi:gi + 1], Ai[:, 0, gi:gi + 1])
                    pX = nxt(pf, "pf")
                    b.mm(pX[:, 0:16], R_all[:, gi, :], U8s[s_][:], True, False, [R_all, U8s[s_]], [pX])
                    b.mm(pX[:, 0:16], Mf[:], h0T[s_][:], False, True, [Mf, h0T[s_]], [pX])
                    b.mm(pX[:, 16:32], T_all[:, gi, :], U8s[s_][:], True, False, [T_all, U8s[s_]], [pX])
                    b.mm(pX[:, 16:32], O_all[:, gi, :, :].rearrange("p t c -> p (t c)"), h0Tb[s_][:], False, True, [O_all, h0Tb[s_]], [pX])
                    b.copy("dve", hoT[s_][:], pX[:, 0:16], [pX], [hoT[s_]])
                    b.copy("dve", Y8s[s_][:], pX[:, 16:32], [pX], [Y8s[s_]])
                    pO = nxt(pf, "pf")
                    b.tr(pO[0:16, 0:128], hoT[s_][:], identf[:], [hoT[s_], identf], [pO])
                    b.copy("dve", st_out[:, gl, :, :].rearrange("b p r -> b r p"), pO[0:16, 0:128].rearrange("b (r p) -> b r p", r=2), [pO], [st_out])
                    ptb3 = nxt(pb, "pb")
                    b.tr(ptb3[0:16, 0:128], Y8s[s_][:], ident[:], [Y8s[s_], ident], [ptb3])
                    b.copy("dve", Yps[:, :, gl * 16:(gl + 1) * 16], ptb3[0:16, 0:128].rearrange("p (t c) -> p t c", t=8), [ptb3], [Yps])
            for j in range(2):
                b.dma(yview[j * 128:(j + 1) * 128, :, c0:c0 + GP * 16], Yp[j][:], [Yp[j]], [], key=f"Yp{j}")
            b.dma(ysv[:, :, c0:c0 + GP * 16], Yps[:], [Yps], [], key="Yps")
            b.dma(g["o_ssm_s"].t[:, half * GP:(half + 1) * GP, :, :].rearrange("b g p r -> b (g p r)"), st_out[:].rearrange("b g p r -> b (g p r)"), [st_out], [], key="st_out")
        pO = nxt(pf, "pf")
        b.tr(pO[0:64, 0:128], hl[:], identf[:], [hl, identf], [pO])
        b.copy("dve", hlo[:, :, :].rearrange("g p r -> g r p"), pO[0:64, 0:128].rearrange("g (r p) -> g r p", r=2), [pO], [hlo])
        b.dma(g["o_ssm_p"].t[:, :, :], hlo[:], [hlo], [], key="hlo")
    P.barrier()


def load_cast_w(cx, b, st, src2d, kchunks, ncols, name, stg, eng_cycle=("pool", "dve", "act")):
    w = cx.sb(st, [128, kchunks, ncols], BF16, name)
    v = src2d.rearrange("(k p) c -> k p c", p=128)
    for k in range(kchunks):
        s = stg[k % len(stg)]
        b.dma(s[:, 0:ncols], v[k], [], [s], key=f"wstg{k % len(stg)}")
        b.copy(eng_cycle[k % len(eng_cycle)], w[:, k, :], s[:, 0:ncols], [s], [w])
    return w


def phase_P(nc, cx, b, P, g):
    ident = g["ident"]
    LT = L_CTX + L_OWN
    NQB = L_OWN // 128
    with contextlib.ExitStack() as st:
        stg = [cx.sb(st, [128, 1024], F32, "stgP") for _ in range(2)]
        wuk = load_cast_w(cx, b, st, g["mla_w_uk"].t, 2, 1024, "wukP", stg)
        wuv = load_cast_w(cx, b, st, g["mla_w_uv"].t, 2, 1024, "wuvP", stg)
        ckvT = cx.sb(st, [128, 2, LT], BF16, "ckvT")
        for k in range(2):
            b.dma(ckvT[:, k, :], g["ckvT_scr"].t[k], [], [ckvT], key="ckvT")
        KT = [cx.sb(st, [96, LT], BF16, "KT") for _ in range(2)]
        for i in range(2):
            b.dma(KT[i][64:96, :], g["krT_scr"].t[:, :], [], [KT[i]], key="KTr")
        Vh = [cx.sb(st, [128, LT // 128, 64], BF16, "Vh") for _ in range(2)]
        qT = [cx.sb(st, [96, L_OWN], BF16, "qTh") for _ in range(2)]
        S_sb = [cx.sb(st, [128, LT], F32, "S_sb") for _ in range(2)]
        P_bf = [cx.sb(st, [128, LT], BF16, "P_bf") for _ in range(2)]
        PT = [cx.sb(st, [128, LT // 128, 128], BF16, "PT") for _ in range(2)]
        sm = [cx.sb(st, [128, 8], F32, "smP") for _ in range(2)]
        o_t = [cx.sb(st, [128, 64], BF16, "o_t") for _ in range(4)]
        tril = cx.sb(st, [128, 128], F32, "tril")
        cbias = cx.sb(st, [128, 1], F32, "cbias")
        b.dma(cbias[:], g["ctx_bias"].t[:, :], [], [cbias], key="cbias")
        b.memset("pool", tril[:], 0.0, [tril])
        P.add("pool", lambda e: e.affine_select(out=tril[:], in_=tril[:], pattern=[[-1, 128]], compare_op=ALU.is_ge,
                                                fill=NEG, base=0, channel_multiplier=1), [tril.r], [tril.r])
        pf = [cx.ps(st, [128, 512], F32, "pfP") for _ in range(4)]
        pb = [cx.ps(st, [128, 1024], BF16, "pbP") for _ in range(2)]
        po = [cx.ps(st, [128, 512], F32, "poP") for _ in range(2)]
        cnt = {"pf": 0, "pb": 0, "po": 0, "ev": 0}

        def nxt(lst, k):
            cnt[k] += 1
            return lst[cnt[k] % len(lst)]

        def ev_eng():
            cnt["ev"] += 1
            return "act" if cnt["ev"] % 2 else "dve"

        MAXENG = "dve"
        work = []

        def head_prep(h):
            hb = h % 2
            b.dma(qT[hb][:, :], g["qT_scr"].t[h], [], [qT[hb]], key=f"qTh{hb}")
            for tg in range(LT // 512):
                pz = nxt(pf, "pf")
                for k in range(2):
                    b.mm(pz[0:64, :], wuk[:, k, h * 64:(h + 1) * 64], ckvT[:, k, tg * 512:(tg + 1) * 512], k == 0, k == 1, [wuk, ckvT], [pz])
                b.copy(ev_eng(), KT[hb][0:64, tg * 512:(tg + 1) * 512], pz[0:64, :], [pz], [KT[hb]])
            for vg in range(LT // 1024):
                pz = nxt(pf, "pf")
                for j in range(8):
                    kt = vg * 8 + j
                    for k in range(2):
                        b.mm(pz[:, j * 64:(j + 1) * 64], ckvT[:, k, kt * 128:(kt + 1) * 128], wuv[:, k, h * 64:(h + 1) * 64], k == 0, k == 1, [ckvT, wuv], [pz])
                b.copy(ev_eng(), Vh[hb][:, vg * 8:(vg + 1) * 8, :].rearrange("p j d -> p (j d)"), pz[:, :], [pz], [Vh[hb]])

        for h in range(NH):
            hb = h % 2
            for j in range(NQB):
                work.append((h, hb, j))
        def part1(it, h, hb, j):
            s_ = it % 2
            nkb = L_CTX // 128 + j + 1
            nk = nkb * 128
            S, sm_ = S_sb[s_], sm[s_]
            for kg in range((nk + 511) // 512):
                n = min(512, nk - kg * 512)
                pz = nxt(pf, "pf")
                b.mm(pz[:, 0:n], qT[hb][:, j * 128:(j + 1) * 128], KT[hb][:, kg * 512:kg * 512 + n], True, True, [qT[hb], KT[hb]], [pz])
                if kg < L_CTX // 512:
                    b.act(S[:, kg * 512:kg * 512 + n], pz[:, 0:n], AF.Identity, [pz, cbias], [S], bias=cbias[:, 0:1], scale=SCALE)
                else:
                    b.ts("dve", S[:, kg * 512:kg * 512 + n], pz[:, 0:n], SCALE, None, ALU.mult, None, [pz], [S])
            b.tt("dve", S[:, nk - 128:nk], S[:, nk - 128:nk], tril[:], ALU.add, [S, tril], [S])
            b.reduce(MAXENG, sm_[:, 0:1], S[:, 0:nk], ALU.max, [S], [sm_])
            b.ts(MAXENG, sm_[:, 1:2], sm_[:, 0:1], -1.0, None, ALU.mult, None, [sm_], [sm_])

        def part2(it, h, hb, j):
            s_ = it % 2
            nkb = L_CTX // 128 + j + 1
            nk = nkb * 128
            S, Pb, PT_, sm_ = S_sb[s_], P_bf[s_], PT[s_], sm[s_]
            b.act(Pb[:, 0:nk], S[:, 0:nk], AF.Exp, [S, sm_], [Pb, sm_], bias=sm_[:, 1:2], scale=1.0, accum=sm_[:, 2:3])
            P.add("dve", lambda e, sm_=sm_: e.reciprocal(out=sm_[:, 3:4], in_=sm_[:, 2:3]), [sm_.r], [sm_.r])
            for kb0 in range(0, nkb, 8):
                nb = min(8, nkb - kb0)
                pt = nxt(pb, "pb")
                for q in range(nb):
                    kb = kb0 + q
                    b.tr(pt[:, q * 128:(q + 1) * 128], Pb[:, kb * 128:(kb + 1) * 128], ident[:], [Pb, ident], [pt])
                b.copy(ev_eng(), PT_[:, kb0:kb0 + nb, :].rearrange("p k q -> p (k q)"), pt[:, 0:nb * 128], [pt], [PT_])
            pov = nxt(po, "po")
            for kb in range(nkb):
                b.mm(pov[:, 0:64], PT_[:, kb, :], Vh[hb][:, kb, :], kb == 0, kb == nkb - 1, [PT_, Vh[hb]], [pov])
            ot = o_t[it % 4]
            b.ts("dve", ot[:], pov[:, 0:64], sm_[:, 3:4], None, ALU.mult, None, [pov, sm_], [ot])
            b.dma(g["o_scr"].t[j * 128:(j + 1) * 128, h * 64:(h + 1) * 64], ot[:], [ot], [], key=f"ot{it % 4}")

        for i, (h, hb, j) in enumerate(work):
            if j == 0:
                head_prep(h)
            part1(i, h, hb, j)
            if i > 0:
                part2(i - 1, *work[i - 1])
        part2(len(work) - 1, *work[-1])
    P.barrier()


def phase_G(nc, cx, b, P, g):
    ident = g["ident"]
    NB = 16
    NSLOT = 8
    NSTEP = 128 // NSLOT
    with contextlib.ExitStack() as st:
        stg = [cx.sb(st, [128, 1024], F32, "stgG") for _ in range(2)]
        wuv = load_cast_w(cx, b, st, g["mla_w_uv"].t, 2, 1024, "wuvG", stg)
        qlT = cx.sb(st, [128, 2, NB, 128], BF16, "qlT")
        qrT = cx.sb(st, [32, NB, 128], BF16, "qrT")
        for k in range(2):
            b.dma(qlT[:, k], g["qlT_scr"].t[k], [], [qlT], key="qlT")
        b.dma(qrT[:], g["qrT_scr"].t[:, :, :], [], [qrT], key="qrT")
        ckvsT = cx.sb(st, [128, 2, 128], BF16, "ckvsT")
        krsT = cx.sb(st, [32, 128], BF16, "krsT")
        ckvs = cx.sb(st, [128, 256], BF16, "ckvs")
        for k in range(2):
            b.dma(ckvsT[:, k, :], g["ckvsT_scr"].t[k], [], [ckvsT], key="ckvsT")
        b.dma(krsT[:], g["krsT_scr"].t[:, :], [], [krsT], key="krsT")
        b.dma(ckvs[:], g["ckvs_scr"].t[:, :], [], [ckvs], key="ckvsG")
        ptab = cx.sb(st, [128, NB], I32, "ptab")
        P.add("sp", lambda e: e.dma_start(out=ptab[:], in_=g["pt_core"].t.rearrange("b n -> n b"), allow_slow_non_contiguous=True), [], [ptab.r], dma=True, key="ptab")
        G_ = [cx.sb(st, [128, NSLOT, 256], F32, "Gf") for _ in range(2)]
        Gr = [cx.sb(st, [128, NSLOT, 32], F32, "Grf") for _ in range(2)]
        Gb = [cx.sb(st, [128, NSLOT, 256], BF16, "Gb") for _ in range(2)]
        Gbk = [cx.sb(st, [128, NSLOT, 32], BF16, "Gbk") for _ in range(2)]
        KTg = [cx.sb(st, [128, 2, NSLOT * 128], BF16, "KTg") for _ in range(2)]
        KrT = [cx.sb(st, [32, NSLOT * 128], BF16, "KrT") for _ in range(2)]
        Pb = [cx.sb(st, [128, NSLOT * 128], BF16, "PbG") for _ in range(2)]
        PTg = [cx.sb(st, [128, NSLOT, 128], BF16, "PTg") for _ in range(2)]
        Snew = cx.sb(st, [128, 128], F32, "Snew")
        msk = cx.sb(st, [128, NB, 128], F32, "mskG")
        b.dma(msk[:], g["smp_mask"].t.rearrange("b r k -> r b k"), [], [msk], key="msk")
        acc = cx.sb(st, [128, 256], F32, "acc")
        sm = [cx.sb(st, [128, 12], F32, "smG") for _ in range(2)]
        ol_bf = cx.sb(st, [128, 256], BF16, "ol_bf")
        olT = cx.sb(st, [128, 2, 128], BF16, "olT")
        oT_s = cx.sb(st, [64, NH, 128], BF16, "oT_s")
        o_smp = cx.sb(st, [128, 1024], BF16, "o_smp")
        pf = [cx.ps(st, [128, 512], F32, "pfG") for _ in range(1)]
        pb = [cx.ps(st, [128, 1024], BF16, "pbG") for _ in range(2)]
        po = [cx.ps(st, [128, 512], F32, "poG") for _ in range(1)]
        Pnew = cx.sb(st, [128, 128], BF16, "Pnew")
        PTnew = cx.sb(st, [128, 128], BF16, "PTnew")
        cnt = {"pf": 0, "pb": 0, "ev": 0}

        def nxt(lst, k):
            cnt[k] += 1
            return lst[cnt[k] % len(lst)]

        def ev_eng():
            cnt["ev"] += 1
            return "act" if cnt["ev"] % 2 else "dve"

        ckv_rows = g["cache_ckv"].t.rearrange("n (s c) -> (n s) c", c=NSLOT * 256)
        kr_rows = g["cache_krope"].t.rearrange("n (s c) -> (n s) c", c=NSLOT * 32)
        ptf = cx.sb(st, [128, NB], F32, "ptf")
        stpf = cx.sb(st, [128, NSTEP], F32, "stpf")
        idx_f = cx.sb(st, [128, NB, NSTEP], F32, "idx_f")
        idx_all = cx.sb(st, [128, NB, NSTEP], I32, "idx_all")
        b.copy("dve", ptf[:], ptab[:], [ptab], [ptf])
        b.ts("dve", ptf[:], ptf[:], float(NSTEP), None, ALU.mult, None, [ptf], [ptf])
        for i in range(NSTEP):
            b.memset("pool", stpf[:, i:i + 1], float(i), [stpf])
        b.tt("dve", idx_f[:], ptf[:].unsqueeze(2).broadcast_to([128, NB, NSTEP]), stpf[:].unsqueeze(1).broadcast_to([128, NB, NSTEP]), ALU.add, [ptf, stpf], [idx_f])
        b.copy("dve", idx_all[:], idx_f[:], [idx_f], [idx_all])
        pS2 = [cx.ps(st, [128, 1024], F32, "pS2") for _ in range(2)]
        smx = [cx.sb(st, [128, 2], F32, "smx") for _ in range(2)]

        def init_sample(bi):
            sm_ = sm[bi % 2]
            pz = pf[0]
            for k in range(2):
                b.mm(pz[:, 0:128], qlT[:, k, bi, :], ckvsT[:, k, :], k == 0, False, [qlT, ckvsT], [pz])
            b.mm(pz[:, 0:128], qrT[:, bi, :], krsT[:, :], False, True, [qrT, krsT], [pz])
            b.tt("dve", Snew[:], pz[:, 0:128], msk[:, bi, :], ALU.add, [pz, msk], [Snew])
            b.reduce("dve", sm_[:, 0:1], Snew[:], ALU.max, [Snew], [sm_])
            b.ts("dve", sm_[:, 1:2], sm_[:, 0:1], -SCALE, None, ALU.mult, None, [sm_], [sm_])
            b.act(Pnew[:], Snew[:], AF.Exp, [Snew, sm_], [Pnew, sm_], bias=sm_[:, 1:2], scale=SCALE, accum=sm_[:, 2:3])
            pt = nxt(pb, "pb")
            b.tr(pt[:, 0:128], Pnew[:], ident[:], [Pnew, ident], [pt])
            b.copy("dve", PTnew[:], pt[:, 0:128], [pt], [PTnew])
            pov = po[0]
            b.mm(pov[:, 0:256], PTnew[:], ckvs[:, :], True, True, [PTnew, ckvs], [pov])
            b.copy("dve", acc[:], pov[:, 0:256], [pov], [acc])

        def stage_x(i, bi, stp):
            q_ = i % 2
            Gf, Grf, Gb_, KT_, KrT_ = G_[q_], Gr[q_], Gb[q_], KTg[q_], KrT[q_]
            P.add("pool", lambda e, Gf=Gf, stp=stp, bi=bi: e.indirect_dma_start(
                out=Gf[:].rearrange("p s c -> p (s c)"), out_offset=None, in_=ckv_rows,
                in_offset=bass.IndirectOffsetOnAxis(ap=idx_all[:, bi, stp:stp + 1], axis=0)), [idx_all.r], [Gf.r], dma=True, key=f"Gf{q_}")
            P.add("pool", lambda e, Grf=Grf, stp=stp, bi=bi: e.indirect_dma_start(
                out=Grf[:].rearrange("p s c -> p (s c)"), out_offset=None, in_=kr_rows,
                in_offset=bass.IndirectOffsetOnAxis(ap=idx_all[:, bi, stp:stp + 1], axis=0)), [idx_all.r], [Grf.r], dma=True, key=f"Grf{q_}")
            Gbk_ = Gbk[q_]
            hs = NSLOT // 2
            b.copy("dve", Gb_[:, 0:hs, :].rearrange("p s c -> p (s c)"), Gf[:, 0:hs, :].rearrange("p s c -> p (s c)"), [Gf], [Gb_])
            b.copy("act", Gb_[:, hs:NSLOT, :].rearrange("p s c -> p (s c)"), Gf[:, hs:NSLOT, :].rearrange("p s c -> p (s c)"), [Gf], [Gb_])
            b.copy("dve", Gbk_[:].rearrange("p s c -> p (s c)"), Grf[:].rearrange("p s c -> p (s c)"), [Grf], [Gbk_])
            for k in range(2):
                pt = nxt(pb, "pb")
                for s in range(NSLOT):
                    b.tr(pt[:, s * 128:(s + 1) * 128], Gb_[:, s, k * 128:(k + 1) * 128], ident[:], [Gb_, ident], [pt])
                b.copy("act" if k else "dve", KT_[:, k, :], pt[:, :], [pt], [KT_])
            pt = nxt(pb, "pb")
            for s in range(NSLOT):
                b.tr(pt[0:32, s * 128:(s + 1) * 128], Gbk_[:, s, :], ident[:], [Gbk_, ident], [pt])
            b.copy("dve", KrT_[:, :], pt[0:32, :], [pt], [KrT_])
            pz = pS2[q_]
            for hf in range(2):
                sl = slice(hf * 512, (hf + 1) * 512)
                for k in range(2):
                    b.mm(pz[:, sl], qlT[:, k, bi, :], KT_[:, k, sl], k == 0, False, [qlT, KT_], [pz])
                b.mm(pz[:, sl], qrT[:, bi, :], KrT_[:, sl], False, True, [qrT, KrT_], [pz])
            b.reduce("dve", smx[q_][:, 0:1], pz[:, :], ALU.max, [pz], [smx[q_]])

        def stage_y(i, bi, stp):
            q_ = i % 2
            sm_ = sm[bi % 2]
            Gb_, Pb_, PT_, pz = Gb[q_], Pb[q_], PTg[q_], pS2[q_]
            b.tt("dve", sm_[:, 6:7], smx[q_][:, 0:1], sm_[:, 0:1], ALU.max, [sm_, smx[q_]], [sm_])
            b.ts("dve", sm_[:, 7:8], sm_[:, 6:7], -SCALE, None, ALU.mult, None, [sm_], [sm_])
            b.act(sm_[:, 8:9], sm_[:, 0:1], AF.Exp, [sm_], [sm_], bias=sm_[:, 7:8], scale=SCALE)
            b.act(Pb_[:, :], pz[:, :], AF.Exp, [pz, sm_], [Pb_, sm_], bias=sm_[:, 7:8], scale=SCALE, accum=sm_[:, 9:10])
            b.stt("dve", sm_[:, 2:3], sm_[:, 2:3], sm_[:, 8:9], sm_[:, 9:10], ALU.mult, ALU.add, [sm_], [sm_])
            b.copy("dve", sm_[:, 0:1], sm_[:, 6:7], [sm_], [sm_])
            pt = nxt(pb, "pb")
            for s in range(NSLOT):
                b.tr(pt[:, s * 128:(s + 1) * 128], Pb_[:, s * 128:(s + 1) * 128], ident[:], [Pb_, ident], [pt])
            b.copy("act", PT_[:].rearrange("p s q -> p (s q)"), pt[:, :], [pt], [PT_])
            pov = po[0]
            for s in range(NSLOT):
                b.mm(pov[:, 0:256], PT_[:, s, :], Gb_[:, s, 0:256], s == 0, s == NSLOT - 1, [PT_, Gb_], [pov])
            b.stt("dve", acc[:], acc[:], sm_[:, 8:9], pov[:, 0:256], ALU.mult, ALU.add, [acc, sm_, pov], [acc])

        def finalize(bi):
            sm_ = sm[bi % 2]
            if g.get("debug"):
                b.dma(g["dbg_sm"].t[bi], sm_[:], [sm_], [], key="dbgsm")
                b.dma(g["dbg_acc"].t[bi], acc[:], [acc], [], key="dbgacc")
            P.add("dve", lambda e, sm_=sm_: e.reciprocal(out=sm_[:, 3:4], in_=sm_[:, 2:3]), [sm_.r], [sm_.r])
            b.ts("dve", ol_bf[:], acc[:], sm_[:, 3:4], None, ALU.mult, None, [acc, sm_], [ol_bf])
            pt = nxt(pb, "pb")
            for k in range(2):
                b.tr(pt[:, k * 128:(k + 1) * 128], ol_bf[:, k * 128:(k + 1) * 128], ident[:], [ol_bf, ident], [pt])
            b.copy("dve", olT[:].rearrange("p k r -> p (k r)"), pt[:, 0:256], [pt], [olT])
            pz = pf[0]
            for h in range(NH):
                for k in range(2):
                    b.mm(pz[0:64, h * 8:(h + 1) * 8], wuv[:, k, h * 64:(h + 1) * 64], olT[:, k, h * 8:(h + 1) * 8], k == 0, k == 1, [wuv, olT], [pz])
            b.copy("dve", oT_s[:, :, bi * 8:(bi + 1) * 8], pz[0:64, 0:128].rearrange("p (h t) -> p h t", h=NH), [pz], [oT_s])

        items = [(bi, stp) for bi in range(NB) for stp in range(NSTEP)]

        def emit_y(i):
            bi, stp = items[i]
            if stp == 0:
                init_sample(bi)
            stage_y(i, bi, stp)
            if stp == NSTEP - 1:
                finalize(bi)

        for i, (bi, stp) in enumerate(items):
            stage_x(i, bi, stp)
            if i > 0:
                emit_y(i - 1)
        emit_y(len(items) - 1)
        for hg in range(2):
            pt = nxt(pb, "pb")
            for hh in range(8):
                b.tr(pt[:, hh * 64:(hh + 1) * 64], oT_s[:, hg * 8 + hh, :], ident[0:64, 0:64], [oT_s, ident], [pt])
            b.copy("dve", o_smp[:, hg * 512:(hg + 1) * 512], pt[:, 0:512], [pt], [o_smp])
        b.dma(g["o_scr"].t[L_OWN:L_OWN + 128, :], o_smp[:], [o_smp], [], key="o_smp")
    P.barrier()


def phase_B(nc, cx, b, P, g):
    ident = g["ident"]
    with contextlib.ExitStack() as st:
        stg = [cx.sb(st, [128, 1024], F32, "stgB") for _ in range(2)]
        wglu = load_cast_w(cx, b, st, g["ssm_w_glu"].t, 8, 1024, "wglu", stg)
        wbs = load_cast_w(cx, b, st, g["w_br_ssm"].t, 8, 1024, "wbs", stg)
        wba = load_cast_w(cx, b, st, g["w_br_attn"].t, 8, 1024, "wba", stg)
        wo = load_cast_w(cx, b, st, g["w_out"].t, 8, 1024, "wo", stg)
        bglu = cx.sb(st, [128, D], F32, "bglu")
        gfin = cx.sb(st, [128, D], F32, "gfin")
        b.dma(bglu[:], bcast_rows(g["ssm_b_glu"].t, D), [], [bglu], key="bglu")
        b.dma(gfin[:], bcast_rows(g["norm_final"].t, D), [], [gfin], key="gfin")
        NB = 2
        ys = [cx.sb(st, [128, D], BF16, "ysB") for _ in range(NB)]
        gt = [cx.sb(st, [128, 4096], BF16, "gtB") for _ in range(NB)]
        ot = [cx.sb(st, [128, D], BF16, "otB") for _ in range(NB)]
        xt = [cx.sb(st, [128, D], F32, "xtB") for _ in range(NB)]
        t1 = cx.sb(st, [128, D], F32, "t1B")
        zg = cx.sb(st, [128, D], F32, "zgB")
        ab = cx.sb(st, [128, D], BF16, "abB")
        aT = [cx.sb(st, [128, 8, 128], BF16, "aTB") for _ in range(2)]
        mg = cx.sb(st, [128, D], F32, "mgB")
        hh = cx.sb(st, [128, D], F32, "hhB")
        yo = [cx.sb(st, [128, D], F32, "yoB") for _ in range(NB)]
        stt_ = [cx.sb(st, [128, 4], F32, "stB") for _ in range(NB)]
        junk = cx.sb(st, [128, D], BF16, "junkB")
        pz = [cx.ps(st, [128, 512], F32, "pzB") for _ in range(4)]
        pT = [cx.ps(st, [128, 1024], BF16, "pTB") for _ in range(2)]
        cnt = {"pz": 0, "pT": 0, "aT": 0}

        def nxt(lst, k):
            cnt[k] += 1
            return lst[cnt[k] % len(lst)]

        def transp(src):
            pt = nxt(pT, "pT")
            for k in range(8):
                b.tr(pt[:, k * 128:(k + 1) * 128], src[:, k * 128:(k + 1) * 128], ident[:], [src, ident], [pt])
            a = nxt(aT, "aT")
            b.copy("act", a[:].rearrange("p k t -> p (k t)"), pt[:], [pt], [a])
            return a

        def linear(a, w, cg):
            p_ = nxt(pz, "pz")
            for k in range(8):
                b.mm(p_[:, :], a[:, k, :], w[:, k, cg * 512:(cg + 1) * 512], k == 0, k == 7, [a, w], [p_])
            return p_

        tiles = [("own", i) for i in range(L_OWN // 128)] + [("smp", 0)]
        def loads_B(tj):
            kd, ii = tiles[tj]
            sj = tj % NB
            rj = ii * 128 if kd == "own" else L_OWN
            Xj = g["x_own"].t[ii * 128:(ii + 1) * 128, :] if kd == "own" else g["x_smp"].t[:, :]
            b.dma(ys[sj][:], g["ys_scr"].t[rj:rj + 128, :], [], [ys[sj]], key=f"ysB{sj}")
            b.dma(gt[sj][:], g["g_scr"].t[rj:rj + 128, :], [], [gt[sj]], key=f"gtB{sj}")
            b.dma(ot[sj][:], g["o_scr"].t[rj:rj + 128, :], [], [ot[sj]], key=f"otB{sj}")
            b.dma(xt[sj][:], Xj, [], [xt[sj]], key=f"xtB{sj}")

        for ti, (kind, i) in enumerate(tiles):
            s_ = ti % NB
            r0 = i * 128 if kind == "own" else L_OWN
            X = g["x_own"].t[i * 128:(i + 1) * 128, :] if kind == "own" else g["x_smp"].t[:, :]
            OUT = g["o_y_p"].t[i * 128:(i + 1) * 128, :] if kind == "own" else g["o_y_s"].t[:, :]
            ys_, gt_, ot_, xt_, st_ = ys[s_], gt[s_], ot[s_], xt[s_], stt_[s_]
            if ti == 0:
                loads_B(0)
            if ti + 1 < len(tiles):
                loads_B(ti + 1)
            b.tt("pool", t1[:], ys_[:], ys_[:], ALU.mult, [ys_], [t1])
            b.ts("pool", t1[:], t1[:], 0.044715, 1.0, ALU.mult, ALU.add, [t1], [t1])
            b.tt("pool", t1[:], t1[:], ys_[:], ALU.mult, [t1, ys_], [t1])
            b.act(t1[:], t1[:], AF.Sigmoid, [t1], [t1], scale=1.5957691216057308)
            b.tt("dve", zg[:], t1[:], ys_[:], ALU.mult, [t1, ys_], [zg])
            b.copy("pool", ab[:], zg[:], [zg], [ab])
            a = transp(ab)
            for cg in range(2):
                p_ = linear(a, wglu, cg)
                sl = slice(cg * 512, (cg + 1) * 512)
                b.tt("dve", t1[:, sl], p_[:, :], bglu[:, sl], ALU.add, [p_, bglu], [t1])
                b.act(t1[:, sl], t1[:, sl], AF.Sigmoid, [t1], [t1])
                b.tt("dve", t1[:, sl], t1[:, sl], zg[:, sl], ALU.mult, [t1, zg], [t1])
            b.tt("dve", ab[:], t1[:], gt_[:, 0:1024], ALU.mult, [t1, gt_], [ab])
            a = transp(ab)
            for cg in range(2):
                p_ = linear(a, wbs, cg)
                sl = slice(cg * 512, (cg + 1) * 512)
                b.tt("dve", mg[:, sl], p_[:, :], gt_[:, 2048 + cg * 512:2048 + (cg + 1) * 512], ALU.mult, [p_, gt_], [mg])
            b.tt("pool", ab[:], ot_[:], gt_[:, 1024:2048], ALU.mult, [ot_, gt_], [ab])
            a = transp(ab)
            for cg in range(2):
                p_ = linear(a, wba, cg)
                sl = slice(cg * 512, (cg + 1) * 512)
                b.tt("dve", t1[:, sl], p_[:, :], gt_[:, 3072 + cg * 512:3072 + (cg + 1) * 512], ALU.mult, [p_, gt_], [t1])
            b.tt("pool", mg[:], mg[:], t1[:], ALU.add, [mg, t1], [mg])
            b.copy("pool", ab[:], mg[:], [mg], [ab])
            a = transp(ab)
            for cg in range(2):
                p_ = linear(a, wo, cg)
                sl = slice(cg * 512, (cg + 1) * 512)
                b.tt("dve", hh[:, sl], p_[:, :], xt_[:, sl], ALU.add, [p_, xt_], [hh])
            b.act(junk[:], hh[:], AF.Square, [hh], [st_], accum=st_[:, 0:1])
            b.rstd(st_, 0, 1, 2, 1.0 / D)
            b.stt("dve", yo[s_][:], hh[:], st_[:, 2:3], gfin[:], ALU.mult, ALU.mult, [hh, st_, gfin], [yo[s_]])
            b.dma(OUT, yo[s_][:], [yo[s_]], [], key=f"yoB{s_}")
```
